# Optimizing a Trainium2 kernel written in Bass

```python
import math
import jax, jax.numpy as jnp
from jax import lax
import numpy as np

D_MODEL = 1024
BATCH = 8
SEQ = 2048
DEPTH = 2
DEC_BATCH = 128
DEC_SEQ = 4
PAST_LEN = 16384
PAGE_SIZE = 128

D_MIX = D_MODEL
D_POOL = D_MIX // 4
D_CONV = D_MIX // 4
D_MLA = D_MIX - D_POOL - D_CONV
POOL_WINDOWS = (2, 4, 8, 16)
POOL_GROUPS = len(POOL_WINDOWS)
POOL_GW = D_POOL // POOL_GROUPS
POOL_MAX = max(POOL_WINDOWS)
CONV_W = 3
N_HEADS = 4
V_DIM = D_MLA // N_HEADS
NOPE = 64
ROPE = 32
Q_LORA = 256
KV_LORA = 128
ROPE_THETA = 10000.0
QBLK = 128
EPS = 1e-6
NEG = -1e30
SM_SCALE = 1.0 / math.sqrt(NOPE + ROPE)
COL_WIDTHS = (D_POOL, D_POOL, D_CONV, D_CONV, D_CONV, D_CONV, Q_LORA, KV_LORA, ROPE, D_MLA)
D_IN = sum(COL_WIDTHS)

kernel_name = "hybrid_pool_conv_mla_decoder_step"


def rmsnorm(x, g):
    xf = x.astype(jnp.float32)
    y = xf * lax.rsqrt(jnp.mean(xf * xf, axis=-1, keepdims=True) + EPS)
    return (y * g.astype(jnp.float32)).astype(x.dtype)


def rope(x, pos):
    inv_freq = 1.0 / (ROPE_THETA ** (jnp.arange(0, ROPE, 2, dtype=jnp.float32) / ROPE))
    ang = pos.astype(jnp.float32)[:, None] * inv_freq[None, :]
    cos = jnp.concatenate([jnp.cos(ang), jnp.cos(ang)], axis=-1)
    sin = jnp.concatenate([jnp.sin(ang), jnp.sin(ang)], axis=-1)
    bshape = (1, pos.shape[0]) + (1,) * (x.ndim - 3) + (ROPE,)
    cos = cos.reshape(bshape)
    sin = sin.reshape(bshape)
    xf = x.astype(jnp.float32)
    x1, x2 = xf[..., : ROPE // 2], xf[..., ROPE // 2:]
    rot = jnp.concatenate([-x2, x1], axis=-1)
    return (xf * cos + rot * sin).astype(x.dtype)


def adaln(x, c, g, w_ada, b_ada):
    mod = jax.nn.silu(c) @ w_ada + b_ada
    shift, scale, gate = jnp.split(mod, 3, axis=-1)
    h = rmsnorm(x, g) * (1.0 + scale[:, None, :]) + shift[:, None, :]
    return h, gate


def split_columns(p):
    idx = []
    acc = 0
    for w in COL_WIDTHS[:-1]:
        acc += w
        idx.append(acc)
    return jnp.split(p, idx, axis=-1)


def pool_mix(u, prev, pos, pool_w, pool_scale):
    B, T, _ = u.shape
    u_ext = jnp.concatenate([prev, u], axis=1)
    cs = jnp.cumsum(u_ext.astype(jnp.float32), axis=1)
    cs = jnp.concatenate([jnp.zeros((B, 1, D_POOL), jnp.float32), cs], axis=1)
    means = []
    for gi, w in enumerate(POOL_WINDOWS):
        sl = slice(gi * POOL_GW, (gi + 1) * POOL_GW)
        wsum = cs[:, POOL_MAX:POOL_MAX + T, sl] - cs[:, POOL_MAX - w:POOL_MAX - w + T, sl]
        cnt = jnp.minimum(pos + 1, w).astype(jnp.float32)[None, :, None]
        means.append(wsum / cnt)
    d = (jnp.concatenate(means, axis=-1) - u.astype(jnp.float32)).astype(u.dtype)
    y = jnp.einsum('btgc,gcd->btgd', d.reshape(B, T, POOL_GROUPS, POOL_GW), pool_w)
    y = y.reshape(B, T, D_POOL) * pool_scale
    return y, u_ext[:, -(POOL_MAX - 1):]


def conv_mix(h, b_gate, c_gate, prev, conv_w):
    T = h.shape[1]
    v_ext = jnp.concatenate([prev, c_gate * h], axis=1)
    y = conv_w[0] * v_ext[:, 0:T]
    for k in range(1, CONV_W):
        y = y + conv_w[k] * v_ext[:, k:k + T]
    return b_gate * y, v_ext[:, -(CONV_W - 1):]


def mla_queries(cq_raw, pos, q_norm_g, w_uq, qn_g, qr_g):
    B, T, _ = cq_raw.shape
    q = (rmsnorm(cq_raw, q_norm_g) @ w_uq).reshape(B, T, N_HEADS, NOPE + ROPE)
    qn = rmsnorm(q[..., :NOPE], qn_g)
    qr = rope(rmsnorm(q[..., NOPE:], qr_g), pos)
    return qn, qr


def mla_latent(ckv_raw, kr_raw, pos, kv_norm_g, kr_g):
    ckv = rmsnorm(ckv_raw, kv_norm_g)
    kr = rope(rmsnorm(kr_raw, kr_g), pos)
    return ckv, kr


def key_nope(ckv, w_uk, kn_g):
    k = (ckv @ w_uk).reshape(ckv.shape[:-1] + (N_HEADS, NOPE))
    return rmsnorm(k, kn_g)


def attn_scores(qn, qr, kn, kr):
    s = jnp.einsum('bqhd,bkhd->bhqk', qn, kn) + jnp.einsum('bqhd,bkd->bhqk', qr, kr)
    return s.astype(jnp.float32) * SM_SCALE


def mla_prompt(qn, qr, ckv, kr, w_uk, w_uv, kn_g):
    B, T = qn.shape[:2]
    kn = key_nope(ckv, w_uk, kn_g)
    v = (ckv @ w_uv).reshape(B, T, N_HEADS, V_DIM)
    nb = T // QBLK
    qn_b = qn.reshape(B, nb, QBLK, N_HEADS, NOPE).swapaxes(0, 1)
    qr_b = qr.reshape(B, nb, QBLK, N_HEADS, ROPE).swapaxes(0, 1)
    kpos = jnp.arange(T)

    def block(args):
        qn_i, qr_i, i = args
        s = attn_scores(qn_i, qr_i, kn, kr)
        qpos = i * QBLK + jnp.arange(QBLK)
        s = jnp.where(kpos[None, :] <= qpos[:, None], s, NEG)
        p = jax.nn.softmax(s, axis=-1).astype(v.dtype)
        return jnp.einsum('bhqk,bkhv->bqhv', p, v)

    o = lax.map(block, (qn_b, qr_b, jnp.arange(nb)))
    return o.swapaxes(0, 1).reshape(B, T, D_MLA)


def mla_sample(qn, qr, ckv_new, kr_new, cache_latent, cache_krope, layer, page_table, w_uk, w_uv, kn_g):
    DB, T = qn.shape[:2]
    kn_new = key_nope(ckv_new, w_uk, kn_g)
    s = attn_scores(qn, qr, kn_new, kr_new)
    s = jnp.where(jnp.tril(jnp.ones((T, T), dtype=bool)), s, NEG)
    m = jnp.max(s, axis=-1)
    p = jnp.exp(s - m[..., None])
    lsum = jnp.sum(p, axis=-1)
    acc = jnp.einsum('bhqk,bkc->bhqc', p, ckv_new.astype(jnp.float32))

    def step(carry, pt):
        m, lsum, acc = carry
        lat = cache_latent[layer, pt]
        krp = cache_krope[layer, pt]
        kn = key_nope(lat, w_uk, kn_g)
        s = attn_scores(qn, qr, kn, krp)
        m_new = jnp.maximum(m, jnp.max(s, axis=-1))
        corr = jnp.exp(m - m_new)
        p = jnp.exp(s - m_new[..., None])
        lsum = lsum * corr + jnp.sum(p, axis=-1)
        acc = acc * corr[..., None] + jnp.einsum('bhqk,bkc->bhqc', p, lat.astype(jnp.float32))
        return (m_new, lsum, acc), None

    (m, lsum, acc), _ = lax.scan(step, (m, lsum, acc), page_table.T)
    lat_out = (acc / lsum[..., None]).astype(qn.dtype)
    o = jnp.einsum('bhqc,chv->bqhv', lat_out, w_uv.reshape(KV_LORA, N_HEADS, V_DIM))
    return o.reshape(DB, T, D_MLA)


def merge_out(x, gate, y_pool, z_pool, y_conv, z_conv, y_mla, z_mla, w_out):
    mixed = jnp.concatenate([y_pool * jax.nn.silu(z_pool),
                             y_conv * jax.nn.silu(z_conv),
                             y_mla * jax.nn.silu(z_mla)], axis=-1)
    return x + gate[:, None, :] * (mixed @ w_out)


def setup_inputs(seed: int = 0) -> dict:
    key = jax.random.key(seed)
    ks = jax.random.split(key, 32)
    n_pages = PAST_LEN // PAGE_SIZE
    used = DEC_BATCH * n_pages
    n_pool = used + max(1, used // 4)
    f32 = jnp.float32

    def nrm(k, shape, s=1.0):
        return jax.random.normal(k, shape, f32) * s

    def gain(k, shape):
        return 1.0 + 0.1 * jax.random.normal(k, shape, f32)

    page_table = jax.random.permutation(ks[6], n_pool)[:used].reshape(DEC_BATCH, n_pages).astype(jnp.int32)
    return {
        "x_prompt": nrm(ks[0], (BATCH, SEQ, D_MODEL)),
        "x_sample": nrm(ks[1], (DEC_BATCH, DEC_SEQ, D_MODEL)),
        "cache_latent": nrm(ks[2], (DEPTH, n_pool, PAGE_SIZE, KV_LORA)),
        "cache_krope": nrm(ks[3], (DEPTH, n_pool, PAGE_SIZE, ROPE)),
        "state_pool": nrm(ks[4], (DEPTH, DEC_BATCH, POOL_MAX - 1, D_POOL)),
        "state_conv": nrm(ks[5], (DEPTH, DEC_BATCH, CONV_W - 1, D_CONV)),
        "page_table": page_table,
        "c_prompt": nrm(ks[7], (BATCH, D_MODEL)),
        "c_sample": nrm(ks[8], (DEC_BATCH, D_MODEL)),
        "norm_g": gain(ks[9], (DEPTH, D_MODEL)),
        "w_ada": nrm(ks[10], (DEPTH, D_MODEL, 3 * D_MODEL), 0.5 * D_MODEL ** -0.5),
        "b_ada": nrm(ks[11], (DEPTH, 3 * D_MODEL), 0.01),
        "w_in": nrm(ks[12], (DEPTH, D_MODEL, D_IN), D_MODEL ** -0.5),
        "pool_w": nrm(ks[13], (DEPTH, POOL_GROUPS, POOL_GW, POOL_GW), POOL_GW ** -0.5),
        "pool_scale": gain(ks[14], (DEPTH, D_POOL)),
        "conv_w": nrm(ks[15], (DEPTH, CONV_W, D_CONV), CONV_W ** -0.5),
        "q_norm_g": gain(ks[16], (DEPTH, Q_LORA)),
        "w_uq": nrm(ks[17], (DEPTH, Q_LORA, N_HEADS * (NOPE + ROPE)), Q_LORA ** -0.5),
        "qn_g": gain(ks[18], (DEPTH, NOPE)),
        "qr_g": gain(ks[19], (DEPTH, ROPE)),
        "kv_norm_g": gain(ks[20], (DEPTH, KV_LORA)),
        "kr_g": gain(ks[21], (DEPTH, ROPE)),
        "w_uk": nrm(ks[22], (DEPTH, KV_LORA, N_HEADS * NOPE), KV_LORA ** -0.5),
        "kn_g": gain(ks[23], (DEPTH, NOPE)),
        "w_uv": nrm(ks[24], (DEPTH, KV_LORA, N_HEADS * V_DIM), KV_LORA ** -0.5),
        "w_out": nrm(ks[25], (DEPTH, D_MIX, D_MODEL), D_MIX ** -0.5),
    }


def reference(x_prompt, x_sample, cache_latent, cache_krope, state_pool, state_conv, page_table,
              c_prompt, c_sample, norm_g, w_ada, b_ada, w_in, pool_w, pool_scale, conv_w,
              q_norm_g, w_uq, qn_g, qr_g, kv_norm_g, kr_g, w_uk, kn_g, w_uv, w_out):
    past_len = page_table.shape[1] * cache_latent.shape[2]
    pos_p = jnp.arange(x_prompt.shape[1], dtype=jnp.int32)
    pos_s = past_len + jnp.arange(x_sample.shape[1], dtype=jnp.int32)
    bp = x_prompt.shape[0]

    xp = x_prompt
    lat_p, kr_p, pool_p, conv_p = [], [], [], []
    for l in range(DEPTH):
        h, gate = adaln(xp, c_prompt, norm_g[l], w_ada[l], b_ada[l])
        u, zp, hc, bg, cg, zc, cq, ckv_raw, kr_raw, zm = split_columns(h @ w_in[l])
        yp, st_pool = pool_mix(u, jnp.zeros((bp, POOL_MAX - 1, D_POOL), xp.dtype), pos_p,
                               pool_w[l], pool_scale[l])
        yc, st_conv = conv_mix(hc, bg, cg, jnp.zeros((bp, CONV_W - 1, D_CONV), xp.dtype), conv_w[l])
        qn, qr = mla_queries(cq, pos_p, q_norm_g[l], w_uq[l], qn_g[l], qr_g[l])
        ckv, kr = mla_latent(ckv_raw, kr_raw, pos_p, kv_norm_g[l], kr_g[l])
        ym = mla_prompt(qn, qr, ckv, kr, w_uk[l], w_uv[l], kn_g[l])
        xp = merge_out(xp, gate, yp, zp, yc, zc, ym, zm, w_out[l])
        lat_p.append(ckv)
        kr_p.append(kr)
        pool_p.append(st_pool)
        conv_p.append(st_conv)

    xs = x_sample
    lat_s, kr_s, pool_s, conv_s = [], [], [], []
    for l in range(DEPTH):
        h, gate = adaln(xs, c_sample, norm_g[l], w_ada[l], b_ada[l])
        u, zp, hc, bg, cg, zc, cq, ckv_raw, kr_raw, zm = split_columns(h @ w_in[l])
        yp, st_pool = pool_mix(u, state_pool[l], pos_s, pool_w[l], pool_scale[l])
        yc, st_conv = conv_mix(hc, bg, cg, state_conv[l], conv_w[l])
        qn, qr = mla_queries(cq, pos_s, q_norm_g[l], w_uq[l], qn_g[l], qr_g[l])
        ckv, kr = mla_latent(ckv_raw, kr_raw, pos_s, kv_norm_g[l], kr_g[l])
        ym = mla_sample(qn, qr, ckv, kr, cache_latent, cache_krope, l, page_table,
                        w_uk[l], w_uv[l], kn_g[l])
        xs = merge_out(xs, gate, yp, zp, yc, zc, ym, zm, w_out[l])
        lat_s.append(ckv)
        kr_s.append(kr)
        pool_s.append(st_pool)
        conv_s.append(st_conv)

    return (xp, xs,
            jnp.stack(lat_p), jnp.stack(kr_p), jnp.stack(pool_p), jnp.stack(conv_p),
            jnp.stack(lat_s), jnp.stack(kr_s), jnp.stack(pool_s), jnp.stack(conv_s))
```

```python
import math
from contextlib import ExitStack
import numpy as np
import concourse.bass as bass
import concourse.mybir as mybir
from concourse.bass_utils import run_bass_kernel_spmd

F32 = mybir.dt.float32
BF16 = mybir.dt.bfloat16
I32 = mybir.dt.int32
U8 = mybir.dt.uint8
AF = mybir.ActivationFunctionType
ALU = mybir.AluOpType
AX = mybir.AxisListType

D_MODEL = 1024
DEPTH = 2
D_IN = 2464
NOPE, ROPE = 64, 32
EPS = 1e-6
SM_SCALE = 1.0 / math.sqrt(NOPE + ROPE)
ROPE_THETA = 10000.0
PAGE_BYTES = 128 * 160 * 4
NV = 48
C_ID, C_ONES, C_B96, C_ROT, C_INVW, C_INVC, C_MASK, C_MASK4, C_SELQ, C_BDM = 0, 128, 256, 352, 448, 450, 482, 610, 626, 754
NCF = 758


class Buf:
    __slots__ = ("w", "r")

    def __init__(self):
        self.w = None
        self.r = {}


class TT:
    def __init__(self, ap, buf=None, excl=False):
        self.ap = ap
        self.b = buf or Buf()
        self.excl = excl

    def __getitem__(self, k):
        return self.ap[k]


class KB:
    ENG = ("sp", "act", "pe", "dve", "pool")

    def __init__(self, nc, es):
        self.nc = nc
        self.es = es
        self.q = {e: [] for e in self.ENG}
        self.waited = {e: {} for e in self.ENG}
        self.cnt = {}
        self.prog = {}
        for e in ("act", "pe", "dve", "pool"):
            s = es.enter_context(nc.semaphore("prog_" + e))
            self.prog[e] = s
            self.cnt[s] = 0
        self.dsems = []
        for i in range(20):
            s = es.enter_context(nc.semaphore("dq%d" % i))
            self.dsems.append(s)
            self.cnt[s] = 0
        self.dnext = 0
        self.swsems = []
        for i in range(8):
            s = es.enter_context(nc.semaphore("sw%d" % i))
            self.swsems.append(s)
            self.cnt[s] = 0
        self.swnext = 0
        self.nid = 0

    def newsem(self, name):
        s = self.es.enter_context(self.nc.semaphore(name))
        self.cnt[s] = 0
        return s

    def barrier(self):
        toks = [(s, c) for s, c in self.cnt.items() if c > 0]
        for e in self.ENG:
            for tok in toks:
                self.wait_tok(e, tok)

    def sb(self, shape, dt, name=None):
        self.nid += 1
        t = self.es.enter_context(self.nc.sbuf_tensor("sb_" + (name or ("t%d" % self.nid)), list(shape), dt))
        return TT(t)

    def ps(self, name):
        t = self.es.enter_context(self.nc.psum_tensor(name, [128, 512], F32))
        return TT(t, excl=True)

    def emit(self, eng, fn, reads=(), writes=(), dsem=None, ninc=1, noinc=False):
        deps = {}
        xr = [t for t in reads if t.excl]
        if xr:
            reads = [t for t in reads if not t.excl]
            writes = list(writes) + [t for t in xr if t not in writes]

        def add(tok):
            if tok is None:
                return
            s, v = tok
            if deps.get(s, 0) < v:
                deps[s] = v

        for t in reads:
            add(t.b.w)
        for t in writes:
            add(t.b.w)
            for s, v in t.b.r.items():
                add((s, v))
        if eng == "sp" or dsem is not None:
            if dsem is None:
                dsem = self.dsems[self.dnext]
                self.dnext = (self.dnext + 1) % len(self.dsems)
            add((dsem, self.cnt[dsem]))
            s = dsem
            inc = 16 * ninc
        else:
            s = self.prog[eng]
            inc = 1
        w = self.waited[eng]
        for ds, v in deps.items():
            if eng == "pe" and ds is self.prog["pe"]:
                continue
            if w.get(ds, 0) < v:
                self.q[eng].append(("wait", ds, v))
                w[ds] = v
        if noinc:
            tok = (s, self.cnt[s] + 1)
            self.q[eng].append(("op", fn, s, 0))
        else:
            self.cnt[s] += inc
            tok = (s, self.cnt[s])
            self.q[eng].append(("op", fn, s, inc))
        for t in writes:
            t.b.w = tok
            t.b.r = {}
        for t in reads:
            if t.b.r.get(s, 0) < tok[1]:
                t.b.r[s] = tok[1]
        return tok

    def wait_tok(self, eng, tok):
        s, v = tok
        w = self.waited[eng]
        if w.get(s, 0) < v:
            self.q[eng].append(("wait", s, v))
            w[s] = v

    def replay(self, eng, e):
        for it in self.q[eng]:
            if it[0] == "wait":
                e.wait_ge(it[1], it[2])
            else:
                _, fn, s, inc = it
                r = fn(e)
                if inc == 0:
                    continue
                if isinstance(r, list):
                    for x in r:
                        x.then_inc(s, inc // len(r))
                else:
                    r.then_inc(s, inc)

    def mm(self, out, lhsT, rhs, reads, writes, start=True, stop=True, skip=False, inc=None):
        if inc is None:
            inc = True
        return self.emit("pe", lambda e: e.matmul(out, lhsT=lhsT, rhs=rhs, start=start, stop=stop,
                                                  skip_group_check=skip), reads, writes, noinc=not inc)

    def tr(self, out, in_, ident, reads, writes, inc=True):
        return self.emit("pe", lambda e: e.transpose(out=out, in_=in_, identity=ident), reads, writes, noinc=not inc)

    def rsqrt(self, out, in_, scale, eps_ap, reads, writes):
        self.act(out, in_, AF.Ln, reads, writes, bias=eps_ap, scale=scale)
        self.act(out, out, AF.Exp, writes, writes, scale=-0.5)

    def act(self, out, in_, func, reads, writes, bias=None, scale=None):
        kw = {}
        if bias is not None:
            kw["bias"] = bias
        if scale is not None:
            kw["scale"] = scale
        return self.emit("act", lambda e: e.activation(out=out, in_=in_, func=func, **kw), reads, writes)

    def tt(self, eng, out, in0, in1, op, reads, writes):
        return self.emit(eng, lambda e: e.tensor_tensor(out=out, in0=in0, in1=in1, op=op), reads, writes)

    def ts(self, eng, out, in0, s1, op0, reads, writes, s2=None, op1=None):
        if op1 is None:
            return self.emit(eng, lambda e: e.tensor_scalar(out=out, in0=in0, scalar1=s1, scalar2=None, op0=op0),
                             reads, writes)
        return self.emit(eng, lambda e: e.tensor_scalar(out=out, in0=in0, scalar1=s1, scalar2=s2, op0=op0, op1=op1),
                         reads, writes)

    def stt(self, eng, out, in0, scalar, in1, op0, op1, reads, writes):
        return self.emit(eng, lambda e: e.scalar_tensor_tensor(out=out, in0=in0, scalar=scalar, in1=in1,
                                                                op0=op0, op1=op1), reads, writes)

    def cp(self, eng, out, in_, reads, writes):
        if eng == "act":
            return self.act(out, in_, AF.Copy, reads, writes)
        return self.emit(eng, lambda e: e.tensor_copy(out=out, in_=in_), reads, writes)

    def recip(self, out, in_, reads, writes):
        return self.emit("dve", lambda e: e.reciprocal(out=out, in_=in_), reads, writes)

    def memset(self, eng, out, val, writes):
        return self.emit(eng, lambda e: e.memset(out, val), (), writes)

    def dma(self, out, in_, reads, writes, eng="sp", **kw):
        if eng == "sp":
            return self.emit("sp", lambda e: e.dma_start(out=out, in_=in_, **kw), reads, writes)
        ds = self.swsems[self.swnext]
        self.swnext = (self.swnext + 1) % len(self.swsems)
        return self.emit("pool", lambda e: e.dma_start(out=out, in_=in_, **kw), reads, writes, dsem=ds)


def bc(ap, shape):
    return ap.broadcast_to(list(shape))


def build(cfg):
    T, NPG, NPOOL, NS, TQ = cfg["T"], cfg["NPG"], cfg["NPOOL"], cfg["NS"], cfg["TQ"]
    NTB = 128
    NBLK = T // NTB
    NSEQ = 1 + NS
    NST = NS * TQ
    GP = 4
    NGRP = NPG // GP
    nc = bass.Bass("TRN2", target_bir_lowering=False)
    D = {}

    def din(name, shape, dt=F32):
        D[name] = nc.dram_tensor(name, list(shape), dt, kind="ExternalInput").ap()

    def dout(name, shape, dt=F32):
        D[name] = nc.dram_tensor(name, list(shape), dt, kind="ExternalOutput").ap()

    din("xT", [128, 8, T]); din("xsT", [128, 8, NST]); din("cT", [128, 8, NSEQ])
    din("cache0", [NPOOL * 128 * 160]); din("cache1", [NPOOL * 128 * 160])
    din("pt", [NS, NPG], I32)
    din("spT", [2, 128, 2, NS, 15]); din("scT", [2, 128, 2, NS, 2])
    din("wada", [2, 24, 128, 8, 128]); din("win", [2, 128, 8, D_IN]); din("wout", [2, 128, 8, 1024])
    din("wuq", [2, 128, 2, 384]); din("wuk", [2, 128, 256]); din("wuv", [2, 128, 512])
    din("wukT", [2, 64, 4, 128]); din("pw", [2, 128, 2, 128]); din("pvec", [2, 128, NV])
    din("cf", [128, NCF]); din("ropeP", [2, 96, T]); din("ropeS", [2, 96, TQ])
    dout("yT", [128, 8, T]); dout("ysT", [128, 8, NST])
    dout("latP", [2, 128, T]); dout("krP", [2, 32, T]); dout("poolP", [2, 128, 2, 15]); dout("convP", [2, 128, 2, 2])
    dout("latS", [2, 128, NST]); dout("krS", [2, 32, NST]); dout("poolS", [2, 128, 2, NS, 15])
    dout("convS", [2, 128, 2, NS, 2])
    cbytes = [D["cache0"].bitcast(U8), D["cache1"].bitcast(U8)]

    es = ExitStack()
    with es:
        k = KB(nc, es)
        xT = k.sb([128, 8, T], F32, "xT")
        xsT = k.sb([128, 8, NST], F32, "xsT")
        win = k.sb([128, 8, D_IN], BF16, "win")
        wout = k.sb([128, 8, 1024], BF16, "wout")
        wuq = k.sb([128, 2, 416], BF16, "wuq")
        wuk = k.sb([128, 256], BF16, "wuk")
        wuv = k.sb([128, 512], BF16, "wuv")
        wukT = k.sb([64, 4, 128], BF16, "wukT")
        pw = k.sb([128, 2, 128], BF16, "pw")
        pvec = k.sb([128, NV], F32, "pvec")
        cf = k.sb([128, NCF], F32, "cf")
        cfb = k.sb([128, 256], BF16, "cfb")
        siluT = k.sb([128, 8, NSEQ], F32, "siluT")
        modT = k.sb([128, 24, NSEQ], F32, "modT")
        amod = k.sb([128, 8, NSEQ], F32, "amod")
        epsT = k.sb([128, 1], F32, "eps")
        pts = k.sb([NS, NPG], I32, "pts")
        offs = k.sb([NS, NPG], I32, "offs")
        wa = [k.sb([128, 8, 128], F32, "wa0")]
        ropeP = k.sb([96, 2, NTB], F32, "ropeP")
        ropeS = k.sb([96, 2, TQ], F32, "ropeS")
        hT = k.sb([128, 8, NTB], BF16, "hT")
        mixed = k.sb([128, 8, NTB], BF16, "mixed")
        SW = max(NS * (16 + TQ), 16 + NTB)
        WN = max(NTB, NST)
        WQ = max(NTB, 2 * NST)

        def slab(name, dt=F32, w=WN):
            return k.sb([128, w], dt, name)

        sq = [slab("sq%d" % i) for i in range(2)]
        rstd = slab("rstd")
        tmpA = slab("tmpA")
        U = k.sb([128, 2, SW], F32, "U")
        S2 = k.sb([128, 2, SW], F32, "S2")
        S4 = k.sb([128, 2, SW], F32, "S4")
        S8 = slab("S8", w=SW)
        S16 = slab("S16", w=SW)
        szp = k.sb([128, 2, WN], BF16, "szp")
        dT = k.sb([128, 2, WN], BF16, "dT")
        cgs = k.sb([128, 2, WN], F32, "cgs")
        Vc = k.sb([128, 2, SW], F32, "Vc")
        bgs = k.sb([128, 2, WN], F32, "bgs")
        cacc = k.sb([128, 2, WN], F32, "cacc")
        szc = k.sb([128, 2, WN], BF16, "szc")
        cqs = k.sb([128, 2, WN], F32, "cqs")
        cqn = k.sb([128, 2, WN], BF16, "cqn")
        ckr = slab("ckr")
        ckvT = slab("ckvT")
        ckvb = slab("ckvb", BF16)
        qraw = slab("qraw", w=WQ)
        qsq = slab("qsq", w=WQ)
        qr_ = slab("qr_", w=WQ)
        qn = slab("qn", w=WQ)
        qt1 = slab("qt1", w=WQ)
        Qb = k.sb([96, 4, WN], BF16, "Qb")
        krT = slab("krT")
        szm = k.sb([128, 4, WN], BF16, "szm")
        PT = [slab("PT%d" % i, BF16, w=512) for i in range(2)]
        On = [slab("On%d" % i, w=128) for i in range(2)]
        rl = [slab("rl%d" % i, w=2) for i in range(2)]
        NSLOT = 4
        samp_specs = [("KsT", [96, 4, NST], BF16), ("qp", [64, 4, NST], BF16), ("qabs", [128, NS, 16], BF16),
                      ("qr4s", [128, NS, 16], F32), ("BD", [128, NS, 4, 16], BF16)]
        samp_specs += [("pg%d" % i, [128, GP, 160], F32) for i in range(NSLOT)]
        samp_specs += [("nat%d" % i, [128, GP, 130], BF16) for i in range(4)]
        samp_specs += [("latT%d" % i, [128, 512], BF16) for i in range(2)]
        samp_specs += [("krTb%d" % i, [128, 128], BF16) for i in range(2)]
        samp_specs += [("krp%d" % i, [128, 128], F32) for i in range(2)]
        samp_specs += [("sqb%d" % i, [128, 512], BF16) for i in range(4)]
        samp_specs += [("ssq%d" % i, [128, 16], F32) for i in range(2)]
        samp_specs += [("rinv%d" % i, [128, 16], F32) for i in range(2)]
        samp_specs += [("stmp%d" % i, [128, 64], F32) for i in range(2)]
        samp_specs += [("stmp2%d" % i, [128, 64], F32) for i in range(2)]
        samp_specs += [("pTt%d" % i, [128, 64], BF16) for i in range(3)]
        samp_specs += [("pn", [128, 16], F32), ("pnb0", [128, 16], BF16), ("pnb1", [128, 16], BF16),
                       ("natn0", [128, 130], BF16), ("natn1", [128, 130], BF16),
                       ("lo", [128, 128], F32), ("loT", [128, 16], BF16), ("rls", [128, 2], F32)]

        def nbytes(shape, dt):
            n = 1
            for d_ in shape[1:]:
                n *= d_
            return ((n * (4 if dt == F32 else 2) + 31) // 32) * 32
        samp_need = sum(nbytes(sh, dt) for _, sh, dt in samp_specs)
        prompt_need = 4 * T * 2 + (T // 128) * 4 * 130 * 2
        ABYTES = max(samp_need, prompt_need)
        arena = es.enter_context(nc.sbuf_tensor("sb_arena", [128, ABYTES // 2], BF16))

        def aview(off_b, shape, dt):
            n = 1
            for d_ in shape[1:]:
                n *= d_
            nb = n * (4 if dt == F32 else 2)
            ap = arena[0:shape[0], off_b // 2:(off_b + nb) // 2]
            if dt == F32:
                ap = ap.bitcast(F32)
            if len(shape) == 3:
                ap = ap.rearrange("p (a b) -> p a b", a=shape[1])
            elif len(shape) == 4:
                ap = ap.rearrange("p (a b c) -> p a b c", a=shape[1], b=shape[2])
            return TT(ap)
        KT = aview(0, [96, 4, T], BF16)
        VX = aview(4 * T * 2, [128, T // 128, 4, 130], BF16)
        SV = {}
        off_ = 0
        for nm, sh, dt in samp_specs:
            SV[nm] = aview(off_, sh, dt)
            off_ += nbytes(sh, dt)
        KsT, qp, qabs, qr4s, BD = SV["KsT"], SV["qp"], SV["qabs"], SV["qr4s"], SV["BD"]
        pg32 = [SV["pg%d" % i] for i in range(NSLOT)]
        pgsem = [k.newsem("pgsem%d" % i) for i in range(NSLOT)]
        nat = [SV["nat%d" % i] for i in range(4)]
        latT = [SV["latT%d" % i] for i in range(2)]
        krTb = [SV["krTb%d" % i] for i in range(2)]
        krp = [SV["krp%d" % i] for i in range(2)]
        sqb = [SV["sqb%d" % i] for i in range(4)]
        ssq = [SV["ssq%d" % i] for i in range(2)]
        rinv = [SV["rinv%d" % i] for i in range(2)]
        stmp = [SV["stmp%d" % i] for i in range(2)]
        stmp2 = [SV["stmp2%d" % i] for i in range(2)]
        pTt = [SV["pTt%d" % i] for i in range(3)]
        pn, lo, loT, rls = SV["pn"], SV["lo"], SV["loT"], SV["rls"]
        pnb = [SV["pnb0"], SV["pnb1"]]
        natn = [SV["natn0"], SV["natn1"]]
        PS = [k.ps("ps%d" % i) for i in range(8)]
        psn = [0]

        def nps():
            p = PS[psn[0] % 8]
            psn[0] += 1
            return p

        ident = cf[:, C_ID:C_ID + 128]
        ones = cf[:, C_ONES:C_ONES + 128]
        B96 = cf[0:96, C_B96:C_B96 + 96]
        ROT = cf[0:96, C_ROT:C_ROT + 96]
        maskb = cfb[:, 0:128]
        mask4 = cf[0:4, C_MASK4:C_MASK4 + 16]
        SELQ = cfb[0:96, 128:256]
        BDM = cf[:, C_BDM:C_BDM + 4]

        k.dma(cf[:], D["cf"][:, :], (), [cf])
        k.dma(xT[:], D["xT"][:, :, :], (), [xT])
        k.dma(xsT[:], D["xsT"][:, :, :], (), [xsT])
        k.dma(siluT[:], D["cT"][:, :, :], (), [siluT])
        k.dma(pts[:], D["pt"][:, :], (), [pts])
        k.dma(ropeS[:], D["ropeS"].rearrange("c p t -> p c t"), (), [ropeS])
        k.cp("dve", cfb[:, 0:128], cf[:, C_MASK:C_MASK + 128], [cf], [cfb])
        k.cp("dve", cfb[:, 128:256], cf[:, C_SELQ:C_SELQ + 128], [cf], [cfb])
        k.memset("dve", epsT[:], EPS, [epsT])
        k.ts("dve", offs[:], pts[:], float(PAGE_BYTES), ALU.mult, [pts], [offs])
        k.act(siluT[:], siluT[:], AF.Silu, [siluT], [siluT])
        k.memset("pool", wuq[:, :, 384:416], 0.0, [wuq])
        def load_weights(l):
            def cast_dma(dst, src, t):
                k.dma(dst, src, (), [t], eng="pool", max_dma_last_dim=4096)
            for kk in range(8):
                cast_dma(win[:, kk, 0:1232], D["win"][l, :, kk, 0:1232], win)
                cast_dma(win[:, kk, 1232:D_IN], D["win"][l, :, kk, 1232:D_IN], win)
            for kk in range(8):
                cast_dma(wout[:, kk, :], D["wout"][l, :, kk, :], wout)
            cast_dma(wuq[:, :, 0:384], D["wuq"][l], wuq)
            cast_dma(wuk[:], D["wuk"][l], wuk)
            cast_dma(wuv[:], D["wuv"][l], wuv)
            cast_dma(wukT[:], D["wukT"][l], wukT)
            cast_dma(pw[:], D["pw"][l], pw)
            k.dma(pvec[:], D["pvec"][l], (), [pvec])

        def adaln(l):
            pm = nps()
            for j in range(24):
                w_ = wa[0]
                k.dma(w_[:], D["wada"][l, j], (), [w_])
                for kk in range(8):
                    k.mm(pm[:, j * NSEQ:(j + 1) * NSEQ], w_[:, kk, :], siluT[:, kk, :], [w_, siluT], [pm],
                         start=(kk == 0), stop=(kk == 7), inc=(kk == 7))
            k.tt("dve", modT[:], pm[:, 0:24 * NSEQ].rearrange("p (j s) -> p j s", s=NSEQ),
                 bc(pvec[:, 8:32].unsqueeze(2), [128, 24, NSEQ]), ALU.add, [pm, pvec], [modT])
            k.ts("dve", amod[:], modT[:, 8:16, :], 1.0, ALU.add, [modT], [amod])
            k.tt("dve", amod[:], amod[:], bc(pvec[:, 0:8].unsqueeze(2), [128, 8, NSEQ]), ALU.mult, [amod, pvec], [amod])

        def rms_stats(srcs, scale, NT, rd):
            pst = nps()
            n = len(srcs)
            for i, (ap, t) in enumerate(srcs):
                s_ = sq[i % 2]
                k.act(s_[:, 0:NT], ap, AF.Square, [t], [s_])
                k.mm(pst[:, 0:NT], ones, s_[:, 0:NT], [cf, s_], [pst], start=(i == 0), stop=(i == n - 1))
            k.rsqrt(rd[:, 0:NT], pst[:, 0:NT], scale, epsT[:, 0:1], [pst, epsT], [rd])

        def norm_rope(src_ps, P_, W, gcol, cos_ap, sin_ap, out_ap, out_t, NTl, extra_reads=(), nh=1):
            def hv(ap):
                return ap if nh == 1 else ap.rearrange("p (h n) -> p h n", h=nh)
            k.cp("dve", qraw[0:P_, 0:W], src_ps[0:P_, 0:W], [src_ps], [qraw])
            k.act(qsq[0:P_, 0:W], src_ps[0:P_, 0:W], AF.Square, [src_ps], [qsq])
            p2 = nps()
            k.mm(p2[0:P_, 0:W], B96[0:P_, 0:P_], qsq[0:P_, 0:W], [cf, qsq], [p2])
            k.rsqrt(qr_[0:P_, 0:W], p2[0:P_, 0:W], 1.0, epsT[0:P_, 0:1], [p2, epsT], [qr_])
            k.stt("dve", qn[0:P_, 0:W], qraw[0:P_, 0:W], pvec[0:P_, gcol:gcol + 1], qr_[0:P_, 0:W], ALU.mult, ALU.mult,
                  [qraw, pvec, qr_], [qn])
            if cos_ap is None:
                k.cp("dve", out_ap, hv(qn[0:P_, 0:W]), [qn], [out_t])
                return
            p3 = nps()
            k.mm(p3[0:P_, 0:W], ROT[0:P_, 0:P_], qn[0:P_, 0:W], [cf, qn], [p3])
            k.tt("dve", hv(qt1[0:P_, 0:W]), hv(qn[0:P_, 0:W]), cos_ap, ALU.mult, [qn] + list(extra_reads), [qt1])
            k.tt("dve", hv(qn[0:P_, 0:W]), hv(p3[0:P_, 0:W]), sin_ap, ALU.mult, [p3] + list(extra_reads), [qn])
            k.tt("dve", out_ap, hv(qt1[0:P_, 0:W]), hv(qn[0:P_, 0:W]), ALU.add, [qt1, qn], [out_t])

        import os as _os2
        SUB = int(_os2.environ.get("K_SUB", "99"))

        def block_front(l, grp, bi):
            if grp == "P":
                nseq, Tq, NT = 1, NTB, NTB
                xv = xT[:, :, bi * NTB:(bi + 1) * NTB]
                xt = xT
                c0 = bi * NTB
            else:
                nseq, Tq, NT = NS, TQ, NST
                xv = xsT[:, :, :]
                xt = xsT
                c0 = 0
            L = 16 + Tq

            def v3(ap):
                return ap.rearrange("p (s t) -> p s t", t=Tq)

            def ext(tile_ap):
                return tile_ap[:, 0:nseq * L].rearrange("p (s l) -> p s l", l=L)

            for tl in (U, Vc):
                for c in range(2):
                    e_ = ext(tl[:, c, :])
                    if grp == "P":
                        if bi == 0:
                            k.memset("pool", e_[:, :, 0:16], 0.0, [tl])
                        else:
                            k.cp("pool", e_[:, :, 0:16], e_[:, :, Tq:Tq + 16], [tl], [tl])
            if grp == "S":
                for c in range(2):
                    k.memset("pool", ext(U[:, c, :])[:, :, 0:1], 0.0, [U])
                    k.dma(ext(U[:, c, :])[:, :, 1:16], D["spT"][l, :, c, :, :], (), [U])
                    k.dma(ext(Vc[:, c, :])[:, :, 14:16], D["scT"][l, :, c, :, :], (), [Vc])
            if grp == "P":
                k.dma(ropeP[:], D["ropeP"][:, :, c0:c0 + NTB].rearrange("c p t -> p c t"), (), [ropeP])
            rms_stats([(xv[:, kk, :], xt) for kk in range(8)], 1.0 / D_MODEL, NT, rstd)
            for kk in range(8):
                if grp == "P":
                    k.stt("dve", tmpA[:, 0:NT], xv[:, kk, :], amod[:, kk, 0:1], rstd[:, 0:NT], ALU.mult, ALU.mult,
                          [xt, amod, rstd], [tmpA])
                    k.act(hT[:, kk, 0:NT], tmpA[:, 0:NT], AF.Identity, [tmpA, modT], [hT], bias=modT[:, kk, 0:1], scale=1.0)
                else:
                    k.tt("dve", tmpA[:, 0:NT], xv[:, kk, :], rstd[:, 0:NT], ALU.mult, [xt, rstd], [tmpA])
                    k.tt("dve", v3(tmpA[:, 0:NT]), v3(tmpA[:, 0:NT]), bc(amod[:, kk, 1:NSEQ].unsqueeze(2), [128, NS, TQ]),
                         ALU.mult, [tmpA, amod], [tmpA])
                    k.tt("dve", v3(hT[:, kk, 0:NT]), v3(tmpA[:, 0:NT]), bc(modT[:, kk, 1:NSEQ].unsqueeze(2), [128, NS, TQ]),
                         ALU.add, [tmpA, modT], [hT])

            if SUB <= 1: return
            def proj(col, M):
                p = nps()
                for kk in range(8):
                    k.mm(p[0:M, 0:NT], win[:, kk, col:col + M], hT[:, kk, 0:NT], [win, hT], [p], start=(kk == 0), stop=(kk == 7),
                         inc=(kk == 7))
                return p

            for c in range(2):
                p = proj(256 + c * 128, 128)
                k.act(szp[:, c, 0:NT], p[:, 0:NT], AF.Silu, [p], [szp])
            for c in range(2):
                p = proj(1280 + c * 128, 128)
                k.act(szc[:, c, 0:NT], p[:, 0:NT], AF.Silu, [p], [szc])
            for c in range(4):
                p = proj(1952 + c * 128, 128)
                k.act(szm[:, c, 0:NT], p[:, 0:NT], AF.Silu, [p], [szm])
            for c in range(2):
                p = proj(c * 128, 128)
                k.cp("act", ext(U[:, c, :])[:, :, 16:L], v3(p[:, 0:NT]), [p], [U])
            if SUB <= 2: return
            for c in range(2):
                u_, s2_, s4_ = ext(U[:, c, :]), ext(S2[:, c, :]), ext(S4[:, c, :])
                k.tt("pool", s2_[:, :, 1:L], u_[:, :, 1:L], u_[:, :, 0:L - 1], ALU.add, [U], [S2])
                k.tt("pool", s4_[:, :, 3:L], s2_[:, :, 3:L], s2_[:, :, 1:L - 2], ALU.add, [S2], [S4])
            s4_, s8_, s16_ = ext(S4[:, 1, :]), ext(S8[:]), ext(S16[:])
            k.tt("pool", s8_[:, :, 7:L], s4_[:, :, 7:L], s4_[:, :, 3:L - 4], ALU.add, [S4], [S8])
            k.tt("pool", s16_[:, :, 15:L], s8_[:, :, 15:L], s8_[:, :, 7:L - 8], ALU.add, [S8], [S16])
            srcs = {(0, 0): (S2[:, 0, :], S2), (1, 0): (S4[:, 0, :], S4), (0, 1): (S8[:], S8), (1, 1): (S16[:], S16)}
            for c in range(2):
                for hf in range(2):
                    pr = slice(hf * 64, (hf + 1) * 64)
                    sap, st = srcs[(hf, c)]
                    s_ = ext(sap)[pr, :, 16:L]
                    k.stt("dve", v3(dT[pr, c, 0:NT]), s_, cf[pr, C_INVW + c:C_INVW + c + 1], ext(U[:, c, :])[pr, :, 16:L],
                          ALU.mult, ALU.subtract, [st, cf, U], [dT])
                    if grp == "P" and bi == 0:
                        k.tt("dve", tmpA[pr, 0:16], sap[pr, 16:32], cf[pr, C_INVC + c * 16:C_INVC + c * 16 + 16], ALU.mult,
                             [st, cf], [tmpA])
                        k.tt("dve", dT[pr, c, 0:16], tmpA[pr, 0:16], U[pr, c, 16:32], ALU.subtract, [tmpA, U], [dT])
            for c in range(2):
                p = nps()
                k.mm(p[:, 0:NT], pw[:, c, :], dT[:, c, 0:NT], [pw, dT], [p])
                k.stt("dve", mixed[:, c, 0:NT], p[:, 0:NT], pvec[:, 32 + c:33 + c], szp[:, c, 0:NT], ALU.mult, ALU.mult,
                      [p, pvec, szp], [mixed])
            if SUB <= 3: return
            if grp == "P":
                if bi == NBLK - 1:
                    for c in range(2):
                        k.dma(D["poolP"][l, :, c, :], U[:, c, 16 + Tq - 15:16 + Tq], [U], ())
            else:
                for c in range(2):
                    k.dma(D["poolS"][l, :, c, :, :], ext(U[:, c, :])[:, :, L - 15:L], [U], ())

            if SUB <= 4: return
            for c in range(2):
                p = proj(1024 + c * 128, 128)
                k.cp("act", cgs[:, c, 0:NT], p[:, 0:NT], [p], [cgs])
            for c in range(2):
                p = proj(512 + c * 128, 128)
                k.tt("dve", ext(Vc[:, c, :])[:, :, 16:L], v3(p[:, 0:NT]), v3(cgs[:, c, 0:NT]), ALU.mult, [p, cgs], [Vc])
            for c in range(2):
                p = proj(768 + c * 128, 128)
                k.cp("act", bgs[:, c, 0:NT], p[:, 0:NT], [p], [bgs])
            for c in range(2):
                v_ = ext(Vc[:, c, :])
                a_ = v3(cacc[:, c, 0:NT])
                k.ts("pool", a_, v_[:, :, 16:L], pvec[:, 34 + 4 + c:35 + 4 + c], ALU.mult, [Vc, pvec], [cacc])
                k.stt("dve", a_, v_[:, :, 15:L - 1], pvec[:, 34 + 2 + c:35 + 2 + c], a_, ALU.mult, ALU.add, [Vc, pvec, cacc], [cacc])
                k.stt("dve", a_, v_[:, :, 14:L - 2], pvec[:, 34 + c:35 + c], a_, ALU.mult, ALU.add, [Vc, pvec, cacc], [cacc])
                k.tt("pool", cacc[:, c, 0:NT], cacc[:, c, 0:NT], bgs[:, c, 0:NT], ALU.mult, [cacc, bgs], [cacc])
                k.tt("pool", mixed[:, 2 + c, 0:NT], cacc[:, c, 0:NT], szc[:, c, 0:NT], ALU.mult, [cacc, szc], [mixed])
            if grp == "P":
                if bi == NBLK - 1:
                    for c in range(2):
                        k.dma(D["convP"][l, :, c, :], Vc[:, c, 16 + Tq - 2:16 + Tq], [Vc], ())
            else:
                for c in range(2):
                    k.dma(D["convS"][l, :, c, :, :], ext(Vc[:, c, :])[:, :, L - 2:L], [Vc], ())

            if SUB <= 5: return
            for c in range(2):
                p = proj(1536 + c * 128, 128)
                k.cp("act", cqs[:, c, 0:NT], p[:, 0:NT], [p], [cqs])
            p = proj(1792, 128)
            k.cp("act", ckr[:, 0:NT], p[:, 0:NT], [p], [ckr])
            pkr = proj(1856, 128)
            if SUB <= 6: return
            if grp == "P":
                cos1, sin1 = ropeP[:, 0, 0:NT], ropeP[:, 1, 0:NT]
                cos2 = bc(ropeP[:, 0:1, 0:NT], [96, 2, NT])
                sin2 = bc(ropeP[:, 1:2, 0:NT], [96, 2, NT])
                rt = ropeP
            else:
                cos1 = bc(ropeS[:, 0:1, :], [96, NS, TQ])
                sin1 = bc(ropeS[:, 1:2, :], [96, NS, TQ])
                cos2 = bc(ropeS[:, 0:1, :], [96, 2 * NS, TQ])
                sin2 = bc(ropeS[:, 1:2, :], [96, 2 * NS, TQ])
                rt = ropeS
            if grp == "P":
                norm_rope(pkr, 96, NT, 44, cos1, sin1, krT[0:96, 0:NT], krT, NT, [rt])
            else:
                norm_rope_s(pkr, 1, 44, cos1, sin1, krT[0:96, 0:NT], krT, rt)
            if SUB <= 7: return
            Kt = KT if grp == "P" else KsT
            for h in range(4):
                k.cp("dve", Kt[64:96, h, c0:c0 + NT], krT[64:96, 0:NT], [krT], [Kt])
            if grp == "P":
                k.dma(D["krP"][l, :, c0:c0 + NT], krT[64:96, 0:NT], [krT], ())
            else:
                k.dma(D["krS"][l, :, :], krT[64:96, 0:NT], [krT], ())
            if SUB <= 8: return
            rms_stats([(ckr[:, 0:NT], ckr)], 1.0 / 128, NT, rstd)
            k.stt("dve", ckvT[:, 0:NT], ckr[:, 0:NT], pvec[:, 42:43], rstd[:, 0:NT], ALU.mult, ALU.mult, [ckr, pvec, rstd], [ckvT])
            k.cp("dve", ckvb[:, 0:NT], ckvT[:, 0:NT], [ckvT], [ckvb])
            if grp == "P":
                k.dma(D["latP"][l, :, c0:c0 + NT], ckvT[:, 0:NT], [ckvT], ())
            else:
                k.dma(D["latS"][l, :, :], ckvT[:, 0:NT], [ckvT], ())
            if SUB <= 9: return
            rms_stats([(cqs[:, c, 0:NT], cqs) for c in range(2)], 1.0 / 256, NT, rstd)
            for c in range(2):
                k.stt("dve", cqn[:, c, 0:NT], cqs[:, c, 0:NT], pvec[:, 40 + c:41 + c], rstd[:, 0:NT], ALU.mult, ALU.mult,
                      [cqs, pvec, rstd], [cqn])
            if SUB <= 10: return
            if grp == "P":
                for h in range(4):
                    p = nps()
                    for kk in range(2):
                        k.mm(p[0:128, 0:NT], wuq[:, kk, h * 96:h * 96 + 128], cqn[:, kk, 0:NT], [wuq, cqn], [p],
                             start=(kk == 0), stop=(kk == 1))
                    norm_rope(p, 96, NT, 43, cos1, sin1, Qb[0:96, h, 0:NT], Qb, NT, [rt])
            else:
                for hp in range(2):
                    p = nps()
                    for hh in range(2):
                        h = 2 * hp + hh
                        for kk in range(2):
                            k.mm(p[0:128, hh * NT:(hh + 1) * NT], wuq[:, kk, h * 96:h * 96 + 128], cqn[:, kk, 0:NT], [wuq, cqn], [p],
                                 start=(kk == 0), stop=(kk == 1))
                    norm_rope_s(p, 2, 43, cos2, sin2, Qb[0:96, 2 * hp:2 * hp + 2, 0:NT], Qb, rt)
            if SUB <= 11: return
            if grp == "P":
                for h in range(4):
                    p = nps()
                    k.mm(p[0:64, 0:NT], wuk[:, h * 64:(h + 1) * 64], ckvb[:, 0:NT], [wuk, ckvb], [p])
                    norm_rope(p, 64, NT, 44, None, None, Kt[0:64, h, c0:c0 + NT], Kt, NT)
            else:
                for hp in range(2):
                    p = nps()
                    for hh in range(2):
                        h = 2 * hp + hh
                        k.mm(p[0:64, hh * NT:(hh + 1) * NT], wuk[:, h * 64:(h + 1) * 64], ckvb[:, 0:NT], [wuk, ckvb], [p])
                    oap = Kt[0:64, 2 * hp:2 * hp + 2, c0:c0 + NT]
                    norm_rope(p, 64, 2 * NT, 44, None, None, oap, Kt, NT, nh=2)
            if SUB <= 12: return
            if grp == "P":
                for j in range(NT // 128):
                    p = nps()
                    k.mm(p[:, 0:512], ckvb[:, j * 128:(j + 1) * 128], wuv[:], [ckvb, wuv], [p])
                    tj = (c0 // 128) + j
                    k.cp("act", VX[:, tj, :, 0:128], p[:, 0:512].rearrange("p (h v) -> p h v", h=4), [p], [VX])

        def norm_rope_s(src_ps, nh, gcol, cos_ap, sin_ap, out_ap, out_t, rt):
            W = nh * NST
            P_ = 96
            k.cp("dve", qraw[0:P_, 0:W], src_ps[0:P_, 0:W], [src_ps], [qraw])
            k.act(qsq[0:P_, 0:W], src_ps[0:P_, 0:W], AF.Square, [src_ps], [qsq])
            p2 = nps()
            k.mm(p2[0:P_, 0:W], B96, qsq[0:P_, 0:W], [cf, qsq], [p2])
            k.rsqrt(qr_[0:P_, 0:W], p2[0:P_, 0:W], 1.0, epsT[0:P_, 0:1], [p2, epsT], [qr_])
            k.stt("dve", qn[0:P_, 0:W], qraw[0:P_, 0:W], pvec[0:P_, gcol:gcol + 1], qr_[0:P_, 0:W], ALU.mult, ALU.mult,
                  [qraw, pvec, qr_], [qn])
            p3 = nps()
            k.mm(p3[0:P_, 0:W], ROT, qn[0:P_, 0:W], [cf, qn], [p3])

            def v3(ap):
                return ap.rearrange("p (s t) -> p s t", t=TQ)
            k.tt("dve", v3(qt1[0:P_, 0:W]), v3(qn[0:P_, 0:W]), cos_ap, ALU.mult, [qn, rt], [qt1])
            k.tt("dve", v3(qn[0:P_, 0:W]), v3(p3[0:P_, 0:W]), sin_ap, ALU.mult, [p3, rt], [qn])
            if nh == 1:
                k.tt("dve", out_ap, qt1[0:P_, 0:W], qn[0:P_, 0:W], ALU.add, [qt1, qn], [out_t])
            else:
                k.tt("dve", out_ap, qt1[0:P_, 0:W].rearrange("p (h n) -> p h n", h=nh),
                     qn[0:P_, 0:W].rearrange("p (h n) -> p h n", h=nh), ALU.add, [qt1, qn], [out_t])

        def block_back(l, grp, bi):
            if grp == "P":
                NT = NTB
                xv = xT[:, :, bi * NTB:(bi + 1) * NTB]
                xt = xT
            else:
                NT = NST
                xv = xsT[:, :, :]
                xt = xsT
            for j in range(8):
                p = nps()
                for kk in range(8):
                    k.mm(p[:, 0:NT], wout[:, kk, j * 128:(j + 1) * 128], mixed[:, kk, 0:NT], [wout, mixed], [p],
                         start=(kk == 0), stop=(kk == 7), inc=(kk == 7))
                if grp == "P":
                    k.stt("dve", xv[:, j, :], p[:, 0:NT], modT[:, 16 + j, 0:1], xv[:, j, :], ALU.mult, ALU.add,
                          [p, modT, xt], [xt])
                else:
                    k.tt("dve", tmpA[:, 0:NT].rearrange("p (s t) -> p s t", t=TQ), p[:, 0:NT].rearrange("p (s t) -> p s t", t=TQ),
                         bc(modT[:, 16 + j, 1:NSEQ].unsqueeze(2), [128, NS, TQ]), ALU.mult, [p, modT], [tmpA])
                    k.tt("dve", xv[:, j, :], xv[:, j, :], tmpA[:, 0:NT], ALU.add, [xt, tmpA], [xt])

        def prompt_attn(l, bi):
            NT = NTB
            assert NT == 128
            qt = bi
            nkt = qt + 1
            batches = [list(range(s0, min(s0 + 4, nkt))) for s0 in range(0, nkt, 4)]
            nb = len(batches)

            def S_stage(h, bidx):
                kts = batches[bidx]
                pss = PS[4 + bidx % 2]
                pt_ = PT[bidx % 2]
                for i, kt in enumerate(kts):
                    k.mm(pss[:, i * 128:(i + 1) * 128], KT[0:96, h, kt * 128:(kt + 1) * 128], Qb[0:96, h, 0:NT], [KT, Qb], [pss],
                         inc=(i == len(kts) - 1))
                w = len(kts) * 128
                k.act(pt_[:, 0:w], pss[:, 0:w], AF.Exp, [pss], [pt_], scale=SM_SCALE)
                if kts[-1] == qt:
                    i = len(kts) - 1
                    k.tt("pool", pt_[:, i * 128:(i + 1) * 128], pt_[:, i * 128:(i + 1) * 128], maskb, ALU.mult, [pt_, cfb], [pt_])

            def PV_stage(h, bidx):
                kts = batches[bidx]
                pt_ = PT[bidx % 2]
                pso = PS[h % 2]
                for i, kt in enumerate(kts):
                    k.mm(pso[:, 0:129], pt_[:, i * 128:(i + 1) * 128], VX[:, kt, h, 0:129], [pt_, VX], [pso],
                         start=(kt == 0), stop=(kt == qt), inc=(i == len(kts) - 1))

            def epilogue(h):
                pso = PS[h % 2]
                r_ = rl[h % 2]
                o_ = On[h % 2]
                k.recip(r_[:, 0:1], pso[:, 128:129], [pso], [r_])
                k.ts("dve", o_[:, 0:128], pso[:, 0:128], r_[:, 0:1], ALU.mult, [pso, r_], [o_])
                ptr = PS[6 + h % 2]
                k.tr(ptr[:, 0:128], o_[:, 0:128], ident, [o_, cf], [ptr])
                k.tt("dve", mixed[:, 4 + h, 0:128], ptr[:, 0:128], szm[:, h, 0:128], ALU.mult, [ptr, szm], [mixed])

            for h in range(4):
                S_stage(h, 0)
                if h > 0:
                    epilogue(h - 1)
                for b_ in range(nb):
                    if b_ + 1 < nb:
                        S_stage(h, b_ + 1)
                    PV_stage(h, b_)
            epilogue(3)

        def sample_attn(l):
            k.ts("dve", qp[:, :, :], Qb[0:64, :, 0:NST], pvec[0:64, 44:45], ALU.mult, [Qb, pvec], [qp])
            p = nps()
            for h in range(4):
                k.mm(p[:, h * NST:(h + 1) * NST], wukT[:, h, :], qp[:, h, :], [wukT, qp], [p])
            k.cp("dve", qabs[:].rearrange("p b (h q) -> p h b q", h=4),
                 p[:, 0:4 * NST].rearrange("p (h b q) -> p h b q", h=4, b=NS), [p], [qabs])
            p = nps()
            for h in range(4):
                k.mm(p[:, h * NST:(h + 1) * NST], SELQ, Qb[0:96, h, 0:NST], [cfb, Qb], [p])
            k.cp("dve", qr4s[:].rearrange("p b (h q) -> p h b q", h=4),
                 p[:, 0:4 * NST].rearrange("p (h b q) -> p h b q", h=4, b=NS), [p], [qr4s])
            k.tt("dve", BD[:], bc(qr4s[:].unsqueeze(2), [128, NS, 4, 16]),
                 bc(BDM.unsqueeze(1).unsqueeze(3), [128, NS, 4, 16]), ALU.mult, [qr4s, cf], [BD])

            psT = [PS[0], PS[1]]
            psKR = [PS[2], PS[3]]
            psA = [PS[4], PS[5]]
            psK = PS[6]
            psAcc = PS[7]
            groups = [(b, g) for b in range(NS) for g in range(NGRP)]
            NG = len(groups)
            NNAT = len(nat)

            def load(i):
                b, g = groups[i]
                s_ = i % NSLOT
                pg = pg32[s_]
                dst = pg[:].bitcast(U8)

                def fn(e):
                    regs = [e.alloc_register("pg%d_%d_%d" % (l, i, j)) for j in range(GP)]
                    e.reg_load(regs, offs[b:b + 1, g * GP:(g + 1) * GP])
                    ins = []
                    for j in range(GP):
                        v = e.snap(regs[j], donate=True, min_val=0, max_val=(NPOOL - 1) * PAGE_BYTES)
                        ins.append(e.dma_start(out=dst[:, j, :],
                                               in_=cbytes[l][bass.ds(v, PAGE_BYTES)].rearrange("(p f) -> p f", p=128)))
                    for r in regs:
                        e.free_register(r)
                    return ins
                k.emit("sp", fn, [offs], [pg], dsem=pgsem[s_], ninc=GP)

            def stageT(i):
                pg = pg32[i % NSLOT]
                n_ = nat[i % NNAT]
                kp_ = krp[i % 2]
                k.cp("pool", kp_[:, 0:128].rearrange("p (g r) -> p g r", g=GP), pg[:, :, 128:160], [pg], [kp_])
                k.cp("pool", n_[:, :, 0:128], pg[:, :, 0:128], [pg], [n_])
                pt_ = psT[i % 2]
                for j in range(GP):
                    k.tr(pt_[:, j * 128:(j + 1) * 128], pg[:, j, 0:128], ident, [pg, cf], [pt_], inc=(j == GP - 1))
                k.tr(psK[:, 0:128], kp_[:, 0:128], ident, [kp_, cf], [psK])
                lt = latT[i % 2]
                k.cp("act", lt[:, 0:512], pt_[:, 0:512], [pt_], [lt])
                kb_ = krTb[i % 2]
                k.cp("dve", kb_[:, 0:128], psK[:, 0:128], [psK], [kb_])

            def stageK1(i):
                b, g = groups[i]
                lt = latT[i % 2]
                kb_ = krTb[i % 2]
                pa = psA[i % 2]
                for j in range(GP):
                    pk = psKR[j // 2]
                    k.mm(pk[:, (j % 2) * 256:(j % 2) * 256 + 256], lt[:, j * 128:(j + 1) * 128], wuk[:], [lt, wuk], [pk],
                         inc=(j % 2 == 1))
                    k.mm(pa[:, j * 16:(j + 1) * 16], lt[:, j * 128:(j + 1) * 128], qabs[:, b, :], [lt, qabs], [pa], inc=False)
                k.mm(pa[:, 64:128], kb_[:, 0:128], BD[:, b, :, :].rearrange("p g n -> p (g n)"), [kb_, BD], [pa])
                sq_ = ssq[i % 2]
                for hb in range(2):
                    sb_ = sqb[(2 * i + hb) % 4]
                    k.act(sb_[:, 0:512], psKR[hb][:, 0:512], AF.Square, [psKR[hb]], [sb_])
                    k.emit("dve", lambda e, sb_=sb_, sq_=sq_, hb=hb: e.tensor_reduce(
                        out=sq_[:, hb * 8:(hb + 1) * 8], in_=sb_[:, 0:512].rearrange("p (a d) -> p a d", d=64),
                        axis=AX.X, op=ALU.add), [sb_], [sq_])

            def stageK2a(i):
                sq_ = ssq[i % 2]
                pa = psA[i % 2]
                ri = rinv[i % 2]
                k.rsqrt(ri[:, 0:16], sq_[:, 0:16], 1.0 / 64, epsT[:, 0:1], [sq_, epsT], [ri])
                t1, t2 = stmp[i % 2], stmp2[i % 2]
                k.tt("dve", t1[:, 0:64].rearrange("p (a q) -> p a q", q=4), pa[:, 0:64].rearrange("p (a q) -> p a q", q=4),
                     bc(ri[:, 0:16].unsqueeze(2), [128, 16, 4]), ALU.mult, [pa, ri], [t1])
                k.tt("dve", t2[:, 0:64], t1[:, 0:64], pa[:, 64:128], ALU.add, [t1, pa], [t2])

            def stageK2b(i):
                t2 = stmp2[i % 2]
                pp = pTt[i % 3]
                k.act(pp[:, 0:64], t2[:, 0:64], AF.Exp, [t2], [pp], scale=SM_SCALE)

            def stageAcc(i):
                b, g = groups[i]
                n_ = nat[i % NNAT]
                pp = pTt[i % 3]
                if g == 0:
                    k.mm(psAcc[0:16, 0:129], pnb[b % 2][0:4, 0:16], natn[b % 2][0:4, 0:129], [pnb[b % 2], natn[b % 2]], [psAcc],
                         start=True, stop=False, skip=True, inc=False)
                for j in range(GP):
                    last = (g == NGRP - 1 and j == GP - 1)
                    k.mm(psAcc[0:16, 0:129], pp[:, j * 16:(j + 1) * 16], n_[:, j, 0:129], [pp, n_], [psAcc],
                         start=False, stop=last, skip=True, inc=(j == GP - 1))
                if g == NGRP - 1:
                    finish(b)

            def start_sample(b):
                cs = slice(b * TQ, (b + 1) * TQ)
                pn_, pnb_, natn_ = pn, pnb[b % 2], natn[b % 2]
                for h in range(4):
                    k.mm(psK[0:4, 128 + h * 4:128 + (h + 1) * 4], KsT[0:96, h, cs], Qb[0:96, h, cs], [KsT, Qb], [psK],
                         inc=(h == 3))
                k.act(pn_[0:4, 0:16], psK[0:4, 128:144], AF.Exp, [psK], [pn_], scale=SM_SCALE)
                k.tt("dve", pnb_[0:4, 0:16], pn_[0:4, 0:16], mask4, ALU.mult, [pn_, cf], [pnb_])
                k.tr(psK[0:4, 256:384], ckvT[:, cs], ident, [ckvT, cf], [psK])
                k.cp("dve", natn_[0:4, 0:128], psK[0:4, 256:384], [psK], [natn_])

            def finish(b):
                cs = slice(b * TQ, (b + 1) * TQ)
                k.recip(rls[0:16, 0:1], psAcc[0:16, 128:129], [psAcc], [rls])
                k.ts("dve", lo[0:16, 0:128], psAcc[0:16, 0:128], rls[0:16, 0:1], ALU.mult, [psAcc, rls], [lo])
                k.tr(psK[:, 384:400], lo[0:16, 0:128], cf[0:16, C_ID:C_ID + 16], [lo, cf], [psK])
                k.cp("dve", loT[:, 0:16], psK[:, 384:400], [psK], [loT])
                for h in range(4):
                    k.mm(psK[:, 400 + h * 4:400 + (h + 1) * 4], wuv[:, h * 128:(h + 1) * 128], loT[:, h * 4:(h + 1) * 4],
                         [wuv, loT], [psK], inc=(h == 3))
                k.tt("dve", mixed[:, 4:8, cs], psK[:, 400:416].rearrange("p (h q) -> p h q", h=4), szm[:, :, cs], ALU.mult,
                     [psK, szm], [mixed])

            PF = 3
            for i in range(min(PF, NG)):
                load(i)
            for i in range(-1, NG + 2):
                if PF <= i + PF < NG:
                    load(i + PF)
                if 0 <= i - 1 < NG:
                    stageK2a(i - 1)
                if 0 <= i + 1 < NG:
                    stageT(i + 1)
                if 0 <= i < NG:
                    if groups[i][1] == 0:
                        start_sample(groups[i][0])
                    stageK1(i)
                if 0 <= i - 1 < NG:
                    stageK2b(i - 1)
                if 0 <= i - 2 < NG:
                    stageAcc(i - 2)

        import os as _os
        STOP = int(_os.environ.get("K_STOP", "99"))
        for l in range(DEPTH):
            if STOP <= 0: break
            load_weights(l)
            if STOP <= 1: break
            adaln(l)
            if STOP <= 2: break
            k.barrier()
            k.memset("pool", VX[:, :, :, 128:130], 1.0, [VX])
            for bi in range(NBLK):
                block_front(l, "P", bi)
                if STOP <= 3: break
                prompt_attn(l, bi)
                if STOP <= 4: break
                block_back(l, "P", bi)
            if STOP <= 5: break
            k.barrier()
            for n_ in nat:
                k.memset("pool", n_[:, :, 128:130], 1.0, [n_])
            for n_ in natn:
                k.memset("pool", n_[:, 128:130], 1.0, [n_])
            block_front(l, "S", 0)
            if STOP <= 6: break
            sample_attn(l)
            if STOP <= 7: break
            block_back(l, "S", 0)
            if STOP <= 8: break
        k.dma(D["yT"][:, :, :], xT[:], [xT], ())
        k.dma(D["ysT"][:, :, :], xsT[:], [xsT], ())
        for s in list(k.dsems) + list(pgsem) + list(k.swsems):
            if k.cnt[s] > 0:
                k.wait_tok("sp", (s, k.cnt[s]))

        with nc.Block() as block:
            @block.sync
            def _(e):
                k.replay("sp", e)

            @block.scalar
            def _(e):
                k.replay("act", e)

            @block.tensor
            def _(e):
                k.replay("pe", e)

            @block.vector
            def _(e):
                k.replay("dve", e)

            @block.gpsimd
            def _(e):
                k.replay("pool", e)
    return nc


def _fm(a):
    r, f = a.shape
    return np.ascontiguousarray(a.reshape(r, f // 128, 128).transpose(2, 1, 0))


def _consts(T, TQ, past_len):
    cf = np.zeros((128, NCF), np.float32)
    cf[:, C_ID:C_ID + 128] = np.eye(128, dtype=np.float32)
    cf[:, C_ONES:C_ONES + 128] = 1.0
    cf[0:64, C_B96:C_B96 + 64] = 1.0 / 64
    cf[64:96, C_B96 + 64:C_B96 + 96] = 1.0 / 32
    for i in range(16):
        cf[64 + 16 + i, C_ROT + 64 + i] = -1.0
        cf[64 + i, C_ROT + 64 + 16 + i] = 1.0
    wins = (2, 4, 8, 16)
    for c in range(2):
        for hf in range(2):
            w = wins[2 * c + hf]
            cf[hf * 64:(hf + 1) * 64, C_INVW + c] = 1.0 / w
            for t in range(16):
                cf[hf * 64:(hf + 1) * 64, C_INVC + c * 16 + t] = 1.0 / min(t + 1, w)
    kk = np.arange(128)[:, None]
    qq = np.arange(128)[None, :]
    cf[:, C_MASK:C_MASK + 128] = (kk <= qq).astype(np.float32)
    for kq in range(4):
        for h in range(4):
            for q in range(4):
                cf[kq, C_MASK4 + h * 4 + q] = 1.0 if kq <= q else 0.0
    for g in range(4):
        for r in range(32):
            cf[64 + r, C_SELQ + g * 32 + r] = 1.0
        cf[g * 32:(g + 1) * 32, C_BDM + g] = 1.0
    inv_freq = (1.0 / (ROPE_THETA ** (np.arange(0, ROPE, 2, dtype=np.float32) / np.float32(ROPE)))).astype(np.float32)

    def tab(pos):
        ang = pos.astype(np.float32)[:, None] * inv_freq[None, :]
        cos = np.concatenate([np.cos(ang), np.cos(ang)], -1).astype(np.float32)
        sin = np.concatenate([np.sin(ang), np.sin(ang)], -1).astype(np.float32)
        o = np.zeros((2, 96, len(pos)), np.float32)
        o[0, 0:64] = 1.0
        o[0, 64:96] = cos.T
        o[1, 64:96] = sin.T
        return o
    return cf, tab(np.arange(T)), tab(past_len + np.arange(TQ))


_NC_CACHE = {}


def kernel(x_prompt, x_sample, cache_latent, cache_krope, state_pool, state_conv, page_table,
           c_prompt, c_sample, norm_g, w_ada, b_ada, w_in, pool_w, pool_scale, conv_w,
           q_norm_g, w_uq, qn_g, qr_g, kv_norm_g, kr_g, w_uk, kn_g, w_uv, w_out, _ncores=None):
    f = lambda a: np.asarray(a, dtype=np.float32)
    x_prompt, x_sample, cache_latent, cache_krope = f(x_prompt), f(x_sample), f(cache_latent), f(cache_krope)
    B, T, _ = x_prompt.shape
    DB, TQ, _ = x_sample.shape
    ncores = _ncores or B
    NS = DB // ncores
    NPOOL = cache_latent.shape[1]
    NPG = page_table.shape[1]
    past_len = NPG * 128
    cfg = dict(T=T, NPG=NPG, NPOOL=NPOOL, NS=NS, TQ=TQ)
    key = tuple(sorted(cfg.items()))
    if key not in _NC_CACHE:
        _NC_CACHE[key] = build(cfg)
    nc = _NC_CACHE[key]

    cf, ropeP, ropeS = _consts(T, TQ, past_len)
    caches = [np.ascontiguousarray(np.concatenate([cache_latent[l], cache_krope[l]], axis=-1)).reshape(-1) for l in range(2)]
    wada = np.ascontiguousarray(f(w_ada).reshape(2, 8, 128, 24, 128).transpose(0, 3, 2, 1, 4))
    win = np.ascontiguousarray(f(w_in).reshape(2, 8, 128, D_IN).transpose(0, 2, 1, 3))
    wout = np.ascontiguousarray(f(w_out).reshape(2, 8, 128, 1024).transpose(0, 2, 1, 3))
    wuq = np.ascontiguousarray(f(w_uq).reshape(2, 2, 128, 384).transpose(0, 2, 1, 3))
    wuk = np.ascontiguousarray(f(w_uk))
    wuv = np.ascontiguousarray(f(w_uv))
    wukT = np.ascontiguousarray(f(w_uk).reshape(2, 128, 4, 64).transpose(0, 3, 2, 1))
    pw = np.zeros((2, 128, 2, 128), np.float32)
    pwf = f(pool_w)
    for c in range(2):
        pw[:, 0:64, c, 0:64] = pwf[:, 2 * c]
        pw[:, 64:128, c, 64:128] = pwf[:, 2 * c + 1]
    pvec = np.zeros((2, 128, NV), np.float32)
    pvec[:, :, 0:8] = f(norm_g).reshape(2, 8, 128).transpose(0, 2, 1)
    pvec[:, :, 8:32] = f(b_ada).reshape(2, 24, 128).transpose(0, 2, 1)
    pvec[:, :, 32:34] = f(pool_scale).reshape(2, 2, 128).transpose(0, 2, 1)
    cw = f(conv_w).reshape(2, 3, 2, 128)
    for t in range(3):
        for c in range(2):
            pvec[:, :, 34 + 2 * t + c] = cw[:, t, c]
    pvec[:, :, 40:42] = f(q_norm_g).reshape(2, 2, 128).transpose(0, 2, 1)
    pvec[:, :, 42] = f(kv_norm_g)
    pvec[:, 0:64, 43] = f(qn_g)
    pvec[:, 64:96, 43] = f(qr_g)
    pvec[:, 0:64, 44] = f(kn_g)
    pvec[:, 64:96, 44] = f(kr_g)
    sp = f(state_pool)
    sc = f(state_conv)
    pt = np.asarray(page_table, dtype=np.int32)
    cp_, cs_ = f(c_prompt), f(c_sample)
    in_maps = []
    for c in range(ncores):
        sl = slice(c * NS, (c + 1) * NS)
        m = {
            "xT": _fm(x_prompt[c]),
            "xsT": _fm(x_sample[sl].reshape(NS * TQ, D_MODEL)),
            "cT": _fm(np.concatenate([cp_[c:c + 1], cs_[sl]], 0)),
            "cache0": caches[0], "cache1": caches[1],
            "pt": np.ascontiguousarray(pt[sl]),
            "spT": np.ascontiguousarray(sp[:, sl].reshape(2, NS, 15, 2, 128).transpose(0, 4, 3, 1, 2)),
            "scT": np.ascontiguousarray(sc[:, sl].reshape(2, NS, 2, 2, 128).transpose(0, 4, 3, 1, 2)),
            "wada": wada, "win": win, "wout": wout, "wuq": wuq, "wuk": wuk, "wuv": wuv, "wukT": wukT, "pw": pw,
            "pvec": pvec, "cf": cf, "ropeP": ropeP, "ropeS": ropeS,
        }
        in_maps.append(m)
    res = run_bass_kernel_spmd(nc, in_maps, core_ids=list(range(ncores)))
    R = res.results

    def unfm(a):
        return np.ascontiguousarray(a.transpose(2, 1, 0).reshape(a.shape[2], -1))
    y_p = np.stack([unfm(R[c]["yT"]) for c in range(ncores)], 0)
    y_s = np.concatenate([unfm(R[c]["ysT"]).reshape(NS, TQ, D_MODEL) for c in range(ncores)], 0)
    lat_p = np.stack([R[c]["latP"].transpose(0, 2, 1) for c in range(ncores)], 1)
    kr_p = np.stack([R[c]["krP"].transpose(0, 2, 1) for c in range(ncores)], 1)
    pool_p = np.stack([R[c]["poolP"].transpose(0, 3, 2, 1).reshape(2, 15, 256) for c in range(ncores)], 1)
    conv_p = np.stack([R[c]["convP"].transpose(0, 3, 2, 1).reshape(2, 2, 256) for c in range(ncores)], 1)
    lat_s = np.concatenate([R[c]["latS"].transpose(0, 2, 1).reshape(2, NS, TQ, 128) for c in range(ncores)], 1)
    kr_s = np.concatenate([R[c]["krS"].transpose(0, 2, 1).reshape(2, NS, TQ, 32) for c in range(ncores)], 1)
    pool_s = np.concatenate([R[c]["poolS"].transpose(0, 3, 4, 2, 1).reshape(2, NS, 15, 256) for c in range(ncores)], 1)
    conv_s = np.concatenate([R[c]["convS"].transpose(0, 3, 4, 2, 1).reshape(2, NS, 2, 256) for c in range(ncores)], 1)
    outs = (y_p, y_s, lat_p, kr_p, pool_p, conv_p, lat_s, kr_s, pool_s, conv_s)
    return tuple(np.ascontiguousarray(o, dtype=np.float32) for o in outs)
```

```python
import math
from contextlib import ExitStack
import numpy as np
import concourse.bass as bass
import concourse.mybir as mybir
from concourse.bass_utils import run_bass_kernel_spmd

F32 = mybir.dt.float32
BF16 = mybir.dt.bfloat16
I32 = mybir.dt.int32
U8 = mybir.dt.uint8
AF = mybir.ActivationFunctionType
ALU = mybir.AluOpType
AX = mybir.AxisListType

D_MODEL = 1024
DEPTH = 2
D_IN = 2464
NOPE, ROPE = 64, 32
EPS = 1e-6
SM_SCALE = 1.0 / math.sqrt(NOPE + ROPE)
ROPE_THETA = 10000.0
PAGE_BYTES = 128 * 160 * 4
NV = 48
C_ID, C_ONES, C_B96, C_ROT, C_INVW, C_INVC, C_MASK, C_MASK4, C_SELQ, C_BDM = 0, 128, 256, 352, 448, 450, 482, 610, 626, 754
NCF = 758


class Buf:
    __slots__ = ("w", "r")

    def __init__(self):
        self.w = None
        self.r = {}


class TT:
    def __init__(self, ap, buf=None, excl=False):
        self.ap = ap
        self.b = buf or Buf()
        self.excl = excl

    def __getitem__(self, k):
        return self.ap[k]


class KB:
    ENG = ("sp", "act", "pe", "dve", "pool")

    def __init__(self, nc, es):
        self.nc = nc
        self.es = es
        self.q = {e: [] for e in self.ENG}
        self.waited = {e: {} for e in self.ENG}
        self.cnt = {}
        self.prog = {}
        for e in ("act", "pe", "dve", "pool"):
            s = es.enter_context(nc.semaphore("prog_" + e))
            self.prog[e] = s
            self.cnt[s] = 0
        self.dsems = []
        for i in range(20):
            s = es.enter_context(nc.semaphore("dq%d" % i))
            self.dsems.append(s)
            self.cnt[s] = 0
        self.dnext = 0
        self.swsems = []
        for i in range(8):
            s = es.enter_context(nc.semaphore("sw%d" % i))
            self.swsems.append(s)
            self.cnt[s] = 0
        self.swnext = 0
        self.nid = 0

    def newsem(self, name):
        s = self.es.enter_context(self.nc.semaphore(name))
        self.cnt[s] = 0
        return s

    def barrier(self):
        toks = [(s, c) for s, c in self.cnt.items() if c > 0]
        for e in self.ENG:
            for tok in toks:
                self.wait_tok(e, tok)

    def sb(self, shape, dt, name=None):
        self.nid += 1
        t = self.es.enter_context(self.nc.sbuf_tensor("sb_" + (name or ("t%d" % self.nid)), list(shape), dt))
        return TT(t)

    def ps(self, name):
        t = self.es.enter_context(self.nc.psum_tensor(name, [128, 512], F32))
        return TT(t, excl=True)

    def emit(self, eng, fn, reads=(), writes=(), dsem=None, ninc=1, noinc=False):
        deps = {}
        xr = [t for t in reads if t.excl]
        if xr:
            reads = [t for t in reads if not t.excl]
            writes = list(writes) + [t for t in xr if t not in writes]

        def add(tok):
            if tok is None:
                return
            s, v = tok
            if deps.get(s, 0) < v:
                deps[s] = v

        for t in reads:
            add(t.b.w)
        for t in writes:
            add(t.b.w)
            for s, v in t.b.r.items():
                add((s, v))
        if eng == "sp" or dsem is not None:
            if dsem is None:
                dsem = self.dsems[self.dnext]
                self.dnext = (self.dnext + 1) % len(self.dsems)
            add((dsem, self.cnt[dsem]))
            s = dsem
            inc = 16 * ninc
        else:
            s = self.prog[eng]
            inc = 1
        w = self.waited[eng]
        for ds, v in deps.items():
            if eng == "pe" and ds is self.prog["pe"]:
                continue
            if w.get(ds, 0) < v:
                self.q[eng].append(("wait", ds, v))
                w[ds] = v
        if noinc:
            tok = (s, self.cnt[s] + 1)
            self.q[eng].append(("op", fn, s, 0))
        else:
            self.cnt[s] += inc
            tok = (s, self.cnt[s])
            self.q[eng].append(("op", fn, s, inc))
        for t in writes:
            t.b.w = tok
            t.b.r = {}
        for t in reads:
            if t.b.r.get(s, 0) < tok[1]:
                t.b.r[s] = tok[1]
        return tok

    def wait_tok(self, eng, tok):
        s, v = tok
        w = self.waited[eng]
        if w.get(s, 0) < v:
            self.q[eng].append(("wait", s, v))
            w[s] = v

    def replay(self, eng, e):
        for it in self.q[eng]:
            if it[0] == "wait":
                e.wait_ge(it[1], it[2])
            else:
                _, fn, s, inc = it
                r = fn(e)
                if inc == 0:
                    continue
                if isinstance(r, list):
                    for x in r:
                        x.then_inc(s, inc // len(r))
                else:
                    r.then_inc(s, inc)

    def mm(self, out, lhsT, rhs, reads, writes, start=True, stop=True, skip=False, inc=None):
        if inc is None:
            inc = True
        return self.emit("pe", lambda e: e.matmul(out, lhsT=lhsT, rhs=rhs, start=start, stop=stop,
                                                  skip_group_check=skip), reads, writes, noinc=not inc)

    def tr(self, out, in_, ident, reads, writes, inc=True):
        return self.emit("pe", lambda e: e.transpose(out=out, in_=in_, identity=ident), reads, writes, noinc=not inc)

    def rsqrt(self, out, in_, scale, eps_ap, reads, writes):
        self.act(out, in_, AF.Ln, reads, writes, bias=eps_ap, scale=scale)
        self.act(out, out, AF.Exp, writes, writes, scale=-0.5)

    def act(self, out, in_, func, reads, writes, bias=None, scale=None):
        kw = {}
        if bias is not None:
            kw["bias"] = bias
        if scale is not None:
            kw["scale"] = scale
        return self.emit("act", lambda e: e.activation(out=out, in_=in_, func=func, **kw), reads, writes)

    def tt(self, eng, out, in0, in1, op, reads, writes):
        return self.emit(eng, lambda e: e.tensor_tensor(out=out, in0=in0, in1=in1, op=op), reads, writes)

    def ts(self, eng, out, in0, s1, op0, reads, writes, s2=None, op1=None):
        if op1 is None:
            return self.emit(eng, lambda e: e.tensor_scalar(out=out, in0=in0, scalar1=s1, scalar2=None, op0=op0),
                             reads, writes)
        return self.emit(eng, lambda e: e.tensor_scalar(out=out, in0=in0, scalar1=s1, scalar2=s2, op0=op0, op1=op1),
                         reads, writes)

    def stt(self, eng, out, in0, scalar, in1, op0, op1, reads, writes):
        return self.emit(eng, lambda e: e.scalar_tensor_tensor(out=out, in0=in0, scalar=scalar, in1=in1,
                                                                op0=op0, op1=op1), reads, writes)

    def cp(self, eng, out, in_, reads, writes):
        if eng == "act":
            return self.act(out, in_, AF.Copy, reads, writes)
        return self.emit(eng, lambda e: e.tensor_copy(out=out, in_=in_), reads, writes)

    def recip(self, out, in_, reads, writes):
        return self.emit("dve", lambda e: e.reciprocal(out=out, in_=in_), reads, writes)

    def memset(self, eng, out, val, writes):
        return self.emit(eng, lambda e: e.memset(out, val), (), writes)

    def dma(self, out, in_, reads, writes, eng="sp", **kw):
        if eng == "sp":
            return self.emit("sp", lambda e: e.dma_start(out=out, in_=in_, **kw), reads, writes)
        ds = self.swsems[self.swnext]
        self.swnext = (self.swnext + 1) % len(self.swsems)
        return self.emit("pool", lambda e: e.dma_start(out=out, in_=in_, **kw), reads, writes, dsem=ds)


def bc(ap, shape):
    return ap.broadcast_to(list(shape))


def build(cfg):
    T, NPG, NPOOL, NS, TQ = cfg["T"], cfg["NPG"], cfg["NPOOL"], cfg["NS"], cfg["TQ"]
    NTB = 128
    NBLK = T // NTB
    NSEQ = 1 + NS
    NST = NS * TQ
    GP = 4
    NGRP = NPG // GP
    nc = bass.Bass("TRN2", target_bir_lowering=False)
    D = {}

    def din(name, shape, dt=F32):
        D[name] = nc.dram_tensor(name, list(shape), dt, kind="ExternalInput").ap()

    def dout(name, shape, dt=F32):
        D[name] = nc.dram_tensor(name, list(shape), dt, kind="ExternalOutput").ap()

    din("xT", [128, 8, T]); din("xsT", [128, 8, NST]); din("cT", [128, 8, NSEQ])
    din("cache0", [NPOOL * 128 * 160]); din("cache1", [NPOOL * 128 * 160])
    din("pt", [NS, NPG], I32)
    din("spT", [2, 128, 2, NS, 15]); din("scT", [2, 128, 2, NS, 2])
    din("wada", [2, 24, 128, 8, 128]); din("win", [2, 128, 8, D_IN]); din("wout", [2, 128, 8, 1024])
    din("wuq", [2, 128, 2, 384]); din("wuk", [2, 128, 256]); din("wuv", [2, 128, 512])
    din("wukT", [2, 64, 4, 128]); din("pw", [2, 128, 2, 128]); din("pvec", [2, 128, NV])
    din("cf", [128, NCF]); din("ropeP", [2, 96, T]); din("ropeS", [2, 96, TQ])
    dout("yT", [128, 8, T]); dout("ysT", [128, 8, NST])
    dout("latP", [2, 128, T]); dout("krP", [2, 32, T]); dout("poolP", [2, 128, 2, 15]); dout("convP", [2, 128, 2, 2])
    dout("latS", [2, 128, NST]); dout("krS", [2, 32, NST]); dout("poolS", [2, 128, 2, NS, 15])
    dout("convS", [2, 128, 2, NS, 2])
    cbytes = [D["cache0"].bitcast(U8), D["cache1"].bitcast(U8)]

    es = ExitStack()
    with es:
        k = KB(nc, es)
        xT = k.sb([128, 8, T], F32, "xT")
        xsT = k.sb([128, 8, NST], F32, "xsT")
        win = k.sb([128, 8, D_IN], BF16, "win")
        wout = k.sb([128, 8, 1024], BF16, "wout")
        wuq = k.sb([128, 2, 416], BF16, "wuq")
        wuk = k.sb([128, 256], BF16, "wuk")
        wuv = k.sb([128, 512], BF16, "wuv")
        wukT = k.sb([64, 4, 128], BF16, "wukT")
        pw = k.sb([128, 2, 128], BF16, "pw")
        pvec = k.sb([128, NV], F32, "pvec")
        cf = k.sb([128, NCF], F32, "cf")
        cfb = k.sb([128, 256], BF16, "cfb")
        siluT = k.sb([128, 8, NSEQ], F32, "siluT")
        modT = k.sb([128, 24, NSEQ], F32, "modT")
        amod = k.sb([128, 8, NSEQ], F32, "amod")
        epsT = k.sb([128, 1], F32, "eps")
        pts = k.sb([NS, NPG], I32, "pts")
        offs = k.sb([NS, NPG], I32, "offs")
        wa = [k.sb([128, 8, 128], F32, "wa0")]
        ropeP = k.sb([96, 2, NTB], F32, "ropeP")
        ropeS = k.sb([96, 2, TQ], F32, "ropeS")
        hT = k.sb([128, 8, NTB], BF16, "hT")
        mixed = k.sb([128, 8, NTB], BF16, "mixed")
        SW = max(NS * (16 + TQ), 16 + NTB)
        WN = max(NTB, NST)
        WQ = max(NTB, 2 * NST)

        def slab(name, dt=F32, w=WN):
            return k.sb([128, w], dt, name)

        sq = [slab("sq%d" % i) for i in range(2)]
        rstd = slab("rstd")
        tmpA = slab("tmpA")
        U = k.sb([128, 2, SW], F32, "U")
        S2 = k.sb([128, 2, SW], F32, "S2")
        S4 = k.sb([128, 2, SW], F32, "S4")
        S8 = slab("S8", w=SW)
        S16 = slab("S16", w=SW)
        szp = k.sb([128, 2, WN], BF16, "szp")
        dT = k.sb([128, 2, WN], BF16, "dT")
        cgs = k.sb([128, 2, WN], F32, "cgs")
        Vc = k.sb([128, 2, SW], F32, "Vc")
        bgs = k.sb([128, 2, WN], F32, "bgs")
        cacc = k.sb([128, 2, WN], F32, "cacc")
        szc = k.sb([128, 2, WN], BF16, "szc")
        cqs = k.sb([128, 2, WN], F32, "cqs")
        cqn = k.sb([128, 2, WN], BF16, "cqn")
        ckr = slab("ckr")
        ckvT = slab("ckvT")
        ckvb = slab("ckvb", BF16)
        qraw = slab("qraw", w=WQ)
        qsq = slab("qsq", w=WQ)
        qr_ = slab("qr_", w=WQ)
        qn = slab("qn", w=WQ)
        qt1 = slab("qt1", w=WQ)
        qrawL = [qraw, slab("qrawB", w=NTB)]
        qsqL = [qsq, slab("qsqB", w=NTB)]
        qrL = [qr_, slab("qrB", w=NTB)]
        qnL = [qn, slab("qnB", w=NTB)]
        qt1L = [qt1, slab("qt1B", w=NTB)]
        Qb = k.sb([96, 4, WN], BF16, "Qb")
        krT = slab("krT")
        szm = k.sb([128, 4, WN], BF16, "szm")
        PT = [slab("PT%d" % i, BF16, w=512) for i in range(2)]
        On = [slab("On%d" % i, w=128) for i in range(2)]
        rl = [slab("rl%d" % i, w=2) for i in range(2)]
        NSLOT = 4
        samp_specs = [("KsT", [96, 4, NST], BF16), ("qp", [64, 4, NST], BF16), ("qabs", [128, NS, 16], BF16),
                      ("qr4s", [128, NS, 16], F32), ("BD", [128, NS, 4, 16], BF16)]
        samp_specs += [("pg%d" % i, [128, GP, 160], F32) for i in range(NSLOT)]
        samp_specs += [("nat%d" % i, [128, GP, 130], BF16) for i in range(4)]
        samp_specs += [("latT%d" % i, [128, 512], BF16) for i in range(2)]
        samp_specs += [("krTb%d" % i, [128, 128], BF16) for i in range(2)]
        samp_specs += [("krp%d" % i, [128, 128], F32) for i in range(2)]
        samp_specs += [("sqb%d" % i, [128, 512], BF16) for i in range(4)]
        samp_specs += [("ssq%d" % i, [128, 16], F32) for i in range(2)]
        samp_specs += [("rinv%d" % i, [128, 16], F32) for i in range(2)]
        samp_specs += [("stmp%d" % i, [128, 64], F32) for i in range(2)]
        samp_specs += [("stmp2%d" % i, [128, 64], F32) for i in range(2)]
        samp_specs += [("pTt%d" % i, [128, 64], BF16) for i in range(3)]
        samp_specs += [("pn", [128, 16], F32), ("pnb0", [128, 16], BF16), ("pnb1", [128, 16], BF16),
                       ("natn0", [128, 130], BF16), ("natn1", [128, 130], BF16),
                       ("lo", [128, 128], F32), ("loT", [128, 16], BF16), ("rls", [128, 2], F32)]

        def nbytes(shape, dt):
            n = 1
            for d_ in shape[1:]:
                n *= d_
            return ((n * (4 if dt == F32 else 2) + 31) // 32) * 32
        samp_need = sum(nbytes(sh, dt) for _, sh, dt in samp_specs)
        prompt_need = 4 * T * 2 + (T // 128) * 4 * 130 * 2
        ABYTES = max(samp_need, prompt_need)
        arena = es.enter_context(nc.sbuf_tensor("sb_arena", [128, ABYTES // 2], BF16))

        def aview(off_b, shape, dt):
            n = 1
            for d_ in shape[1:]:
                n *= d_
            nb = n * (4 if dt == F32 else 2)
            ap = arena[0:shape[0], off_b // 2:(off_b + nb) // 2]
            if dt == F32:
                ap = ap.bitcast(F32)
            if len(shape) == 3:
                ap = ap.rearrange("p (a b) -> p a b", a=shape[1])
            elif len(shape) == 4:
                ap = ap.rearrange("p (a b c) -> p a b c", a=shape[1], b=shape[2])
            return TT(ap)
        KT = aview(0, [96, 4, T], BF16)
        VX = aview(4 * T * 2, [128, T // 128, 4, 130], BF16)
        SV = {}
        off_ = 0
        for nm, sh, dt in samp_specs:
            SV[nm] = aview(off_, sh, dt)
            off_ += nbytes(sh, dt)
        KsT, qp, qabs, qr4s, BD = SV["KsT"], SV["qp"], SV["qabs"], SV["qr4s"], SV["BD"]
        pg32 = [SV["pg%d" % i] for i in range(NSLOT)]
        pgsem = [k.newsem("pgsem%d" % i) for i in range(NSLOT)]
        nat = [SV["nat%d" % i] for i in range(4)]
        latT = [SV["latT%d" % i] for i in range(2)]
        krTb = [SV["krTb%d" % i] for i in range(2)]
        krp = [SV["krp%d" % i] for i in range(2)]
        sqb = [SV["sqb%d" % i] for i in range(4)]
        ssq = [SV["ssq%d" % i] for i in range(2)]
        rinv = [SV["rinv%d" % i] for i in range(2)]
        stmp = [SV["stmp%d" % i] for i in range(2)]
        stmp2 = [SV["stmp2%d" % i] for i in range(2)]
        pTt = [SV["pTt%d" % i] for i in range(3)]
        pn, lo, loT, rls = SV["pn"], SV["lo"], SV["loT"], SV["rls"]
        pnb = [SV["pnb0"], SV["pnb1"]]
        natn = [SV["natn0"], SV["natn1"]]
        PS = [k.ps("ps%d" % i) for i in range(8)]
        psn = [0]

        def nps():
            p = PS[psn[0] % 8]
            psn[0] += 1
            return p

        ident = cf[:, C_ID:C_ID + 128]
        ones = cf[:, C_ONES:C_ONES + 128]
        B96 = cf[0:96, C_B96:C_B96 + 96]
        ROT = cf[0:96, C_ROT:C_ROT + 96]
        maskb = cfb[:, 0:128]
        mask4 = cf[0:4, C_MASK4:C_MASK4 + 16]
        SELQ = cfb[0:96, 128:256]
        BDM = cf[:, C_BDM:C_BDM + 4]

        k.dma(cf[:], D["cf"][:, :], (), [cf])
        k.dma(xT[:], D["xT"][:, :, :], (), [xT])
        k.dma(xsT[:], D["xsT"][:, :, :], (), [xsT])
        k.dma(siluT[:], D["cT"][:, :, :], (), [siluT])
        k.dma(pts[:], D["pt"][:, :], (), [pts])
        k.dma(ropeS[:], D["ropeS"].rearrange("c p t -> p c t"), (), [ropeS])
        k.cp("dve", cfb[:, 0:128], cf[:, C_MASK:C_MASK + 128], [cf], [cfb])
        k.cp("dve", cfb[:, 128:256], cf[:, C_SELQ:C_SELQ + 128], [cf], [cfb])
        k.memset("dve", epsT[:], EPS, [epsT])
        k.ts("dve", offs[:], pts[:], float(PAGE_BYTES), ALU.mult, [pts], [offs])
        k.act(siluT[:], siluT[:], AF.Silu, [siluT], [siluT])
        k.memset("pool", wuq[:, :, 384:416], 0.0, [wuq])
        def load_weights(l):
            def cast_dma(dst, src, t):
                k.dma(dst, src, (), [t], eng="pool", max_dma_last_dim=4096)
            for kk in range(8):
                cast_dma(win[:, kk, 0:1232], D["win"][l, :, kk, 0:1232], win)
                cast_dma(win[:, kk, 1232:D_IN], D["win"][l, :, kk, 1232:D_IN], win)
            for kk in range(8):
                cast_dma(wout[:, kk, :], D["wout"][l, :, kk, :], wout)
            cast_dma(wuq[:, :, 0:384], D["wuq"][l], wuq)
            cast_dma(wuk[:], D["wuk"][l], wuk)
            cast_dma(wuv[:], D["wuv"][l], wuv)
            cast_dma(wukT[:], D["wukT"][l], wukT)
            cast_dma(pw[:], D["pw"][l], pw)
            k.dma(pvec[:], D["pvec"][l], (), [pvec])

        def adaln(l):
            pm = nps()
            for j in range(24):
                w_ = wa[0]
                k.dma(w_[:], D["wada"][l, j], (), [w_])
                for kk in range(8):
                    k.mm(pm[:, j * NSEQ:(j + 1) * NSEQ], w_[:, kk, :], siluT[:, kk, :], [w_, siluT], [pm],
                         start=(kk == 0), stop=(kk == 7), inc=(kk == 7))
            k.tt("dve", modT[:], pm[:, 0:24 * NSEQ].rearrange("p (j s) -> p j s", s=NSEQ),
                 bc(pvec[:, 8:32].unsqueeze(2), [128, 24, NSEQ]), ALU.add, [pm, pvec], [modT])
            k.ts("dve", amod[:], modT[:, 8:16, :], 1.0, ALU.add, [modT], [amod])
            k.tt("dve", amod[:], amod[:], bc(pvec[:, 0:8].unsqueeze(2), [128, 8, NSEQ]), ALU.mult, [amod, pvec], [amod])

        def rms_stats(srcs, scale, NT, rd):
            pst = nps()
            n = len(srcs)
            for i, (ap, t) in enumerate(srcs):
                s_ = sq[i % 2]
                k.act(s_[:, 0:NT], ap, AF.Square, [t], [s_])
                k.mm(pst[:, 0:NT], ones, s_[:, 0:NT], [cf, s_], [pst], start=(i == 0), stop=(i == n - 1))
            k.rsqrt(rd[:, 0:NT], pst[:, 0:NT], scale, epsT[:, 0:1], [pst, epsT], [rd])

        def norm_rope(src_ps, P_, W, gcol, cos_ap, sin_ap, out_ap, out_t, NTl, extra_reads=(), nh=1):
            def hv(ap):
                return ap if nh == 1 else ap.rearrange("p (h n) -> p h n", h=nh)
            k.cp("dve", qraw[0:P_, 0:W], src_ps[0:P_, 0:W], [src_ps], [qraw])
            k.act(qsq[0:P_, 0:W], src_ps[0:P_, 0:W], AF.Square, [src_ps], [qsq])
            p2 = nps()
            k.mm(p2[0:P_, 0:W], B96[0:P_, 0:P_], qsq[0:P_, 0:W], [cf, qsq], [p2])
            k.rsqrt(qr_[0:P_, 0:W], p2[0:P_, 0:W], 1.0, epsT[0:P_, 0:1], [p2, epsT], [qr_])
            k.stt("dve", qn[0:P_, 0:W], qraw[0:P_, 0:W], pvec[0:P_, gcol:gcol + 1], qr_[0:P_, 0:W], ALU.mult, ALU.mult,
                  [qraw, pvec, qr_], [qn])
            if cos_ap is None:
                k.cp("dve", out_ap, hv(qn[0:P_, 0:W]), [qn], [out_t])
                return
            p3 = nps()
            k.mm(p3[0:P_, 0:W], ROT[0:P_, 0:P_], qn[0:P_, 0:W], [cf, qn], [p3])
            k.tt("dve", hv(qt1[0:P_, 0:W]), hv(qn[0:P_, 0:W]), cos_ap, ALU.mult, [qn] + list(extra_reads), [qt1])
            k.tt("dve", hv(qn[0:P_, 0:W]), hv(p3[0:P_, 0:W]), sin_ap, ALU.mult, [p3] + list(extra_reads), [qn])
            k.tt("dve", out_ap, hv(qt1[0:P_, 0:W]), hv(qn[0:P_, 0:W]), ALU.add, [qt1, qn], [out_t])

        def norm_rope_pipe(calls, W, cos_ap, sin_ap, rt):
            n = len(calls)
            st = [dict() for _ in calls]

            def sA(c):
                st[c]["src"] = calls[c]["src"]()

            def sB(c):
                s_, P_, src = c % 2, calls[c]["P"], st[c]["src"]
                k.cp("dve", qrawL[s_][0:P_, 0:W], src[0:P_, 0:W], [src], [qrawL[s_]])
                k.act(qsqL[s_][0:P_, 0:W], src[0:P_, 0:W], AF.Square, [src], [qsqL[s_]])
                p2 = nps()
                st[c]["p2"] = p2
                k.mm(p2[0:P_, 0:W], B96[0:P_, 0:P_], qsqL[s_][0:P_, 0:W], [cf, qsqL[s_]], [p2])

            def sC(c):
                s_, P_, p2, g = c % 2, calls[c]["P"], st[c]["p2"], calls[c]["g"]
                k.rsqrt(qrL[s_][0:P_, 0:W], p2[0:P_, 0:W], 1.0, epsT[0:P_, 0:1], [p2, epsT], [qrL[s_]])
                k.stt("dve", qnL[s_][0:P_, 0:W], qrawL[s_][0:P_, 0:W], pvec[0:P_, g:g + 1], qrL[s_][0:P_, 0:W], ALU.mult, ALU.mult,
                      [qrawL[s_], pvec, qrL[s_]], [qnL[s_]])
                if calls[c]["rope"]:
                    p3 = nps()
                    st[c]["p3"] = p3
                    k.mm(p3[0:P_, 0:W], ROT[0:P_, 0:P_], qnL[s_][0:P_, 0:W], [cf, qnL[s_]], [p3])
                else:
                    k.cp("dve", calls[c]["out"], qnL[s_][0:P_, 0:W], [qnL[s_]], [calls[c]["ot"]])

            def sD(c):
                if not calls[c]["rope"]:
                    return
                s_, P_, p3 = c % 2, calls[c]["P"], st[c]["p3"]
                k.tt("dve", qt1L[s_][0:P_, 0:W], qnL[s_][0:P_, 0:W], cos_ap, ALU.mult, [qnL[s_], rt], [qt1L[s_]])
                k.tt("dve", qnL[s_][0:P_, 0:W], p3[0:P_, 0:W], sin_ap, ALU.mult, [p3, rt], [qnL[s_]])
                k.tt("dve", calls[c]["out"], qt1L[s_][0:P_, 0:W], qnL[s_][0:P_, 0:W], ALU.add, [qt1L[s_], qnL[s_]], [calls[c]["ot"]])

            for t in range(n + 3):
                if t < n:
                    sA(t)
                if 0 <= t - 1 < n:
                    sB(t - 1)
                if 0 <= t - 2 < n:
                    sC(t - 2)
                if 0 <= t - 3 < n:
                    sD(t - 3)

        def block_front(l, grp, bi):
            if grp == "P":
                nseq, Tq, NT = 1, NTB, NTB
                xv = xT[:, :, bi * NTB:(bi + 1) * NTB]
                xt = xT
                c0 = bi * NTB
            else:
                nseq, Tq, NT = NS, TQ, NST
                xv = xsT[:, :, :]
                xt = xsT
                c0 = 0
            L = 16 + Tq

            def v3(ap):
                return ap.rearrange("p (s t) -> p s t", t=Tq)

            def ext(tile_ap):
                return tile_ap[:, 0:nseq * L].rearrange("p (s l) -> p s l", l=L)

            for tl in (U, Vc):
                for c in range(2):
                    e_ = ext(tl[:, c, :])
                    if grp == "P":
                        if bi == 0:
                            k.memset("pool", e_[:, :, 0:16], 0.0, [tl])
                        else:
                            k.cp("pool", e_[:, :, 0:16], e_[:, :, Tq:Tq + 16], [tl], [tl])
            if grp == "S":
                for c in range(2):
                    k.memset("pool", ext(U[:, c, :])[:, :, 0:1], 0.0, [U])
                    k.dma(ext(U[:, c, :])[:, :, 1:16], D["spT"][l, :, c, :, :], (), [U])
                    k.dma(ext(Vc[:, c, :])[:, :, 14:16], D["scT"][l, :, c, :, :], (), [Vc])
            if grp == "P":
                k.dma(ropeP[:], D["ropeP"][:, :, c0:c0 + NTB].rearrange("c p t -> p c t"), (), [ropeP])
            rms_stats([(xv[:, kk, :], xt) for kk in range(8)], 1.0 / D_MODEL, NT, rstd)
            for kk in range(8):
                if grp == "P":
                    k.stt("dve", tmpA[:, 0:NT], xv[:, kk, :], amod[:, kk, 0:1], rstd[:, 0:NT], ALU.mult, ALU.mult,
                          [xt, amod, rstd], [tmpA])
                    k.act(hT[:, kk, 0:NT], tmpA[:, 0:NT], AF.Identity, [tmpA, modT], [hT], bias=modT[:, kk, 0:1], scale=1.0)
                else:
                    k.tt("dve", tmpA[:, 0:NT], xv[:, kk, :], rstd[:, 0:NT], ALU.mult, [xt, rstd], [tmpA])
                    k.tt("dve", v3(tmpA[:, 0:NT]), v3(tmpA[:, 0:NT]), bc(amod[:, kk, 1:NSEQ].unsqueeze(2), [128, NS, TQ]),
                         ALU.mult, [tmpA, amod], [tmpA])
                    k.tt("dve", v3(hT[:, kk, 0:NT]), v3(tmpA[:, 0:NT]), bc(modT[:, kk, 1:NSEQ].unsqueeze(2), [128, NS, TQ]),
                         ALU.add, [tmpA, modT], [hT])

            def proj(col, M):
                p = nps()
                for kk in range(8):
                    k.mm(p[0:M, 0:NT], win[:, kk, col:col + M], hT[:, kk, 0:NT], [win, hT], [p], start=(kk == 0), stop=(kk == 7),
                         inc=(kk == 7))
                return p

            for c in range(2):
                p = proj(256 + c * 128, 128)
                k.act(szp[:, c, 0:NT], p[:, 0:NT], AF.Silu, [p], [szp])
            for c in range(2):
                p = proj(1280 + c * 128, 128)
                k.act(szc[:, c, 0:NT], p[:, 0:NT], AF.Silu, [p], [szc])
            for c in range(4):
                p = proj(1952 + c * 128, 128)
                k.act(szm[:, c, 0:NT], p[:, 0:NT], AF.Silu, [p], [szm])
            for c in range(2):
                p = proj(c * 128, 128)
                k.cp("act", ext(U[:, c, :])[:, :, 16:L], v3(p[:, 0:NT]), [p], [U])
            for c in range(2):
                p = proj(1024 + c * 128, 128)
                k.cp("act", cgs[:, c, 0:NT], p[:, 0:NT], [p], [cgs])
            for c in range(2):
                p = proj(512 + c * 128, 128)
                k.tt("dve", ext(Vc[:, c, :])[:, :, 16:L], v3(p[:, 0:NT]), v3(cgs[:, c, 0:NT]), ALU.mult, [p, cgs], [Vc])
            for c in range(2):
                p = proj(768 + c * 128, 128)
                k.cp("act", bgs[:, c, 0:NT], p[:, 0:NT], [p], [bgs])
            for c in range(2):
                p = proj(1536 + c * 128, 128)
                k.cp("act", cqs[:, c, 0:NT], p[:, 0:NT], [p], [cqs])
            p = proj(1792, 128)
            k.cp("act", ckr[:, 0:NT], p[:, 0:NT], [p], [ckr])
            pkr = proj(1856, 128)
            if grp == "P":
                cos1, sin1 = ropeP[:, 0, 0:NT], ropeP[:, 1, 0:NT]
                cos2 = bc(ropeP[:, 0:1, 0:NT], [96, 2, NT])
                sin2 = bc(ropeP[:, 1:2, 0:NT], [96, 2, NT])
                rt = ropeP
            else:
                cos1 = bc(ropeS[:, 0:1, :], [96, NS, TQ])
                sin1 = bc(ropeS[:, 1:2, :], [96, NS, TQ])
                cos2 = bc(ropeS[:, 0:1, :], [96, 2 * NS, TQ])
                sin2 = bc(ropeS[:, 1:2, :], [96, 2 * NS, TQ])
                rt = ropeS
            Kt = KT if grp == "P" else KsT
            rms_stats([(ckr[:, 0:NT], ckr)], 1.0 / 128, NT, rstd)
            k.stt("dve", ckvT[:, 0:NT], ckr[:, 0:NT], pvec[:, 42:43], rstd[:, 0:NT], ALU.mult, ALU.mult, [ckr, pvec, rstd], [ckvT])
            k.cp("dve", ckvb[:, 0:NT], ckvT[:, 0:NT], [ckvT], [ckvb])
            if grp == "P":
                k.dma(D["latP"][l, :, c0:c0 + NT], ckvT[:, 0:NT], [ckvT], ())
            else:
                k.dma(D["latS"][l, :, :], ckvT[:, 0:NT], [ckvT], ())
            rms_stats([(cqs[:, c, 0:NT], cqs) for c in range(2)], 1.0 / 256, NT, rstd)
            for c in range(2):
                k.stt("dve", cqn[:, c, 0:NT], cqs[:, c, 0:NT], pvec[:, 40 + c:41 + c], rstd[:, 0:NT], ALU.mult, ALU.mult,
                      [cqs, pvec, rstd], [cqn])
            if grp == "P":
                calls = [dict(src=(lambda: pkr), P=96, g=44, rope=True, out=krT[0:96, 0:NT], ot=krT)]
                for h in range(4):
                    def srck(h=h):
                        p = nps()
                        k.mm(p[0:64, 0:NT], wuk[:, h * 64:(h + 1) * 64], ckvb[:, 0:NT], [wuk, ckvb], [p])
                        return p
                    calls.append(dict(src=srck, P=64, g=44, rope=False, out=Kt[0:64, h, c0:c0 + NT], ot=Kt))
                for h in range(4):
                    def srcq(h=h):
                        p = nps()
                        for kk in range(2):
                            k.mm(p[0:128, 0:NT], wuq[:, kk, h * 96:h * 96 + 128], cqn[:, kk, 0:NT], [wuq, cqn], [p],
                                 start=(kk == 0), stop=(kk == 1), inc=(kk == 1))
                        return p
                    calls.append(dict(src=srcq, P=96, g=43, rope=True, out=Qb[0:96, h, 0:NT], ot=Qb))
                norm_rope_pipe(calls, NT, cos1, sin1, rt)
            else:
                norm_rope_s(pkr, 1, 44, cos1, sin1, krT[0:96, 0:NT], krT, rt)
                for hp in range(2):
                    p = nps()
                    for hh in range(2):
                        h = 2 * hp + hh
                        for kk in range(2):
                            k.mm(p[0:128, hh * NT:(hh + 1) * NT], wuq[:, kk, h * 96:h * 96 + 128], cqn[:, kk, 0:NT], [wuq, cqn], [p],
                                 start=(kk == 0), stop=(kk == 1))
                    norm_rope_s(p, 2, 43, cos2, sin2, Qb[0:96, 2 * hp:2 * hp + 2, 0:NT], Qb, rt)
                for hp in range(2):
                    p = nps()
                    for hh in range(2):
                        h = 2 * hp + hh
                        k.mm(p[0:64, hh * NT:(hh + 1) * NT], wuk[:, h * 64:(h + 1) * 64], ckvb[:, 0:NT], [wuk, ckvb], [p])
                    oap = Kt[0:64, 2 * hp:2 * hp + 2, c0:c0 + NT]
                    norm_rope(p, 64, 2 * NT, 44, None, None, oap, Kt, NT, nh=2)
            for h in range(4):
                k.cp("dve", Kt[64:96, h, c0:c0 + NT], krT[64:96, 0:NT], [krT], [Kt])
            if grp == "P":
                k.dma(D["krP"][l, :, c0:c0 + NT], krT[64:96, 0:NT], [krT], ())
            else:
                k.dma(D["krS"][l, :, :], krT[64:96, 0:NT], [krT], ())
            if grp == "P":
                for j in range(NT // 128):
                    p = nps()
                    k.mm(p[:, 0:512], ckvb[:, j * 128:(j + 1) * 128], wuv[:], [ckvb, wuv], [p])
                    tj = (c0 // 128) + j
                    k.cp("act", VX[:, tj, :, 0:128], p[:, 0:512].rearrange("p (h v) -> p h v", h=4), [p], [VX])

            for c in range(2):
                u_, s2_, s4_ = ext(U[:, c, :]), ext(S2[:, c, :]), ext(S4[:, c, :])
                k.tt("pool", s2_[:, :, 1:L], u_[:, :, 1:L], u_[:, :, 0:L - 1], ALU.add, [U], [S2])
                k.tt("pool", s4_[:, :, 3:L], s2_[:, :, 3:L], s2_[:, :, 1:L - 2], ALU.add, [S2], [S4])
            s4_, s8_, s16_ = ext(S4[:, 1, :]), ext(S8[:]), ext(S16[:])
            k.tt("pool", s8_[:, :, 7:L], s4_[:, :, 7:L], s4_[:, :, 3:L - 4], ALU.add, [S4], [S8])
            k.tt("pool", s16_[:, :, 15:L], s8_[:, :, 15:L], s8_[:, :, 7:L - 8], ALU.add, [S8], [S16])
            srcs = {(0, 0): (S2[:, 0, :], S2), (1, 0): (S4[:, 0, :], S4), (0, 1): (S8[:], S8), (1, 1): (S16[:], S16)}
            for c in range(2):
                for hf in range(2):
                    pr = slice(hf * 64, (hf + 1) * 64)
                    sap, st = srcs[(hf, c)]
                    s_ = ext(sap)[pr, :, 16:L]
                    k.stt("dve", v3(dT[pr, c, 0:NT]), s_, cf[pr, C_INVW + c:C_INVW + c + 1], ext(U[:, c, :])[pr, :, 16:L],
                          ALU.mult, ALU.subtract, [st, cf, U], [dT])
                    if grp == "P" and bi == 0:
                        k.tt("dve", tmpA[pr, 0:16], sap[pr, 16:32], cf[pr, C_INVC + c * 16:C_INVC + c * 16 + 16], ALU.mult,
                             [st, cf], [tmpA])
                        k.tt("dve", dT[pr, c, 0:16], tmpA[pr, 0:16], U[pr, c, 16:32], ALU.subtract, [tmpA, U], [dT])
            for c in range(2):
                p = nps()
                k.mm(p[:, 0:NT], pw[:, c, :], dT[:, c, 0:NT], [pw, dT], [p])
                k.stt("dve", mixed[:, c, 0:NT], p[:, 0:NT], pvec[:, 32 + c:33 + c], szp[:, c, 0:NT], ALU.mult, ALU.mult,
                      [p, pvec, szp], [mixed])
            if grp == "P":
                if bi == NBLK - 1:
                    for c in range(2):
                        k.dma(D["poolP"][l, :, c, :], U[:, c, 16 + Tq - 15:16 + Tq], [U], ())
            else:
                for c in range(2):
                    k.dma(D["poolS"][l, :, c, :, :], ext(U[:, c, :])[:, :, L - 15:L], [U], ())

            for c in range(2):
                v_ = ext(Vc[:, c, :])
                a_ = v3(cacc[:, c, 0:NT])
                k.ts("pool", a_, v_[:, :, 16:L], pvec[:, 34 + 4 + c:35 + 4 + c], ALU.mult, [Vc, pvec], [cacc])
                k.stt("dve", a_, v_[:, :, 15:L - 1], pvec[:, 34 + 2 + c:35 + 2 + c], a_, ALU.mult, ALU.add, [Vc, pvec, cacc], [cacc])
                k.stt("dve", a_, v_[:, :, 14:L - 2], pvec[:, 34 + c:35 + c], a_, ALU.mult, ALU.add, [Vc, pvec, cacc], [cacc])
                k.tt("pool", cacc[:, c, 0:NT], cacc[:, c, 0:NT], bgs[:, c, 0:NT], ALU.mult, [cacc, bgs], [cacc])
                k.tt("pool", mixed[:, 2 + c, 0:NT], cacc[:, c, 0:NT], szc[:, c, 0:NT], ALU.mult, [cacc, szc], [mixed])
            if grp == "P":
                if bi == NBLK - 1:
                    for c in range(2):
                        k.dma(D["convP"][l, :, c, :], Vc[:, c, 16 + Tq - 2:16 + Tq], [Vc], ())
            else:
                for c in range(2):
                    k.dma(D["convS"][l, :, c, :, :], ext(Vc[:, c, :])[:, :, L - 2:L], [Vc], ())


        def norm_rope_s(src_ps, nh, gcol, cos_ap, sin_ap, out_ap, out_t, rt):
            W = nh * NST
            P_ = 96
            k.cp("dve", qraw[0:P_, 0:W], src_ps[0:P_, 0:W], [src_ps], [qraw])
            k.act(qsq[0:P_, 0:W], src_ps[0:P_, 0:W], AF.Square, [src_ps], [qsq])
            p2 = nps()
            k.mm(p2[0:P_, 0:W], B96, qsq[0:P_, 0:W], [cf, qsq], [p2])
            k.rsqrt(qr_[0:P_, 0:W], p2[0:P_, 0:W], 1.0, epsT[0:P_, 0:1], [p2, epsT], [qr_])
            k.stt("dve", qn[0:P_, 0:W], qraw[0:P_, 0:W], pvec[0:P_, gcol:gcol + 1], qr_[0:P_, 0:W], ALU.mult, ALU.mult,
                  [qraw, pvec, qr_], [qn])
            p3 = nps()
            k.mm(p3[0:P_, 0:W], ROT, qn[0:P_, 0:W], [cf, qn], [p3])

            def v3(ap):
                return ap.rearrange("p (s t) -> p s t", t=TQ)
            k.tt("dve", v3(qt1[0:P_, 0:W]), v3(qn[0:P_, 0:W]), cos_ap, ALU.mult, [qn, rt], [qt1])
            k.tt("dve", v3(qn[0:P_, 0:W]), v3(p3[0:P_, 0:W]), sin_ap, ALU.mult, [p3, rt], [qn])
            if nh == 1:
                k.tt("dve", out_ap, qt1[0:P_, 0:W], qn[0:P_, 0:W], ALU.add, [qt1, qn], [out_t])
            else:
                k.tt("dve", out_ap, qt1[0:P_, 0:W].rearrange("p (h n) -> p h n", h=nh),
                     qn[0:P_, 0:W].rearrange("p (h n) -> p h n", h=nh), ALU.add, [qt1, qn], [out_t])

        def block_back(l, grp, bi):
            if grp == "P":
                NT = NTB
                xv = xT[:, :, bi * NTB:(bi + 1) * NTB]
                xt = xT
            else:
                NT = NST
                xv = xsT[:, :, :]
                xt = xsT
            for j in range(8):
                p = nps()
                for kk in range(8):
                    k.mm(p[:, 0:NT], wout[:, kk, j * 128:(j + 1) * 128], mixed[:, kk, 0:NT], [wout, mixed], [p],
                         start=(kk == 0), stop=(kk == 7), inc=(kk == 7))
                if grp == "P":
                    k.stt("dve", xv[:, j, :], p[:, 0:NT], modT[:, 16 + j, 0:1], xv[:, j, :], ALU.mult, ALU.add,
                          [p, modT, xt], [xt])
                else:
                    k.tt("dve", tmpA[:, 0:NT].rearrange("p (s t) -> p s t", t=TQ), p[:, 0:NT].rearrange("p (s t) -> p s t", t=TQ),
                         bc(modT[:, 16 + j, 1:NSEQ].unsqueeze(2), [128, NS, TQ]), ALU.mult, [p, modT], [tmpA])
                    k.tt("dve", xv[:, j, :], xv[:, j, :], tmpA[:, 0:NT], ALU.add, [xt, tmpA], [xt])

        def prompt_attn(l, bi):
            NT = NTB
            assert NT == 128
            qt = bi
            nkt = qt + 1
            batches = [list(range(s0, min(s0 + 4, nkt))) for s0 in range(0, nkt, 4)]
            nb = len(batches)

            def S_stage(h, bidx):
                kts = batches[bidx]
                pss = PS[4 + bidx % 2]
                pt_ = PT[bidx % 2]
                for i, kt in enumerate(kts):
                    k.mm(pss[:, i * 128:(i + 1) * 128], KT[0:96, h, kt * 128:(kt + 1) * 128], Qb[0:96, h, 0:NT], [KT, Qb], [pss],
                         inc=(i == len(kts) - 1))
                w = len(kts) * 128
                k.act(pt_[:, 0:w], pss[:, 0:w], AF.Exp, [pss], [pt_], scale=SM_SCALE)
                if kts[-1] == qt:
                    i = len(kts) - 1
                    k.tt("pool", pt_[:, i * 128:(i + 1) * 128], pt_[:, i * 128:(i + 1) * 128], maskb, ALU.mult, [pt_, cfb], [pt_])

            def PV_stage(h, bidx):
                kts = batches[bidx]
                pt_ = PT[bidx % 2]
                pso = PS[h % 2]
                for i, kt in enumerate(kts):
                    k.mm(pso[:, 0:129], pt_[:, i * 128:(i + 1) * 128], VX[:, kt, h, 0:129], [pt_, VX], [pso],
                         start=(kt == 0), stop=(kt == qt), inc=(i == len(kts) - 1))

            def epilogue(h):
                pso = PS[h % 2]
                r_ = rl[h % 2]
                o_ = On[h % 2]
                k.recip(r_[:, 0:1], pso[:, 128:129], [pso], [r_])
                k.ts("dve", o_[:, 0:128], pso[:, 0:128], r_[:, 0:1], ALU.mult, [pso, r_], [o_])
                ptr = PS[6 + h % 2]
                k.tr(ptr[:, 0:128], o_[:, 0:128], ident, [o_, cf], [ptr])
                k.tt("dve", mixed[:, 4 + h, 0:128], ptr[:, 0:128], szm[:, h, 0:128], ALU.mult, [ptr, szm], [mixed])

            for h in range(4):
                S_stage(h, 0)
                if h > 0:
                    epilogue(h - 1)
                for b_ in range(nb):
                    if b_ + 1 < nb:
                        S_stage(h, b_ + 1)
                    PV_stage(h, b_)
            epilogue(3)

        def sample_attn(l):
            k.ts("dve", qp[:, :, :], Qb[0:64, :, 0:NST], pvec[0:64, 44:45], ALU.mult, [Qb, pvec], [qp])
            p = nps()
            for h in range(4):
                k.mm(p[:, h * NST:(h + 1) * NST], wukT[:, h, :], qp[:, h, :], [wukT, qp], [p])
            k.cp("dve", qabs[:].rearrange("p b (h q) -> p h b q", h=4),
                 p[:, 0:4 * NST].rearrange("p (h b q) -> p h b q", h=4, b=NS), [p], [qabs])
            p = nps()
            for h in range(4):
                k.mm(p[:, h * NST:(h + 1) * NST], SELQ, Qb[0:96, h, 0:NST], [cfb, Qb], [p])
            k.cp("dve", qr4s[:].rearrange("p b (h q) -> p h b q", h=4),
                 p[:, 0:4 * NST].rearrange("p (h b q) -> p h b q", h=4, b=NS), [p], [qr4s])
            k.tt("dve", BD[:], bc(qr4s[:].unsqueeze(2), [128, NS, 4, 16]),
                 bc(BDM.unsqueeze(1).unsqueeze(3), [128, NS, 4, 16]), ALU.mult, [qr4s, cf], [BD])

            psT = [PS[0], PS[1]]
            psKR = [PS[2], PS[3]]
            psA = [PS[4], PS[5]]
            psK = PS[6]
            psAcc = PS[7]
            groups = [(b, g) for b in range(NS) for g in range(NGRP)]
            NG = len(groups)
            NNAT = len(nat)

            def load(i):
                b, g = groups[i]
                s_ = i % NSLOT
                pg = pg32[s_]
                dst = pg[:].bitcast(U8)

                def fn(e):
                    regs = [e.alloc_register("pg%d_%d_%d" % (l, i, j)) for j in range(GP)]
                    e.reg_load(regs, offs[b:b + 1, g * GP:(g + 1) * GP])
                    ins = []
                    for j in range(GP):
                        v = e.snap(regs[j], donate=True, min_val=0, max_val=(NPOOL - 1) * PAGE_BYTES)
                        ins.append(e.dma_start(out=dst[:, j, :],
                                               in_=cbytes[l][bass.ds(v, PAGE_BYTES)].rearrange("(p f) -> p f", p=128)))
                    for r in regs:
                        e.free_register(r)
                    return ins
                k.emit("sp", fn, [offs], [pg], dsem=pgsem[s_], ninc=GP)

            def stageT(i):
                pg = pg32[i % NSLOT]
                n_ = nat[i % NNAT]
                kp_ = krp[i % 2]
                k.cp("pool", kp_[:, 0:128].rearrange("p (g r) -> p g r", g=GP), pg[:, :, 128:160], [pg], [kp_])
                k.cp("pool", n_[:, :, 0:128], pg[:, :, 0:128], [pg], [n_])
                pt_ = psT[i % 2]
                for j in range(GP):
                    k.tr(pt_[:, j * 128:(j + 1) * 128], pg[:, j, 0:128], ident, [pg, cf], [pt_], inc=(j == GP - 1))
                k.tr(psK[:, 0:128], kp_[:, 0:128], ident, [kp_, cf], [psK])
                lt = latT[i % 2]
                k.cp("act", lt[:, 0:512], pt_[:, 0:512], [pt_], [lt])
                kb_ = krTb[i % 2]
                k.cp("dve", kb_[:, 0:128], psK[:, 0:128], [psK], [kb_])

            def stageK1(i):
                b, g = groups[i]
                lt = latT[i % 2]
                kb_ = krTb[i % 2]
                pa = psA[i % 2]
                for j in range(GP):
                    pk = psKR[j // 2]
                    k.mm(pk[:, (j % 2) * 256:(j % 2) * 256 + 256], lt[:, j * 128:(j + 1) * 128], wuk[:], [lt, wuk], [pk],
                         inc=(j % 2 == 1))
                    k.mm(pa[:, j * 16:(j + 1) * 16], lt[:, j * 128:(j + 1) * 128], qabs[:, b, :], [lt, qabs], [pa], inc=False)
                k.mm(pa[:, 64:128], kb_[:, 0:128], BD[:, b, :, :].rearrange("p g n -> p (g n)"), [kb_, BD], [pa])
                sq_ = ssq[i % 2]
                for hb in range(2):
                    sb_ = sqb[(2 * i + hb) % 4]
                    k.act(sb_[:, 0:512], psKR[hb][:, 0:512], AF.Square, [psKR[hb]], [sb_])
                    k.emit("dve", lambda e, sb_=sb_, sq_=sq_, hb=hb: e.tensor_reduce(
                        out=sq_[:, hb * 8:(hb + 1) * 8], in_=sb_[:, 0:512].rearrange("p (a d) -> p a d", d=64),
                        axis=AX.X, op=ALU.add), [sb_], [sq_])

            def stageK2a(i):
                sq_ = ssq[i % 2]
                pa = psA[i % 2]
                ri = rinv[i % 2]
                k.rsqrt(ri[:, 0:16], sq_[:, 0:16], 1.0 / 64, epsT[:, 0:1], [sq_, epsT], [ri])
                t1, t2 = stmp[i % 2], stmp2[i % 2]
                k.tt("dve", t1[:, 0:64].rearrange("p (a q) -> p a q", q=4), pa[:, 0:64].rearrange("p (a q) -> p a q", q=4),
                     bc(ri[:, 0:16].unsqueeze(2), [128, 16, 4]), ALU.mult, [pa, ri], [t1])
                k.tt("dve", t2[:, 0:64], t1[:, 0:64], pa[:, 64:128], ALU.add, [t1, pa], [t2])

            def stageK2b(i):
                t2 = stmp2[i % 2]
                pp = pTt[i % 3]
                k.act(pp[:, 0:64], t2[:, 0:64], AF.Exp, [t2], [pp], scale=SM_SCALE)

            def stageAcc(i):
                b, g = groups[i]
                n_ = nat[i % NNAT]
                pp = pTt[i % 3]
                if g == 0:
                    k.mm(psAcc[0:16, 0:129], pnb[b % 2][0:4, 0:16], natn[b % 2][0:4, 0:129], [pnb[b % 2], natn[b % 2]], [psAcc],
                         start=True, stop=False, skip=True, inc=False)
                for j in range(GP):
                    last = (g == NGRP - 1 and j == GP - 1)
                    k.mm(psAcc[0:16, 0:129], pp[:, j * 16:(j + 1) * 16], n_[:, j, 0:129], [pp, n_], [psAcc],
                         start=False, stop=last, skip=True, inc=(j == GP - 1))
                if g == NGRP - 1:
                    finish(b)

            def start_sample(b):
                cs = slice(b * TQ, (b + 1) * TQ)
                pn_, pnb_, natn_ = pn, pnb[b % 2], natn[b % 2]
                for h in range(4):
                    k.mm(psK[0:4, 128 + h * 4:128 + (h + 1) * 4], KsT[0:96, h, cs], Qb[0:96, h, cs], [KsT, Qb], [psK],
                         inc=(h == 3))
                k.act(pn_[0:4, 0:16], psK[0:4, 128:144], AF.Exp, [psK], [pn_], scale=SM_SCALE)
                k.tt("dve", pnb_[0:4, 0:16], pn_[0:4, 0:16], mask4, ALU.mult, [pn_, cf], [pnb_])
                k.tr(psK[0:4, 256:384], ckvT[:, cs], ident, [ckvT, cf], [psK])
                k.cp("dve", natn_[0:4, 0:128], psK[0:4, 256:384], [psK], [natn_])

            def finish(b):
                cs = slice(b * TQ, (b + 1) * TQ)
                k.recip(rls[0:16, 0:1], psAcc[0:16, 128:129], [psAcc], [rls])
                k.ts("dve", lo[0:16, 0:128], psAcc[0:16, 0:128], rls[0:16, 0:1], ALU.mult, [psAcc, rls], [lo])
                k.tr(psK[:, 384:400], lo[0:16, 0:128], cf[0:16, C_ID:C_ID + 16], [lo, cf], [psK])
                k.cp("dve", loT[:, 0:16], psK[:, 384:400], [psK], [loT])
                for h in range(4):
                    k.mm(psK[:, 400 + h * 4:400 + (h + 1) * 4], wuv[:, h * 128:(h + 1) * 128], loT[:, h * 4:(h + 1) * 4],
                         [wuv, loT], [psK], inc=(h == 3))
                k.tt("dve", mixed[:, 4:8, cs], psK[:, 400:416].rearrange("p (h q) -> p h q", h=4), szm[:, :, cs], ALU.mult,
                     [psK, szm], [mixed])

            PF = 3
            for i in range(min(PF, NG)):
                load(i)
            for i in range(-1, NG + 2):
                if PF <= i + PF < NG:
                    load(i + PF)
                if 0 <= i - 1 < NG:
                    stageK2a(i - 1)
                if 0 <= i + 1 < NG:
                    stageT(i + 1)
                if 0 <= i < NG:
                    if groups[i][1] == 0:
                        start_sample(groups[i][0])
                    stageK1(i)
                if 0 <= i - 1 < NG:
                    stageK2b(i - 1)
                if 0 <= i - 2 < NG:
                    stageAcc(i - 2)

        for l in range(DEPTH):
            load_weights(l)
            adaln(l)
            k.barrier()
            k.memset("pool", VX[:, :, :, 128:130], 1.0, [VX])
            for bi in range(NBLK):
                block_front(l, "P", bi)
                prompt_attn(l, bi)
                block_back(l, "P", bi)
            k.barrier()
            for n_ in nat:
                k.memset("pool", n_[:, :, 128:130], 1.0, [n_])
            for n_ in natn:
                k.memset("pool", n_[:, 128:130], 1.0, [n_])
            block_front(l, "S", 0)
            sample_attn(l)
            block_back(l, "S", 0)
        k.dma(D["yT"][:, :, :], xT[:], [xT], ())
        k.dma(D["ysT"][:, :, :], xsT[:], [xsT], ())
        for s in list(k.dsems) + list(pgsem) + list(k.swsems):
            if k.cnt[s] > 0:
                k.wait_tok("sp", (s, k.cnt[s]))

        with nc.Block() as block:
            @block.sync
            def _(e):
                k.replay("sp", e)

            @block.scalar
            def _(e):
                k.replay("act", e)

            @block.tensor
            def _(e):
                k.replay("pe", e)

            @block.vector
            def _(e):
                k.replay("dve", e)

            @block.gpsimd
            def _(e):
                k.replay("pool", e)
    return nc


def _fm(a):
    r, f = a.shape
    return np.ascontiguousarray(a.reshape(r, f // 128, 128).transpose(2, 1, 0))


def _consts(T, TQ, past_len):
    cf = np.zeros((128, NCF), np.float32)
    cf[:, C_ID:C_ID + 128] = np.eye(128, dtype=np.float32)
    cf[:, C_ONES:C_ONES + 128] = 1.0
    cf[0:64, C_B96:C_B96 + 64] = 1.0 / 64
    cf[64:96, C_B96 + 64:C_B96 + 96] = 1.0 / 32
    for i in range(16):
        cf[64 + 16 + i, C_ROT + 64 + i] = -1.0
        cf[64 + i, C_ROT + 64 + 16 + i] = 1.0
    wins = (2, 4, 8, 16)
    for c in range(2):
        for hf in range(2):
            w = wins[2 * c + hf]
            cf[hf * 64:(hf + 1) * 64, C_INVW + c] = 1.0 / w
            for t in range(16):
                cf[hf * 64:(hf + 1) * 64, C_INVC + c * 16 + t] = 1.0 / min(t + 1, w)
    kk = np.arange(128)[:, None]
    qq = np.arange(128)[None, :]
    cf[:, C_MASK:C_MASK + 128] = (kk <= qq).astype(np.float32)
    for kq in range(4):
        for h in range(4):
            for q in range(4):
                cf[kq, C_MASK4 + h * 4 + q] = 1.0 if kq <= q else 0.0
    for g in range(4):
        for r in range(32):
            cf[64 + r, C_SELQ + g * 32 + r] = 1.0
        cf[g * 32:(g + 1) * 32, C_BDM + g] = 1.0
    inv_freq = (1.0 / (ROPE_THETA ** (np.arange(0, ROPE, 2, dtype=np.float32) / np.float32(ROPE)))).astype(np.float32)

    def tab(pos):
        ang = pos.astype(np.float32)[:, None] * inv_freq[None, :]
        cos = np.concatenate([np.cos(ang), np.cos(ang)], -1).astype(np.float32)
        sin = np.concatenate([np.sin(ang), np.sin(ang)], -1).astype(np.float32)
        o = np.zeros((2, 96, len(pos)), np.float32)
        o[0, 0:64] = 1.0
        o[0, 64:96] = cos.T
        o[1, 64:96] = sin.T
        return o
    return cf, tab(np.arange(T)), tab(past_len + np.arange(TQ))


_NC_CACHE = {}


def kernel(x_prompt, x_sample, cache_latent, cache_krope, state_pool, state_conv, page_table,
           c_prompt, c_sample, norm_g, w_ada, b_ada, w_in, pool_w, pool_scale, conv_w,
           q_norm_g, w_uq, qn_g, qr_g, kv_norm_g, kr_g, w_uk, kn_g, w_uv, w_out, _ncores=None):
    f = lambda a: np.asarray(a, dtype=np.float32)
    x_prompt, x_sample, cache_latent, cache_krope = f(x_prompt), f(x_sample), f(cache_latent), f(cache_krope)
    B, T, _ = x_prompt.shape
    DB, TQ, _ = x_sample.shape
    ncores = _ncores or B
    NS = DB // ncores
    NPOOL = cache_latent.shape[1]
    NPG = page_table.shape[1]
    past_len = NPG * 128
    cfg = dict(T=T, NPG=NPG, NPOOL=NPOOL, NS=NS, TQ=TQ)
    key = tuple(sorted(cfg.items()))
    if key not in _NC_CACHE:
        _NC_CACHE[key] = build(cfg)
    nc = _NC_CACHE[key]

    cf, ropeP, ropeS = _consts(T, TQ, past_len)
    caches = [np.ascontiguousarray(np.concatenate([cache_latent[l], cache_krope[l]], axis=-1)).reshape(-1) for l in range(2)]
    wada = np.ascontiguousarray(f(w_ada).reshape(2, 8, 128, 24, 128).transpose(0, 3, 2, 1, 4))
    win = np.ascontiguousarray(f(w_in).reshape(2, 8, 128, D_IN).transpose(0, 2, 1, 3))
    wout = np.ascontiguousarray(f(w_out).reshape(2, 8, 128, 1024).transpose(0, 2, 1, 3))
    wuq = np.ascontiguousarray(f(w_uq).reshape(2, 2, 128, 384).transpose(0, 2, 1, 3))
    wuk = np.ascontiguousarray(f(w_uk))
    wuv = np.ascontiguousarray(f(w_uv))
    wukT = np.ascontiguousarray(f(w_uk).reshape(2, 128, 4, 64).transpose(0, 3, 2, 1))
    pw = np.zeros((2, 128, 2, 128), np.float32)
    pwf = f(pool_w)
    for c in range(2):
        pw[:, 0:64, c, 0:64] = pwf[:, 2 * c]
        pw[:, 64:128, c, 64:128] = pwf[:, 2 * c + 1]
    pvec = np.zeros((2, 128, NV), np.float32)
    pvec[:, :, 0:8] = f(norm_g).reshape(2, 8, 128).transpose(0, 2, 1)
    pvec[:, :, 8:32] = f(b_ada).reshape(2, 24, 128).transpose(0, 2, 1)
    pvec[:, :, 32:34] = f(pool_scale).reshape(2, 2, 128).transpose(0, 2, 1)
    cw = f(conv_w).reshape(2, 3, 2, 128)
    for t in range(3):
        for c in range(2):
            pvec[:, :, 34 + 2 * t + c] = cw[:, t, c]
    pvec[:, :, 40:42] = f(q_norm_g).reshape(2, 2, 128).transpose(0, 2, 1)
    pvec[:, :, 42] = f(kv_norm_g)
    pvec[:, 0:64, 43] = f(qn_g)
    pvec[:, 64:96, 43] = f(qr_g)
    pvec[:, 0:64, 44] = f(kn_g)
    pvec[:, 64:96, 44] = f(kr_g)
    sp = f(state_pool)
    sc = f(state_conv)
    pt = np.asarray(page_table, dtype=np.int32)
    cp_, cs_ = f(c_prompt), f(c_sample)
    in_maps = []
    for c in range(ncores):
        sl = slice(c * NS, (c + 1) * NS)
        m = {
            "xT": _fm(x_prompt[c]),
            "xsT": _fm(x_sample[sl].reshape(NS * TQ, D_MODEL)),
            "cT": _fm(np.concatenate([cp_[c:c + 1], cs_[sl]], 0)),
            "cache0": caches[0], "cache1": caches[1],
            "pt": np.ascontiguousarray(pt[sl]),
            "spT": np.ascontiguousarray(sp[:, sl].reshape(2, NS, 15, 2, 128).transpose(0, 4, 3, 1, 2)),
            "scT": np.ascontiguousarray(sc[:, sl].reshape(2, NS, 2, 2, 128).transpose(0, 4, 3, 1, 2)),
            "wada": wada, "win": win, "wout": wout, "wuq": wuq, "wuk": wuk, "wuv": wuv, "wukT": wukT, "pw": pw,
            "pvec": pvec, "cf": cf, "ropeP": ropeP, "ropeS": ropeS,
        }
        in_maps.append(m)
    res = run_bass_kernel_spmd(nc, in_maps, core_ids=list(range(ncores)))
    R = res.results

    def unfm(a):
        return np.ascontiguousarray(a.transpose(2, 1, 0).reshape(a.shape[2], -1))
    y_p = np.stack([unfm(R[c]["yT"]) for c in range(ncores)], 0)
    y_s = np.concatenate([unfm(R[c]["ysT"]).reshape(NS, TQ, D_MODEL) for c in range(ncores)], 0)
    lat_p = np.stack([R[c]["latP"].transpose(0, 2, 1) for c in range(ncores)], 1)
    kr_p = np.stack([R[c]["krP"].transpose(0, 2, 1) for c in range(ncores)], 1)
    pool_p = np.stack([R[c]["poolP"].transpose(0, 3, 2, 1).reshape(2, 15, 256) for c in range(ncores)], 1)
    conv_p = np.stack([R[c]["convP"].transpose(0, 3, 2, 1).reshape(2, 2, 256) for c in range(ncores)], 1)
    lat_s = np.concatenate([R[c]["latS"].transpose(0, 2, 1).reshape(2, NS, TQ, 128) for c in range(ncores)], 1)
    kr_s = np.concatenate([R[c]["krS"].transpose(0, 2, 1).reshape(2, NS, TQ, 32) for c in range(ncores)], 1)
    pool_s = np.concatenate([R[c]["poolS"].transpose(0, 3, 4, 2, 1).reshape(2, NS, 15, 256) for c in range(ncores)], 1)
    conv_s = np.concatenate([R[c]["convS"].transpose(0, 3, 4, 2, 1).reshape(2, NS, 2, 256) for c in range(ncores)], 1)
    outs = (y_p, y_s, lat_p, kr_p, pool_p, conv_p, lat_s, kr_s, pool_s, conv_s)
    return tuple(np.ascontiguousarray(o, dtype=np.float32) for o in outs)
```

```python
import math
from contextlib import ExitStack
import numpy as np
import concourse.bass as bass
import concourse.mybir as mybir
from concourse.bass_utils import run_bass_kernel_spmd

F32 = mybir.dt.float32
BF16 = mybir.dt.bfloat16
I32 = mybir.dt.int32
U8 = mybir.dt.uint8
AF = mybir.ActivationFunctionType
ALU = mybir.AluOpType
AX = mybir.AxisListType

D_MODEL = 1024
DEPTH = 2
D_IN = 2464
NOPE, ROPE = 64, 32
EPS = 1e-6
SM_SCALE = 1.0 / math.sqrt(NOPE + ROPE)
ROPE_THETA = 10000.0
PAGE_BYTES = 128 * 160 * 4
NV = 48
C_ID, C_ONES, C_B96, C_ROT, C_INVW, C_INVC, C_MASK, C_MASK4, C_SELQ, C_BDM = 0, 128, 256, 352, 448, 450, 482, 610, 626, 754
NCF = 758


class Buf:
    __slots__ = ("w", "r")

    def __init__(self):
        self.w = None
        self.r = {}


class TT:
    def __init__(self, ap, buf=None, excl=False):
        self.ap = ap
        self.b = buf or Buf()
        self.excl = excl

    def __getitem__(self, k):
        return self.ap[k]


class KB:
    ENG = ("sp", "act", "pe", "dve", "pool")

    def __init__(self, nc, es):
        self.nc = nc
        self.es = es
        self.q = {e: [] for e in self.ENG}
        self.waited = {e: {} for e in self.ENG}
        self.cnt = {}
        self.prog = {}
        for e in ("act", "pe", "dve", "pool"):
            s = es.enter_context(nc.semaphore("prog_" + e))
            self.prog[e] = s
            self.cnt[s] = 0
        self.dsems = []
        for i in range(20):
            s = es.enter_context(nc.semaphore("dq%d" % i))
            self.dsems.append(s)
            self.cnt[s] = 0
        self.dnext = 0
        self.swsems = []
        for i in range(8):
            s = es.enter_context(nc.semaphore("sw%d" % i))
            self.swsems.append(s)
            self.cnt[s] = 0
        self.swnext = 0
        self.nid = 0

    def newsem(self, name):
        s = self.es.enter_context(self.nc.semaphore(name))
        self.cnt[s] = 0
        return s

    def barrier(self):
        toks = [(s, c) for s, c in self.cnt.items() if c > 0]
        for e in self.ENG:
            for tok in toks:
                self.wait_tok(e, tok)

    def sb(self, shape, dt, name=None):
        self.nid += 1
        t = self.es.enter_context(self.nc.sbuf_tensor("sb_" + (name or ("t%d" % self.nid)), list(shape), dt))
        return TT(t)

    def ps(self, name):
        t = self.es.enter_context(self.nc.psum_tensor(name, [128, 512], F32))
        return TT(t, excl=True)

    def emit(self, eng, fn, reads=(), writes=(), dsem=None, ninc=1, noinc=False):
        deps = {}
        xr = [t for t in reads if t.excl]
        if xr:
            reads = [t for t in reads if not t.excl]
            writes = list(writes) + [t for t in xr if t not in writes]

        def add(tok):
            if tok is None:
                return
            s, v = tok
            if deps.get(s, 0) < v:
                deps[s] = v

        for t in reads:
            add(t.b.w)
        for t in writes:
            add(t.b.w)
            for s, v in t.b.r.items():
                add((s, v))
        if eng == "sp" or dsem is not None:
            if dsem is None:
                dsem = self.dsems[self.dnext]
                self.dnext = (self.dnext + 1) % len(self.dsems)
            add((dsem, self.cnt[dsem]))
            s = dsem
            inc = 16 * ninc
        else:
            s = self.prog[eng]
            inc = 1
        w = self.waited[eng]
        for ds, v in deps.items():
            if eng == "pe" and ds is self.prog["pe"]:
                continue
            if w.get(ds, 0) < v:
                self.q[eng].append(("wait", ds, v))
                w[ds] = v
        if noinc:
            tok = (s, self.cnt[s] + 1)
            self.q[eng].append(("op", fn, s, 0))
        else:
            self.cnt[s] += inc
            tok = (s, self.cnt[s])
            self.q[eng].append(("op", fn, s, inc))
        for t in writes:
            t.b.w = tok
            t.b.r = {}
        for t in reads:
            if t.b.r.get(s, 0) < tok[1]:
                t.b.r[s] = tok[1]
        return tok

    def wait_tok(self, eng, tok):
        s, v = tok
        w = self.waited[eng]
        if w.get(s, 0) < v:
            self.q[eng].append(("wait", s, v))
            w[s] = v

    def replay(self, eng, e):
        for it in self.q[eng]:
            if it[0] == "wait":
                e.wait_ge(it[1], it[2])
            else:
                _, fn, s, inc = it
                r = fn(e)
                if inc == 0:
                    continue
                if isinstance(r, list):
                    for x in r:
                        x.then_inc(s, inc // len(r))
                else:
                    r.then_inc(s, inc)

    def mm(self, out, lhsT, rhs, reads, writes, start=True, stop=True, skip=False, inc=None):
        if inc is None:
            inc = True
        return self.emit("pe", lambda e: e.matmul(out, lhsT=lhsT, rhs=rhs, start=start, stop=stop,
                                                  skip_group_check=skip), reads, writes, noinc=not inc)

    def tr(self, out, in_, ident, reads, writes, inc=True):
        return self.emit("pe", lambda e: e.transpose(out=out, in_=in_, identity=ident), reads, writes, noinc=not inc)

    def rsqrt(self, out, in_, scale, eps_ap, reads, writes):
        self.act(out, in_, AF.Ln, reads, writes, bias=eps_ap, scale=scale)
        self.act(out, out, AF.Exp, writes, writes, scale=-0.5)

    def act(self, out, in_, func, reads, writes, bias=None, scale=None):
        kw = {}
        if bias is not None:
            kw["bias"] = bias
        if scale is not None:
            kw["scale"] = scale
        return self.emit("act", lambda e: e.activation(out=out, in_=in_, func=func, **kw), reads, writes)

    def tt(self, eng, out, in0, in1, op, reads, writes):
        return self.emit(eng, lambda e: e.tensor_tensor(out=out, in0=in0, in1=in1, op=op), reads, writes)

    def ts(self, eng, out, in0, s1, op0, reads, writes, s2=None, op1=None):
        if op1 is None:
            return self.emit(eng, lambda e: e.tensor_scalar(out=out, in0=in0, scalar1=s1, scalar2=None, op0=op0),
                             reads, writes)
        return self.emit(eng, lambda e: e.tensor_scalar(out=out, in0=in0, scalar1=s1, scalar2=s2, op0=op0, op1=op1),
                         reads, writes)

    def stt(self, eng, out, in0, scalar, in1, op0, op1, reads, writes):
        return self.emit(eng, lambda e: e.scalar_tensor_tensor(out=out, in0=in0, scalar=scalar, in1=in1,
                                                                op0=op0, op1=op1), reads, writes)

    def cp(self, eng, out, in_, reads, writes):
        if eng == "act":
            return self.act(out, in_, AF.Copy, reads, writes)
        return self.emit(eng, lambda e: e.tensor_copy(out=out, in_=in_), reads, writes)

    def recip(self, out, in_, reads, writes):
        return self.emit("dve", lambda e: e.reciprocal(out=out, in_=in_), reads, writes)

    def memset(self, eng, out, val, writes):
        return self.emit(eng, lambda e: e.memset(out, val), (), writes)

    def dma(self, out, in_, reads, writes, eng="sp", **kw):
        if eng == "sp":
            return self.emit("sp", lambda e: e.dma_start(out=out, in_=in_, **kw), reads, writes)
        ds = self.swsems[self.swnext]
        self.swnext = (self.swnext + 1) % len(self.swsems)
        return self.emit("pool", lambda e: e.dma_start(out=out, in_=in_, **kw), reads, writes, dsem=ds)


def bc(ap, shape):
    return ap.broadcast_to(list(shape))


def build(cfg):
    T, NPG, NPOOL, NS, TQ = cfg["T"], cfg["NPG"], cfg["NPOOL"], cfg["NS"], cfg["TQ"]
    NTB = 128
    NBLK = T // NTB
    NSEQ = 1 + NS
    NST = NS * TQ
    GP = 4
    NGRP = NPG // GP
    nc = bass.Bass("TRN2", target_bir_lowering=False)
    D = {}

    def din(name, shape, dt=F32):
        D[name] = nc.dram_tensor(name, list(shape), dt, kind="ExternalInput").ap()

    def dout(name, shape, dt=F32):
        D[name] = nc.dram_tensor(name, list(shape), dt, kind="ExternalOutput").ap()

    din("xT", [128, 8, T]); din("xsT", [128, 8, NST]); din("cT", [128, 8, NSEQ])
    din("cache0", [NPOOL * 128 * 160]); din("cache1", [NPOOL * 128 * 160])
    din("pt", [NS, NPG], I32)
    din("spT", [2, 128, 2, NS, 15]); din("scT", [2, 128, 2, NS, 2])
    din("wada", [2, 24, 128, 8, 128]); din("win", [2, 128, 8, D_IN]); din("wout", [2, 128, 8, 1024])
    din("wuq", [2, 128, 2, 384]); din("wuk", [2, 128, 256]); din("wuv", [2, 128, 512])
    din("wukT", [2, 64, 4, 128]); din("pw", [2, 128, 2, 128]); din("pvec", [2, 128, NV])
    din("cf", [128, NCF]); din("ropeP", [2, 96, T]); din("ropeS", [2, 96, TQ])
    dout("yT", [128, 8, T]); dout("ysT", [128, 8, NST])
    dout("latP", [2, 128, T]); dout("krP", [2, 32, T]); dout("poolP", [2, 128, 2, 15]); dout("convP", [2, 128, 2, 2])
    dout("latS", [2, 128, NST]); dout("krS", [2, 32, NST]); dout("poolS", [2, 128, 2, NS, 15])
    dout("convS", [2, 128, 2, NS, 2])
    cbytes = [D["cache0"].bitcast(U8), D["cache1"].bitcast(U8)]

    es = ExitStack()
    with es:
        k = KB(nc, es)
        xT = k.sb([128, 8, T], F32, "xT")
        xsT = k.sb([128, 8, NST], F32, "xsT")
        win = k.sb([128, 8, D_IN], BF16, "win")
        wout = k.sb([128, 8, 1024], BF16, "wout")
        wuq = k.sb([128, 2, 416], BF16, "wuq")
        wuk = k.sb([128, 256], BF16, "wuk")
        wuv = k.sb([128, 512], BF16, "wuv")
        wukT = k.sb([64, 4, 128], BF16, "wukT")
        pw = k.sb([128, 2, 128], BF16, "pw")
        pvec = k.sb([128, NV], F32, "pvec")
        cf = k.sb([128, NCF], F32, "cf")
        cfb = k.sb([128, 256], BF16, "cfb")
        siluT = k.sb([128, 8, NSEQ], F32, "siluT")
        modT = k.sb([128, 24, NSEQ], F32, "modT")
        amod = k.sb([128, 8, NSEQ], F32, "amod")
        epsT = k.sb([128, 1], F32, "eps")
        pts = k.sb([NS, NPG], I32, "pts")
        offs = k.sb([NS, NPG], I32, "offs")
        wa = [k.sb([128, 8, 128], F32, "wa0")]
        ropeP = k.sb([96, 2, NTB], F32, "ropeP")
        ropeS = k.sb([96, 2, TQ], F32, "ropeS")
        hT = k.sb([128, 8, NTB], BF16, "hT")
        mixed = k.sb([128, 8, NTB], BF16, "mixed")
        SW = max(NS * (16 + TQ), 16 + NTB)
        WN = max(NTB, NST)
        WQ = max(NTB, 2 * NST)

        def slab(name, dt=F32, w=WN):
            return k.sb([128, w], dt, name)

        sq = [slab("sq%d" % i) for i in range(2)]
        rstd = slab("rstd")
        rstdP = [slab("rstdP0", w=NTB), slab("rstdP1", w=NTB)]
        pre_rstd = {}
        tmpA = slab("tmpA")
        U = k.sb([128, 2, SW], F32, "U")
        S2 = k.sb([128, 2, SW], F32, "S2")
        S4 = k.sb([128, 2, SW], F32, "S4")
        S8 = slab("S8", w=SW)
        S16 = slab("S16", w=SW)
        szp = k.sb([128, 2, WN], BF16, "szp")
        dT = k.sb([128, 2, WN], BF16, "dT")
        cgs = k.sb([128, 2, WN], F32, "cgs")
        Vc = k.sb([128, 2, SW], F32, "Vc")
        bgs = k.sb([128, 2, WN], F32, "bgs")
        cacc = k.sb([128, 2, WN], F32, "cacc")
        szc = k.sb([128, 2, WN], BF16, "szc")
        cqs = k.sb([128, 2, WN], F32, "cqs")
        cqn = k.sb([128, 2, WN], BF16, "cqn")
        ckr = slab("ckr")
        ckvT = slab("ckvT")
        ckvb = slab("ckvb", BF16)
        qraw = slab("qraw", w=WQ)
        qsq = slab("qsq", w=WQ)
        qr_ = slab("qr_", w=WQ)
        qn = slab("qn", w=WQ)
        qt1 = slab("qt1", w=WQ)
        qrawL = [qraw, slab("qrawB", w=NTB)]
        qsqL = [qsq, slab("qsqB", w=NTB)]
        qrL = [qr_, slab("qrB", w=NTB)]
        qnL = [qn, slab("qnB", w=NTB)]
        qt1L = [qt1, slab("qt1B", w=NTB)]
        Qb = k.sb([96, 4, WN], BF16, "Qb")
        krT = slab("krT")
        szm = k.sb([128, 4, WN], BF16, "szm")
        PT = [slab("PT%d" % i, BF16, w=512) for i in range(2)]
        On = [slab("On%d" % i, w=128) for i in range(2)]
        rl = [slab("rl%d" % i, w=2) for i in range(2)]
        NSLOT = 4
        samp_specs = [("KsT", [96, 4, NST], BF16), ("qp", [64, 4, NST], BF16), ("qabs", [128, NS, 16], BF16),
                      ("qr4s", [128, NS, 16], F32), ("BD", [128, NS, 4, 16], BF16)]
        samp_specs += [("pg%d" % i, [128, GP, 160], F32) for i in range(NSLOT)]
        samp_specs += [("nat%d" % i, [128, GP, 130], BF16) for i in range(4)]
        samp_specs += [("latT%d" % i, [128, 512], BF16) for i in range(2)]
        samp_specs += [("krTb%d" % i, [128, 128], BF16) for i in range(2)]
        samp_specs += [("krp%d" % i, [128, 128], F32) for i in range(2)]
        samp_specs += [("sqb%d" % i, [128, 512], BF16) for i in range(4)]
        samp_specs += [("ssq%d" % i, [128, 16], F32) for i in range(2)]
        samp_specs += [("rinv%d" % i, [128, 16], F32) for i in range(2)]
        samp_specs += [("stmp%d" % i, [128, 64], F32) for i in range(2)]
        samp_specs += [("stmp2%d" % i, [128, 64], F32) for i in range(2)]
        samp_specs += [("pTt%d" % i, [128, 64], BF16) for i in range(3)]
        samp_specs += [("pn", [128, 16], F32), ("pnb0", [128, 16], BF16), ("pnb1", [128, 16], BF16),
                       ("natn0", [128, 130], BF16), ("natn1", [128, 130], BF16),
                       ("lo", [128, 128], F32), ("loT", [128, 16], BF16), ("rls", [128, 2], F32)]

        def nbytes(shape, dt):
            n = 1
            for d_ in shape[1:]:
                n *= d_
            return ((n * (4 if dt == F32 else 2) + 31) // 32) * 32
        samp_need = sum(nbytes(sh, dt) for _, sh, dt in samp_specs)
        prompt_need = 4 * T * 2 + (T // 128) * 4 * 130 * 2
        ABYTES = max(samp_need, prompt_need)
        arena = es.enter_context(nc.sbuf_tensor("sb_arena", [128, ABYTES // 2], BF16))

        def aview(off_b, shape, dt):
            n = 1
            for d_ in shape[1:]:
                n *= d_
            nb = n * (4 if dt == F32 else 2)
            ap = arena[0:shape[0], off_b // 2:(off_b + nb) // 2]
            if dt == F32:
                ap = ap.bitcast(F32)
            if len(shape) == 3:
                ap = ap.rearrange("p (a b) -> p a b", a=shape[1])
            elif len(shape) == 4:
                ap = ap.rearrange("p (a b c) -> p a b c", a=shape[1], b=shape[2])
            return TT(ap)
        KT = aview(0, [96, 4, T], BF16)
        VX = aview(4 * T * 2, [128, T // 128, 4, 130], BF16)
        SV = {}
        off_ = 0
        for nm, sh, dt in samp_specs:
            SV[nm] = aview(off_, sh, dt)
            off_ += nbytes(sh, dt)
        KsT, qp, qabs, qr4s, BD = SV["KsT"], SV["qp"], SV["qabs"], SV["qr4s"], SV["BD"]
        pg32 = [SV["pg%d" % i] for i in range(NSLOT)]
        pgsem = [k.newsem("pgsem%d" % i) for i in range(NSLOT)]
        nat = [SV["nat%d" % i] for i in range(4)]
        latT = [SV["latT%d" % i] for i in range(2)]
        krTb = [SV["krTb%d" % i] for i in range(2)]
        krp = [SV["krp%d" % i] for i in range(2)]
        sqb = [SV["sqb%d" % i] for i in range(4)]
        ssq = [SV["ssq%d" % i] for i in range(2)]
        rinv = [SV["rinv%d" % i] for i in range(2)]
        stmp = [SV["stmp%d" % i] for i in range(2)]
        stmp2 = [SV["stmp2%d" % i] for i in range(2)]
        pTt = [SV["pTt%d" % i] for i in range(3)]
        pn, lo, loT, rls = SV["pn"], SV["lo"], SV["loT"], SV["rls"]
        pnb = [SV["pnb0"], SV["pnb1"]]
        natn = [SV["natn0"], SV["natn1"]]
        PS = [k.ps("ps%d" % i) for i in range(8)]
        psn = [0]

        def nps():
            p = PS[psn[0] % 8]
            psn[0] += 1
            return p

        ident = cf[:, C_ID:C_ID + 128]
        ones = cf[:, C_ONES:C_ONES + 128]
        B96 = cf[0:96, C_B96:C_B96 + 96]
        ROT = cf[0:96, C_ROT:C_ROT + 96]
        maskb = cfb[:, 0:128]
        mask4 = cf[0:4, C_MASK4:C_MASK4 + 16]
        SELQ = cfb[0:96, 128:256]
        BDM = cf[:, C_BDM:C_BDM + 4]

        k.dma(cf[:], D["cf"][:, :], (), [cf])
        k.dma(xT[:], D["xT"][:, :, :], (), [xT])
        k.dma(xsT[:], D["xsT"][:, :, :], (), [xsT])
        k.dma(siluT[:], D["cT"][:, :, :], (), [siluT])
        k.dma(pts[:], D["pt"][:, :], (), [pts])
        k.dma(ropeS[:], D["ropeS"].rearrange("c p t -> p c t"), (), [ropeS])
        k.cp("dve", cfb[:, 0:128], cf[:, C_MASK:C_MASK + 128], [cf], [cfb])
        k.cp("dve", cfb[:, 128:256], cf[:, C_SELQ:C_SELQ + 128], [cf], [cfb])
        k.memset("dve", epsT[:], EPS, [epsT])
        k.ts("dve", offs[:], pts[:], float(PAGE_BYTES), ALU.mult, [pts], [offs])
        k.act(siluT[:], siluT[:], AF.Silu, [siluT], [siluT])
        k.memset("pool", wuq[:, :, 384:416], 0.0, [wuq])
        def load_weights(l):
            def cast_dma(dst, src, t):
                k.dma(dst, src, (), [t], eng="pool", max_dma_last_dim=4096)
            for kk in range(8):
                cast_dma(win[:, kk, 0:1232], D["win"][l, :, kk, 0:1232], win)
                cast_dma(win[:, kk, 1232:D_IN], D["win"][l, :, kk, 1232:D_IN], win)
            for kk in range(8):
                cast_dma(wout[:, kk, :], D["wout"][l, :, kk, :], wout)
            cast_dma(wuq[:, :, 0:384], D["wuq"][l], wuq)
            cast_dma(wuk[:], D["wuk"][l], wuk)
            cast_dma(wuv[:], D["wuv"][l], wuv)
            cast_dma(wukT[:], D["wukT"][l], wukT)
            cast_dma(pw[:], D["pw"][l], pw)
            k.dma(pvec[:], D["pvec"][l], (), [pvec])

        def adaln(l):
            pm = nps()
            for j in range(24):
                w_ = wa[0]
                k.dma(w_[:], D["wada"][l, j], (), [w_])
                for kk in range(8):
                    k.mm(pm[:, j * NSEQ:(j + 1) * NSEQ], w_[:, kk, :], siluT[:, kk, :], [w_, siluT], [pm],
                         start=(kk == 0), stop=(kk == 7), inc=(kk == 7))
            k.tt("dve", modT[:], pm[:, 0:24 * NSEQ].rearrange("p (j s) -> p j s", s=NSEQ),
                 bc(pvec[:, 8:32].unsqueeze(2), [128, 24, NSEQ]), ALU.add, [pm, pvec], [modT])
            k.ts("dve", amod[:], modT[:, 8:16, :], 1.0, ALU.add, [modT], [amod])
            k.tt("dve", amod[:], amod[:], bc(pvec[:, 0:8].unsqueeze(2), [128, 8, NSEQ]), ALU.mult, [amod, pvec], [amod])

        def rms_stats(srcs, scale, NT, rd):
            pst = nps()
            n = len(srcs)
            for i, (ap, t) in enumerate(srcs):
                s_ = sq[i % 2]
                k.act(s_[:, 0:NT], ap, AF.Square, [t], [s_])
                k.mm(pst[:, 0:NT], ones, s_[:, 0:NT], [cf, s_], [pst], start=(i == 0), stop=(i == n - 1))
            k.rsqrt(rd[:, 0:NT], pst[:, 0:NT], scale, epsT[:, 0:1], [pst, epsT], [rd])

        def norm_rope(src_ps, P_, W, gcol, cos_ap, sin_ap, out_ap, out_t, NTl, extra_reads=(), nh=1):
            def hv(ap):
                return ap if nh == 1 else ap.rearrange("p (h n) -> p h n", h=nh)
            k.cp("dve", qraw[0:P_, 0:W], src_ps[0:P_, 0:W], [src_ps], [qraw])
            k.act(qsq[0:P_, 0:W], src_ps[0:P_, 0:W], AF.Square, [src_ps], [qsq])
            p2 = nps()
            k.mm(p2[0:P_, 0:W], B96[0:P_, 0:P_], qsq[0:P_, 0:W], [cf, qsq], [p2])
            k.rsqrt(qr_[0:P_, 0:W], p2[0:P_, 0:W], 1.0, epsT[0:P_, 0:1], [p2, epsT], [qr_])
            k.stt("dve", qn[0:P_, 0:W], qraw[0:P_, 0:W], pvec[0:P_, gcol:gcol + 1], qr_[0:P_, 0:W], ALU.mult, ALU.mult,
                  [qraw, pvec, qr_], [qn])
            if cos_ap is None:
                k.cp("dve", out_ap, hv(qn[0:P_, 0:W]), [qn], [out_t])
                return
            p3 = nps()
            k.mm(p3[0:P_, 0:W], ROT[0:P_, 0:P_], qn[0:P_, 0:W], [cf, qn], [p3])
            k.tt("dve", hv(qt1[0:P_, 0:W]), hv(qn[0:P_, 0:W]), cos_ap, ALU.mult, [qn] + list(extra_reads), [qt1])
            k.tt("dve", hv(qn[0:P_, 0:W]), hv(p3[0:P_, 0:W]), sin_ap, ALU.mult, [p3] + list(extra_reads), [qn])
            k.tt("dve", out_ap, hv(qt1[0:P_, 0:W]), hv(qn[0:P_, 0:W]), ALU.add, [qt1, qn], [out_t])

        def norm_rope_pipe(calls, W, cos_ap, sin_ap, rt):
            n = len(calls)
            st = [dict() for _ in calls]

            def sA(c):
                st[c]["src"] = calls[c]["src"]()

            def sB(c):
                s_, P_, src = c % 2, calls[c]["P"], st[c]["src"]
                k.cp("dve", qrawL[s_][0:P_, 0:W], src[0:P_, 0:W], [src], [qrawL[s_]])
                k.act(qsqL[s_][0:P_, 0:W], src[0:P_, 0:W], AF.Square, [src], [qsqL[s_]])
                p2 = nps()
                st[c]["p2"] = p2
                k.mm(p2[0:P_, 0:W], B96[0:P_, 0:P_], qsqL[s_][0:P_, 0:W], [cf, qsqL[s_]], [p2])

            def sC(c):
                s_, P_, p2, g = c % 2, calls[c]["P"], st[c]["p2"], calls[c]["g"]
                k.rsqrt(qrL[s_][0:P_, 0:W], p2[0:P_, 0:W], 1.0, epsT[0:P_, 0:1], [p2, epsT], [qrL[s_]])
                k.stt("dve", qnL[s_][0:P_, 0:W], qrawL[s_][0:P_, 0:W], pvec[0:P_, g:g + 1], qrL[s_][0:P_, 0:W], ALU.mult, ALU.mult,
                      [qrawL[s_], pvec, qrL[s_]], [qnL[s_]])
                if calls[c]["rope"]:
                    p3 = nps()
                    st[c]["p3"] = p3
                    k.mm(p3[0:P_, 0:W], ROT[0:P_, 0:P_], qnL[s_][0:P_, 0:W], [cf, qnL[s_]], [p3])
                else:
                    k.cp("dve", calls[c]["out"], qnL[s_][0:P_, 0:W], [qnL[s_]], [calls[c]["ot"]])

            def sD(c):
                if not calls[c]["rope"]:
                    return
                s_, P_, p3 = c % 2, calls[c]["P"], st[c]["p3"]
                k.tt("dve", qt1L[s_][0:P_, 0:W], qnL[s_][0:P_, 0:W], cos_ap, ALU.mult, [qnL[s_], rt], [qt1L[s_]])
                k.tt("dve", qnL[s_][0:P_, 0:W], p3[0:P_, 0:W], sin_ap, ALU.mult, [p3, rt], [qnL[s_]])
                k.tt("dve", calls[c]["out"], qt1L[s_][0:P_, 0:W], qnL[s_][0:P_, 0:W], ALU.add, [qt1L[s_], qnL[s_]], [calls[c]["ot"]])

            for t in range(n + 3):
                if t < n:
                    sA(t)
                if 0 <= t - 1 < n:
                    sB(t - 1)
                if 0 <= t - 2 < n:
                    sC(t - 2)
                if 0 <= t - 3 < n:
                    sD(t - 3)

        def block_front(l, grp, bi):
            if grp == "P":
                nseq, Tq, NT = 1, NTB, NTB
                xv = xT[:, :, bi * NTB:(bi + 1) * NTB]
                xt = xT
                c0 = bi * NTB
            else:
                nseq, Tq, NT = NS, TQ, NST
                xv = xsT[:, :, :]
                xt = xsT
                c0 = 0
            L = 16 + Tq

            def v3(ap):
                return ap.rearrange("p (s t) -> p s t", t=Tq)

            def ext(tile_ap):
                return tile_ap[:, 0:nseq * L].rearrange("p (s l) -> p s l", l=L)

            for tl in (U, Vc):
                for c in range(2):
                    e_ = ext(tl[:, c, :])
                    if grp == "P":
                        if bi == 0:
                            k.memset("pool", e_[:, :, 0:16], 0.0, [tl])
                        else:
                            k.cp("pool", e_[:, :, 0:16], e_[:, :, Tq:Tq + 16], [tl], [tl])
            if grp == "S":
                for c in range(2):
                    k.memset("pool", ext(U[:, c, :])[:, :, 0:1], 0.0, [U])
                    k.dma(ext(U[:, c, :])[:, :, 1:16], D["spT"][l, :, c, :, :], (), [U])
                    k.dma(ext(Vc[:, c, :])[:, :, 14:16], D["scT"][l, :, c, :, :], (), [Vc])
            if grp == "P":
                k.dma(ropeP[:], D["ropeP"][:, :, c0:c0 + NTB].rearrange("c p t -> p c t"), (), [ropeP])
            if grp == "P":
                rs_ = rstdP[bi % 2]
                if (l, bi) not in pre_rstd:
                    rms_stats([(xv[:, kk, :], xt) for kk in range(8)], 1.0 / D_MODEL, NT, rs_)
            else:
                rs_ = rstd
                rms_stats([(xv[:, kk, :], xt) for kk in range(8)], 1.0 / D_MODEL, NT, rs_)
            for kk in range(8):
                if grp == "P":
                    k.stt("dve", tmpA[:, 0:NT], xv[:, kk, :], amod[:, kk, 0:1], rs_[:, 0:NT], ALU.mult, ALU.mult,
                          [xt, amod, rs_], [tmpA])
                    k.act(hT[:, kk, 0:NT], tmpA[:, 0:NT], AF.Identity, [tmpA, modT], [hT], bias=modT[:, kk, 0:1], scale=1.0)
                else:
                    k.tt("dve", tmpA[:, 0:NT], xv[:, kk, :], rstd[:, 0:NT], ALU.mult, [xt, rstd], [tmpA])
                    k.tt("dve", v3(tmpA[:, 0:NT]), v3(tmpA[:, 0:NT]), bc(amod[:, kk, 1:NSEQ].unsqueeze(2), [128, NS, TQ]),
                         ALU.mult, [tmpA, amod], [tmpA])
                    k.tt("dve", v3(hT[:, kk, 0:NT]), v3(tmpA[:, 0:NT]), bc(modT[:, kk, 1:NSEQ].unsqueeze(2), [128, NS, TQ]),
                         ALU.add, [tmpA, modT], [hT])

            def proj(col, M):
                p = nps()
                for kk in range(8):
                    k.mm(p[0:M, 0:NT], win[:, kk, col:col + M], hT[:, kk, 0:NT], [win, hT], [p], start=(kk == 0), stop=(kk == 7),
                         inc=(kk == 7))
                return p

            for c in range(2):
                p = proj(256 + c * 128, 128)
                k.act(szp[:, c, 0:NT], p[:, 0:NT], AF.Silu, [p], [szp])
            for c in range(2):
                p = proj(1280 + c * 128, 128)
                k.act(szc[:, c, 0:NT], p[:, 0:NT], AF.Silu, [p], [szc])
            for c in range(4):
                p = proj(1952 + c * 128, 128)
                k.act(szm[:, c, 0:NT], p[:, 0:NT], AF.Silu, [p], [szm])
            for c in range(2):
                p = proj(c * 128, 128)
                k.cp("act", ext(U[:, c, :])[:, :, 16:L], v3(p[:, 0:NT]), [p], [U])
            for c in range(2):
                p = proj(1024 + c * 128, 128)
                k.cp("act", cgs[:, c, 0:NT], p[:, 0:NT], [p], [cgs])
            for c in range(2):
                p = proj(512 + c * 128, 128)
                k.tt("dve", ext(Vc[:, c, :])[:, :, 16:L], v3(p[:, 0:NT]), v3(cgs[:, c, 0:NT]), ALU.mult, [p, cgs], [Vc])
            for c in range(2):
                p = proj(768 + c * 128, 128)
                k.cp("act", bgs[:, c, 0:NT], p[:, 0:NT], [p], [bgs])
            for c in range(2):
                p = proj(1536 + c * 128, 128)
                k.cp("act", cqs[:, c, 0:NT], p[:, 0:NT], [p], [cqs])
            p = proj(1792, 128)
            k.cp("act", ckr[:, 0:NT], p[:, 0:NT], [p], [ckr])
            pkr = proj(1856, 128)
            if grp == "P":
                cos1, sin1 = ropeP[:, 0, 0:NT], ropeP[:, 1, 0:NT]
                cos2 = bc(ropeP[:, 0:1, 0:NT], [96, 2, NT])
                sin2 = bc(ropeP[:, 1:2, 0:NT], [96, 2, NT])
                rt = ropeP
            else:
                cos1 = bc(ropeS[:, 0:1, :], [96, NS, TQ])
                sin1 = bc(ropeS[:, 1:2, :], [96, NS, TQ])
                cos2 = bc(ropeS[:, 0:1, :], [96, 2 * NS, TQ])
                sin2 = bc(ropeS[:, 1:2, :], [96, 2 * NS, TQ])
                rt = ropeS
            Kt = KT if grp == "P" else KsT
            rms_stats([(ckr[:, 0:NT], ckr)], 1.0 / 128, NT, rstd)
            k.stt("dve", ckvT[:, 0:NT], ckr[:, 0:NT], pvec[:, 42:43], rstd[:, 0:NT], ALU.mult, ALU.mult, [ckr, pvec, rstd], [ckvT])
            k.cp("dve", ckvb[:, 0:NT], ckvT[:, 0:NT], [ckvT], [ckvb])
            if grp == "P":
                k.dma(D["latP"][l, :, c0:c0 + NT], ckvT[:, 0:NT], [ckvT], ())
            else:
                k.dma(D["latS"][l, :, :], ckvT[:, 0:NT], [ckvT], ())
            rms_stats([(cqs[:, c, 0:NT], cqs) for c in range(2)], 1.0 / 256, NT, rstd)
            for c in range(2):
                k.stt("dve", cqn[:, c, 0:NT], cqs[:, c, 0:NT], pvec[:, 40 + c:41 + c], rstd[:, 0:NT], ALU.mult, ALU.mult,
                      [cqs, pvec, rstd], [cqn])
            if grp == "P":
                calls = [dict(src=(lambda: pkr), P=96, g=44, rope=True, out=krT[0:96, 0:NT], ot=krT)]
                for h in range(4):
                    def srck(h=h):
                        p = nps()
                        k.mm(p[0:64, 0:NT], wuk[:, h * 64:(h + 1) * 64], ckvb[:, 0:NT], [wuk, ckvb], [p])
                        return p
                    calls.append(dict(src=srck, P=64, g=44, rope=False, out=Kt[0:64, h, c0:c0 + NT], ot=Kt))
                for h in range(4):
                    def srcq(h=h):
                        p = nps()
                        for kk in range(2):
                            k.mm(p[0:128, 0:NT], wuq[:, kk, h * 96:h * 96 + 128], cqn[:, kk, 0:NT], [wuq, cqn], [p],
                                 start=(kk == 0), stop=(kk == 1), inc=(kk == 1))
                        return p
                    calls.append(dict(src=srcq, P=96, g=43, rope=True, out=Qb[0:96, h, 0:NT], ot=Qb))
                norm_rope_pipe(calls, NT, cos1, sin1, rt)
            else:
                norm_rope_s(pkr, 1, 44, cos1, sin1, krT[0:96, 0:NT], krT, rt)
                for hp in range(2):
                    p = nps()
                    for hh in range(2):
                        h = 2 * hp + hh
                        for kk in range(2):
                            k.mm(p[0:128, hh * NT:(hh + 1) * NT], wuq[:, kk, h * 96:h * 96 + 128], cqn[:, kk, 0:NT], [wuq, cqn], [p],
                                 start=(kk == 0), stop=(kk == 1))
                    norm_rope_s(p, 2, 43, cos2, sin2, Qb[0:96, 2 * hp:2 * hp + 2, 0:NT], Qb, rt)
                for hp in range(2):
                    p = nps()
                    for hh in range(2):
                        h = 2 * hp + hh
                        k.mm(p[0:64, hh * NT:(hh + 1) * NT], wuk[:, h * 64:(h + 1) * 64], ckvb[:, 0:NT], [wuk, ckvb], [p])
                    oap = Kt[0:64, 2 * hp:2 * hp + 2, c0:c0 + NT]
                    norm_rope(p, 64, 2 * NT, 44, None, None, oap, Kt, NT, nh=2)
            for h in range(4):
                k.cp("dve", Kt[64:96, h, c0:c0 + NT], krT[64:96, 0:NT], [krT], [Kt])
            if grp == "P":
                k.dma(D["krP"][l, :, c0:c0 + NT], krT[64:96, 0:NT], [krT], ())
            else:
                k.dma(D["krS"][l, :, :], krT[64:96, 0:NT], [krT], ())
            if grp == "P":
                for j in range(NT // 128):
                    p = nps()
                    k.mm(p[:, 0:512], ckvb[:, j * 128:(j + 1) * 128], wuv[:], [ckvb, wuv], [p])
                    tj = (c0 // 128) + j
                    k.cp("act", VX[:, tj, :, 0:128], p[:, 0:512].rearrange("p (h v) -> p h v", h=4), [p], [VX])

            if grp == "P" and bi + 1 < NBLK:
                xn = xT[:, :, (bi + 1) * NTB:(bi + 2) * NTB]
                rms_stats([(xn[:, kk, :], xT) for kk in range(8)], 1.0 / D_MODEL, NTB, rstdP[(bi + 1) % 2])
                pre_rstd[(l, bi + 1)] = True
            for c in range(2):
                u_, s2_, s4_ = ext(U[:, c, :]), ext(S2[:, c, :]), ext(S4[:, c, :])
                k.tt("pool", s2_[:, :, 1:L], u_[:, :, 1:L], u_[:, :, 0:L - 1], ALU.add, [U], [S2])
                k.tt("pool", s4_[:, :, 3:L], s2_[:, :, 3:L], s2_[:, :, 1:L - 2], ALU.add, [S2], [S4])
            s4_, s8_, s16_ = ext(S4[:, 1, :]), ext(S8[:]), ext(S16[:])
            k.tt("pool", s8_[:, :, 7:L], s4_[:, :, 7:L], s4_[:, :, 3:L - 4], ALU.add, [S4], [S8])
            k.tt("pool", s16_[:, :, 15:L], s8_[:, :, 15:L], s8_[:, :, 7:L - 8], ALU.add, [S8], [S16])
            srcs = {(0, 0): (S2[:, 0, :], S2), (1, 0): (S4[:, 0, :], S4), (0, 1): (S8[:], S8), (1, 1): (S16[:], S16)}
            for c in range(2):
                for hf in range(2):
                    pr = slice(hf * 64, (hf + 1) * 64)
                    sap, st = srcs[(hf, c)]
                    s_ = ext(sap)[pr, :, 16:L]
                    k.stt("dve", v3(dT[pr, c, 0:NT]), s_, cf[pr, C_INVW + c:C_INVW + c + 1], ext(U[:, c, :])[pr, :, 16:L],
                          ALU.mult, ALU.subtract, [st, cf, U], [dT])
                    if grp == "P" and bi == 0:
                        k.tt("dve", tmpA[pr, 0:16], sap[pr, 16:32], cf[pr, C_INVC + c * 16:C_INVC + c * 16 + 16], ALU.mult,
                             [st, cf], [tmpA])
                        k.tt("dve", dT[pr, c, 0:16], tmpA[pr, 0:16], U[pr, c, 16:32], ALU.subtract, [tmpA, U], [dT])
            for c in range(2):
                p = nps()
                k.mm(p[:, 0:NT], pw[:, c, :], dT[:, c, 0:NT], [pw, dT], [p])
                k.stt("dve", mixed[:, c, 0:NT], p[:, 0:NT], pvec[:, 32 + c:33 + c], szp[:, c, 0:NT], ALU.mult, ALU.mult,
                      [p, pvec, szp], [mixed])
            if grp == "P":
                if bi == NBLK - 1:
                    for c in range(2):
                        k.dma(D["poolP"][l, :, c, :], U[:, c, 16 + Tq - 15:16 + Tq], [U], ())
            else:
                for c in range(2):
                    k.dma(D["poolS"][l, :, c, :, :], ext(U[:, c, :])[:, :, L - 15:L], [U], ())

            for c in range(2):
                v_ = ext(Vc[:, c, :])
                a_ = v3(cacc[:, c, 0:NT])
                k.ts("pool", a_, v_[:, :, 16:L], pvec[:, 34 + 4 + c:35 + 4 + c], ALU.mult, [Vc, pvec], [cacc])
                k.stt("dve", a_, v_[:, :, 15:L - 1], pvec[:, 34 + 2 + c:35 + 2 + c], a_, ALU.mult, ALU.add, [Vc, pvec, cacc], [cacc])
                k.stt("dve", a_, v_[:, :, 14:L - 2], pvec[:, 34 + c:35 + c], a_, ALU.mult, ALU.add, [Vc, pvec, cacc], [cacc])
                k.tt("pool", cacc[:, c, 0:NT], cacc[:, c, 0:NT], bgs[:, c, 0:NT], ALU.mult, [cacc, bgs], [cacc])
                k.tt("pool", mixed[:, 2 + c, 0:NT], cacc[:, c, 0:NT], szc[:, c, 0:NT], ALU.mult, [cacc, szc], [mixed])
            if grp == "P":
                if bi == NBLK - 1:
                    for c in range(2):
                        k.dma(D["convP"][l, :, c, :], Vc[:, c, 16 + Tq - 2:16 + Tq], [Vc], ())
            else:
                for c in range(2):
                    k.dma(D["convS"][l, :, c, :, :], ext(Vc[:, c, :])[:, :, L - 2:L], [Vc], ())


        def norm_rope_s(src_ps, nh, gcol, cos_ap, sin_ap, out_ap, out_t, rt):
            W = nh * NST
            P_ = 96
            k.cp("dve", qraw[0:P_, 0:W], src_ps[0:P_, 0:W], [src_ps], [qraw])
            k.act(qsq[0:P_, 0:W], src_ps[0:P_, 0:W], AF.Square, [src_ps], [qsq])
            p2 = nps()
            k.mm(p2[0:P_, 0:W], B96, qsq[0:P_, 0:W], [cf, qsq], [p2])
            k.rsqrt(qr_[0:P_, 0:W], p2[0:P_, 0:W], 1.0, epsT[0:P_, 0:1], [p2, epsT], [qr_])
            k.stt("dve", qn[0:P_, 0:W], qraw[0:P_, 0:W], pvec[0:P_, gcol:gcol + 1], qr_[0:P_, 0:W], ALU.mult, ALU.mult,
                  [qraw, pvec, qr_], [qn])
            p3 = nps()
            k.mm(p3[0:P_, 0:W], ROT, qn[0:P_, 0:W], [cf, qn], [p3])

            def v3(ap):
                return ap.rearrange("p (s t) -> p s t", t=TQ)
            k.tt("dve", v3(qt1[0:P_, 0:W]), v3(qn[0:P_, 0:W]), cos_ap, ALU.mult, [qn, rt], [qt1])
            k.tt("dve", v3(qn[0:P_, 0:W]), v3(p3[0:P_, 0:W]), sin_ap, ALU.mult, [p3, rt], [qn])
            if nh == 1:
                k.tt("dve", out_ap, qt1[0:P_, 0:W], qn[0:P_, 0:W], ALU.add, [qt1, qn], [out_t])
            else:
                k.tt("dve", out_ap, qt1[0:P_, 0:W].rearrange("p (h n) -> p h n", h=nh),
                     qn[0:P_, 0:W].rearrange("p (h n) -> p h n", h=nh), ALU.add, [qt1, qn], [out_t])

        def block_back(l, grp, bi):
            if grp == "P":
                NT = NTB
                xv = xT[:, :, bi * NTB:(bi + 1) * NTB]
                xt = xT
            else:
                NT = NST
                xv = xsT[:, :, :]
                xt = xsT
            for j in range(8):
                p = nps()
                for kk in range(8):
                    k.mm(p[:, 0:NT], wout[:, kk, j * 128:(j + 1) * 128], mixed[:, kk, 0:NT], [wout, mixed], [p],
                         start=(kk == 0), stop=(kk == 7), inc=(kk == 7))
                if grp == "P":
                    k.stt("dve", xv[:, j, :], p[:, 0:NT], modT[:, 16 + j, 0:1], xv[:, j, :], ALU.mult, ALU.add,
                          [p, modT, xt], [xt])
                else:
                    k.tt("dve", tmpA[:, 0:NT].rearrange("p (s t) -> p s t", t=TQ), p[:, 0:NT].rearrange("p (s t) -> p s t", t=TQ),
                         bc(modT[:, 16 + j, 1:NSEQ].unsqueeze(2), [128, NS, TQ]), ALU.mult, [p, modT], [tmpA])
                    k.tt("dve", xv[:, j, :], xv[:, j, :], tmpA[:, 0:NT], ALU.add, [xt, tmpA], [xt])

        def prompt_attn(l, bi):
            NT = NTB
            assert NT == 128
            qt = bi
            nkt = qt + 1
            batches = [list(range(s0, min(s0 + 4, nkt))) for s0 in range(0, nkt, 4)]
            nb = len(batches)

            def S_stage(h, bidx):
                kts = batches[bidx]
                pss = PS[4 + bidx % 2]
                pt_ = PT[bidx % 2]
                for i, kt in enumerate(kts):
                    k.mm(pss[:, i * 128:(i + 1) * 128], KT[0:96, h, kt * 128:(kt + 1) * 128], Qb[0:96, h, 0:NT], [KT, Qb], [pss],
                         inc=(i == len(kts) - 1))
                w = len(kts) * 128
                k.act(pt_[:, 0:w], pss[:, 0:w], AF.Exp, [pss], [pt_], scale=SM_SCALE)
                if kts[-1] == qt:
                    i = len(kts) - 1
                    k.tt("dve", pt_[:, i * 128:(i + 1) * 128], pt_[:, i * 128:(i + 1) * 128], maskb, ALU.mult, [pt_, cfb], [pt_])

            def PV_stage(h, bidx):
                kts = batches[bidx]
                pt_ = PT[bidx % 2]
                pso = PS[h % 2]
                for i, kt in enumerate(kts):
                    k.mm(pso[:, 0:129], pt_[:, i * 128:(i + 1) * 128], VX[:, kt, h, 0:129], [pt_, VX], [pso],
                         start=(kt == 0), stop=(kt == qt), inc=(i == len(kts) - 1))

            def epilogue(h):
                pso = PS[h % 2]
                r_ = rl[h % 2]
                o_ = On[h % 2]
                k.recip(r_[:, 0:1], pso[:, 128:129], [pso], [r_])
                k.ts("dve", o_[:, 0:128], pso[:, 0:128], r_[:, 0:1], ALU.mult, [pso, r_], [o_])
                ptr = PS[6 + h % 2]
                k.tr(ptr[:, 0:128], o_[:, 0:128], ident, [o_, cf], [ptr])
                k.tt("dve", mixed[:, 4 + h, 0:128], ptr[:, 0:128], szm[:, h, 0:128], ALU.mult, [ptr, szm], [mixed])

            for h in range(4):
                S_stage(h, 0)
                if h > 0:
                    epilogue(h - 1)
                for b_ in range(nb):
                    if b_ + 1 < nb:
                        S_stage(h, b_ + 1)
                    PV_stage(h, b_)
            epilogue(3)

        def sample_attn(l):
            k.ts("dve", qp[:, :, :], Qb[0:64, :, 0:NST], pvec[0:64, 44:45], ALU.mult, [Qb, pvec], [qp])
            p = nps()
            for h in range(4):
                k.mm(p[:, h * NST:(h + 1) * NST], wukT[:, h, :], qp[:, h, :], [wukT, qp], [p])
            k.cp("dve", qabs[:].rearrange("p b (h q) -> p h b q", h=4),
                 p[:, 0:4 * NST].rearrange("p (h b q) -> p h b q", h=4, b=NS), [p], [qabs])
            p = nps()
            for h in range(4):
                k.mm(p[:, h * NST:(h + 1) * NST], SELQ, Qb[0:96, h, 0:NST], [cfb, Qb], [p])
            k.cp("dve", qr4s[:].rearrange("p b (h q) -> p h b q", h=4),
                 p[:, 0:4 * NST].rearrange("p (h b q) -> p h b q", h=4, b=NS), [p], [qr4s])
            k.tt("dve", BD[:], bc(qr4s[:].unsqueeze(2), [128, NS, 4, 16]),
                 bc(BDM.unsqueeze(1).unsqueeze(3), [128, NS, 4, 16]), ALU.mult, [qr4s, cf], [BD])

            psT = [PS[0], PS[1]]
            psKR = [PS[2], PS[3]]
            psA = [PS[4], PS[5]]
            psK = PS[6]
            psAcc = PS[7]
            groups = [(b, g) for b in range(NS) for g in range(NGRP)]
            NG = len(groups)
            NNAT = len(nat)

            def load(i):
                b, g = groups[i]
                s_ = i % NSLOT
                pg = pg32[s_]
                dst = pg[:].bitcast(U8)

                def fn(e):
                    regs = [e.alloc_register("pg%d_%d_%d" % (l, i, j)) for j in range(GP)]
                    e.reg_load(regs, offs[b:b + 1, g * GP:(g + 1) * GP])
                    ins = []
                    for j in range(GP):
                        v = e.snap(regs[j], donate=True, min_val=0, max_val=(NPOOL - 1) * PAGE_BYTES)
                        ins.append(e.dma_start(out=dst[:, j, :],
                                               in_=cbytes[l][bass.ds(v, PAGE_BYTES)].rearrange("(p f) -> p f", p=128)))
                    for r in regs:
                        e.free_register(r)
                    return ins
                k.emit("sp", fn, [offs], [pg], dsem=pgsem[s_], ninc=GP)

            def stageT(i):
                pg = pg32[i % NSLOT]
                n_ = nat[i % NNAT]
                kp_ = krp[i % 2]
                k.cp("pool", kp_[:, 0:128].rearrange("p (g r) -> p g r", g=GP), pg[:, :, 128:160], [pg], [kp_])
                k.cp("pool", n_[:, :, 0:128], pg[:, :, 0:128], [pg], [n_])
                pt_ = psT[i % 2]
                for j in range(GP):
                    k.tr(pt_[:, j * 128:(j + 1) * 128], pg[:, j, 0:128], ident, [pg, cf], [pt_], inc=(j == GP - 1))
                k.tr(psK[:, 0:128], kp_[:, 0:128], ident, [kp_, cf], [psK])
                lt = latT[i % 2]
                k.cp("act", lt[:, 0:512], pt_[:, 0:512], [pt_], [lt])
                kb_ = krTb[i % 2]
                k.cp("dve", kb_[:, 0:128], psK[:, 0:128], [psK], [kb_])

            def stageK1(i):
                b, g = groups[i]
                lt = latT[i % 2]
                kb_ = krTb[i % 2]
                pa = psA[i % 2]
                for j in range(GP):
                    pk = psKR[j // 2]
                    k.mm(pk[:, (j % 2) * 256:(j % 2) * 256 + 256], lt[:, j * 128:(j + 1) * 128], wuk[:], [lt, wuk], [pk],
                         inc=(j % 2 == 1))
                    k.mm(pa[:, j * 16:(j + 1) * 16], lt[:, j * 128:(j + 1) * 128], qabs[:, b, :], [lt, qabs], [pa], inc=False)
                k.mm(pa[:, 64:128], kb_[:, 0:128], BD[:, b, :, :].rearrange("p g n -> p (g n)"), [kb_, BD], [pa])
                sq_ = ssq[i % 2]
                for hb in range(2):
                    sb_ = sqb[(2 * i + hb) % 4]
                    k.act(sb_[:, 0:512], psKR[hb][:, 0:512], AF.Square, [psKR[hb]], [sb_])
                    k.emit("dve", lambda e, sb_=sb_, sq_=sq_, hb=hb: e.tensor_reduce(
                        out=sq_[:, hb * 8:(hb + 1) * 8], in_=sb_[:, 0:512].rearrange("p (a d) -> p a d", d=64),
                        axis=AX.X, op=ALU.add), [sb_], [sq_])

            def stageK2a(i):
                sq_ = ssq[i % 2]
                pa = psA[i % 2]
                ri = rinv[i % 2]
                k.rsqrt(ri[:, 0:16], sq_[:, 0:16], 1.0 / 64, epsT[:, 0:1], [sq_, epsT], [ri])
                t1, t2 = stmp[i % 2], stmp2[i % 2]
                k.tt("dve", t1[:, 0:64].rearrange("p (a q) -> p a q", q=4), pa[:, 0:64].rearrange("p (a q) -> p a q", q=4),
                     bc(ri[:, 0:16].unsqueeze(2), [128, 16, 4]), ALU.mult, [pa, ri], [t1])
                k.tt("dve", t2[:, 0:64], t1[:, 0:64], pa[:, 64:128], ALU.add, [t1, pa], [t2])

            def stageK2b(i):
                t2 = stmp2[i % 2]
                pp = pTt[i % 3]
                k.act(pp[:, 0:64], t2[:, 0:64], AF.Exp, [t2], [pp], scale=SM_SCALE)

            def stageAcc(i):
                b, g = groups[i]
                n_ = nat[i % NNAT]
                pp = pTt[i % 3]
                if g == 0:
                    k.mm(psAcc[0:16, 0:129], pnb[b % 2][0:4, 0:16], natn[b % 2][0:4, 0:129], [pnb[b % 2], natn[b % 2]], [psAcc],
                         start=True, stop=False, skip=True, inc=False)
                for j in range(GP):
                    last = (g == NGRP - 1 and j == GP - 1)
                    k.mm(psAcc[0:16, 0:129], pp[:, j * 16:(j + 1) * 16], n_[:, j, 0:129], [pp, n_], [psAcc],
                         start=False, stop=last, skip=True, inc=(j == GP - 1))
                if g == NGRP - 1:
                    finish(b)

            def start_sample(b):
                cs = slice(b * TQ, (b + 1) * TQ)
                pn_, pnb_, natn_ = pn, pnb[b % 2], natn[b % 2]
                for h in range(4):
                    k.mm(psK[0:4, 128 + h * 4:128 + (h + 1) * 4], KsT[0:96, h, cs], Qb[0:96, h, cs], [KsT, Qb], [psK],
                         inc=(h == 3))
                k.act(pn_[0:4, 0:16], psK[0:4, 128:144], AF.Exp, [psK], [pn_], scale=SM_SCALE)
                k.tt("dve", pnb_[0:4, 0:16], pn_[0:4, 0:16], mask4, ALU.mult, [pn_, cf], [pnb_])
                k.tr(psK[0:4, 256:384], ckvT[:, cs], ident, [ckvT, cf], [psK])
                k.cp("dve", natn_[0:4, 0:128], psK[0:4, 256:384], [psK], [natn_])

            def finish(b):
                cs = slice(b * TQ, (b + 1) * TQ)
                k.recip(rls[0:16, 0:1], psAcc[0:16, 128:129], [psAcc], [rls])
                k.ts("dve", lo[0:16, 0:128], psAcc[0:16, 0:128], rls[0:16, 0:1], ALU.mult, [psAcc, rls], [lo])
                k.tr(psK[:, 384:400], lo[0:16, 0:128], cf[0:16, C_ID:C_ID + 16], [lo, cf], [psK])
                k.cp("dve", loT[:, 0:16], psK[:, 384:400], [psK], [loT])
                for h in range(4):
                    k.mm(psK[:, 400 + h * 4:400 + (h + 1) * 4], wuv[:, h * 128:(h + 1) * 128], loT[:, h * 4:(h + 1) * 4],
                         [wuv, loT], [psK], inc=(h == 3))
                k.tt("dve", mixed[:, 4:8, cs], psK[:, 400:416].rearrange("p (h q) -> p h q", h=4), szm[:, :, cs], ALU.mult,
                     [psK, szm], [mixed])

            PF = 3
            for i in range(min(PF, NG)):
                load(i)
            for i in range(-1, NG + 2):
                if PF <= i + PF < NG:
                    load(i + PF)
                if 0 <= i - 1 < NG:
                    stageK2a(i - 1)
                if 0 <= i + 1 < NG:
                    stageT(i + 1)
                if 0 <= i < NG:
                    if groups[i][1] == 0:
                        start_sample(groups[i][0])
                    stageK1(i)
                if 0 <= i - 1 < NG:
                    stageK2b(i - 1)
                if 0 <= i - 2 < NG:
                    stageAcc(i - 2)

        for l in range(DEPTH):
            load_weights(l)
            adaln(l)
            k.barrier()
            k.memset("pool", VX[:, :, :, 128:130], 1.0, [VX])
            for bi in range(NBLK):
                block_front(l, "P", bi)
                prompt_attn(l, bi)
                block_back(l, "P", bi)
            k.barrier()
            for n_ in nat:
                k.memset("pool", n_[:, :, 128:130], 1.0, [n_])
            for n_ in natn:
                k.memset("pool", n_[:, 128:130], 1.0, [n_])
            block_front(l, "S", 0)
            sample_attn(l)
            block_back(l, "S", 0)
        k.dma(D["yT"][:, :, :], xT[:], [xT], ())
        k.dma(D["ysT"][:, :, :], xsT[:], [xsT], ())
        for s in list(k.dsems) + list(pgsem) + list(k.swsems):
            if k.cnt[s] > 0:
                k.wait_tok("sp", (s, k.cnt[s]))

        with nc.Block() as block:
            @block.sync
            def _(e):
                k.replay("sp", e)

            @block.scalar
            def _(e):
                k.replay("act", e)

            @block.tensor
            def _(e):
                k.replay("pe", e)

            @block.vector
            def _(e):
                k.replay("dve", e)

            @block.gpsimd
            def _(e):
                k.replay("pool", e)
    return nc


def _fm(a):
    r, f = a.shape
    return np.ascontiguousarray(a.reshape(r, f // 128, 128).transpose(2, 1, 0))


def _consts(T, TQ, past_len):
    cf = np.zeros((128, NCF), np.float32)
    cf[:, C_ID:C_ID + 128] = np.eye(128, dtype=np.float32)
    cf[:, C_ONES:C_ONES + 128] = 1.0
    cf[0:64, C_B96:C_B96 + 64] = 1.0 / 64
    cf[64:96, C_B96 + 64:C_B96 + 96] = 1.0 / 32
    for i in range(16):
        cf[64 + 16 + i, C_ROT + 64 + i] = -1.0
        cf[64 + i, C_ROT + 64 + 16 + i] = 1.0
    wins = (2, 4, 8, 16)
    for c in range(2):
        for hf in range(2):
            w = wins[2 * c + hf]
            cf[hf * 64:(hf + 1) * 64, C_INVW + c] = 1.0 / w
            for t in range(16):
                cf[hf * 64:(hf + 1) * 64, C_INVC + c * 16 + t] = 1.0 / min(t + 1, w)
    kk = np.arange(128)[:, None]
    qq = np.arange(128)[None, :]
    cf[:, C_MASK:C_MASK + 128] = (kk <= qq).astype(np.float32)
    for kq in range(4):
        for h in range(4):
            for q in range(4):
                cf[kq, C_MASK4 + h * 4 + q] = 1.0 if kq <= q else 0.0
    for g in range(4):
        for r in range(32):
            cf[64 + r, C_SELQ + g * 32 + r] = 1.0
        cf[g * 32:(g + 1) * 32, C_BDM + g] = 1.0
    inv_freq = (1.0 / (ROPE_THETA ** (np.arange(0, ROPE, 2, dtype=np.float32) / np.float32(ROPE)))).astype(np.float32)

    def tab(pos):
        ang = pos.astype(np.float32)[:, None] * inv_freq[None, :]
        cos = np.concatenate([np.cos(ang), np.cos(ang)], -1).astype(np.float32)
        sin = np.concatenate([np.sin(ang), np.sin(ang)], -1).astype(np.float32)
        o = np.zeros((2, 96, len(pos)), np.float32)
        o[0, 0:64] = 1.0
        o[0, 64:96] = cos.T
        o[1, 64:96] = sin.T
        return o
    return cf, tab(np.arange(T)), tab(past_len + np.arange(TQ))


_NC_CACHE = {}


def kernel(x_prompt, x_sample, cache_latent, cache_krope, state_pool, state_conv, page_table,
           c_prompt, c_sample, norm_g, w_ada, b_ada, w_in, pool_w, pool_scale, conv_w,
           q_norm_g, w_uq, qn_g, qr_g, kv_norm_g, kr_g, w_uk, kn_g, w_uv, w_out, _ncores=None):
    f = lambda a: np.asarray(a, dtype=np.float32)
    x_prompt, x_sample, cache_latent, cache_krope = f(x_prompt), f(x_sample), f(cache_latent), f(cache_krope)
    B, T, _ = x_prompt.shape
    DB, TQ, _ = x_sample.shape
    ncores = _ncores or B
    NS = DB // ncores
    NPOOL = cache_latent.shape[1]
    NPG = page_table.shape[1]
    past_len = NPG * 128
    cfg = dict(T=T, NPG=NPG, NPOOL=NPOOL, NS=NS, TQ=TQ)
    key = tuple(sorted(cfg.items()))
    if key not in _NC_CACHE:
        _NC_CACHE[key] = build(cfg)
    nc = _NC_CACHE[key]

    cf, ropeP, ropeS = _consts(T, TQ, past_len)
    caches = [np.ascontiguousarray(np.concatenate([cache_latent[l], cache_krope[l]], axis=-1)).reshape(-1) for l in range(2)]
    wada = np.ascontiguousarray(f(w_ada).reshape(2, 8, 128, 24, 128).transpose(0, 3, 2, 1, 4))
    win = np.ascontiguousarray(f(w_in).reshape(2, 8, 128, D_IN).transpose(0, 2, 1, 3))
    wout = np.ascontiguousarray(f(w_out).reshape(2, 8, 128, 1024).transpose(0, 2, 1, 3))
    wuq = np.ascontiguousarray(f(w_uq).reshape(2, 2, 128, 384).transpose(0, 2, 1, 3))
    wuk = np.ascontiguousarray(f(w_uk))
    wuv = np.ascontiguousarray(f(w_uv))
    wukT = np.ascontiguousarray(f(w_uk).reshape(2, 128, 4, 64).transpose(0, 3, 2, 1))
    pw = np.zeros((2, 128, 2, 128), np.float32)
    pwf = f(pool_w)
    for c in range(2):
        pw[:, 0:64, c, 0:64] = pwf[:, 2 * c]
        pw[:, 64:128, c, 64:128] = pwf[:, 2 * c + 1]
    pvec = np.zeros((2, 128, NV), np.float32)
    pvec[:, :, 0:8] = f(norm_g).reshape(2, 8, 128).transpose(0, 2, 1)
    pvec[:, :, 8:32] = f(b_ada).reshape(2, 24, 128).transpose(0, 2, 1)
    pvec[:, :, 32:34] = f(pool_scale).reshape(2, 2, 128).transpose(0, 2, 1)
    cw = f(conv_w).reshape(2, 3, 2, 128)
    for t in range(3):
        for c in range(2):
            pvec[:, :, 34 + 2 * t + c] = cw[:, t, c]
    pvec[:, :, 40:42] = f(q_norm_g).reshape(2, 2, 128).transpose(0, 2, 1)
    pvec[:, :, 42] = f(kv_norm_g)
    pvec[:, 0:64, 43] = f(qn_g)
    pvec[:, 64:96, 43] = f(qr_g)
    pvec[:, 0:64, 44] = f(kn_g)
    pvec[:, 64:96, 44] = f(kr_g)
    sp = f(state_pool)
    sc = f(state_conv)
    pt = np.asarray(page_table, dtype=np.int32)
    cp_, cs_ = f(c_prompt), f(c_sample)
    in_maps = []
    for c in range(ncores):
        sl = slice(c * NS, (c + 1) * NS)
        m = {
            "xT": _fm(x_prompt[c]),
            "xsT": _fm(x_sample[sl].reshape(NS * TQ, D_MODEL)),
            "cT": _fm(np.concatenate([cp_[c:c + 1], cs_[sl]], 0)),
            "cache0": caches[0], "cache1": caches[1],
            "pt": np.ascontiguousarray(pt[sl]),
            "spT": np.ascontiguousarray(sp[:, sl].reshape(2, NS, 15, 2, 128).transpose(0, 4, 3, 1, 2)),
            "scT": np.ascontiguousarray(sc[:, sl].reshape(2, NS, 2, 2, 128).transpose(0, 4, 3, 1, 2)),
            "wada": wada, "win": win, "wout": wout, "wuq": wuq, "wuk": wuk, "wuv": wuv, "wukT": wukT, "pw": pw,
            "pvec": pvec, "cf": cf, "ropeP": ropeP, "ropeS": ropeS,
        }
        in_maps.append(m)
    res = run_bass_kernel_spmd(nc, in_maps, core_ids=list(range(ncores)))
    R = res.results

    def unfm(a):
        return np.ascontiguousarray(a.transpose(2, 1, 0).reshape(a.shape[2], -1))
    y_p = np.stack([unfm(R[c]["yT"]) for c in range(ncores)], 0)
    y_s = np.concatenate([unfm(R[c]["ysT"]).reshape(NS, TQ, D_MODEL) for c in range(ncores)], 0)
    lat_p = np.stack([R[c]["latP"].transpose(0, 2, 1) for c in range(ncores)], 1)
    kr_p = np.stack([R[c]["krP"].transpose(0, 2, 1) for c in range(ncores)], 1)
    pool_p = np.stack([R[c]["poolP"].transpose(0, 3, 2, 1).reshape(2, 15, 256) for c in range(ncores)], 1)
    conv_p = np.stack([R[c]["convP"].transpose(0, 3, 2, 1).reshape(2, 2, 256) for c in range(ncores)], 1)
    lat_s = np.concatenate([R[c]["latS"].transpose(0, 2, 1).reshape(2, NS, TQ, 128) for c in range(ncores)], 1)
    kr_s = np.concatenate([R[c]["krS"].transpose(0, 2, 1).reshape(2, NS, TQ, 32) for c in range(ncores)], 1)
    pool_s = np.concatenate([R[c]["poolS"].transpose(0, 3, 4, 2, 1).reshape(2, NS, 15, 256) for c in range(ncores)], 1)
    conv_s = np.concatenate([R[c]["convS"].transpose(0, 3, 4, 2, 1).reshape(2, NS, 2, 256) for c in range(ncores)], 1)
    outs = (y_p, y_s, lat_p, kr_p, pool_p, conv_p, lat_s, kr_s, pool_s, conv_s)
    return tuple(np.ascontiguousarray(o, dtype=np.float32) for o in outs)
```

```python
import math
from contextlib import ExitStack
import numpy as np
import concourse.bass as bass
import concourse.mybir as mybir
from concourse.bass_utils import run_bass_kernel_spmd

F32 = mybir.dt.float32
BF16 = mybir.dt.bfloat16
I32 = mybir.dt.int32
U8 = mybir.dt.uint8
AF = mybir.ActivationFunctionType
ALU = mybir.AluOpType
AX = mybir.AxisListType

D_MODEL = 1024
DEPTH = 2
D_IN = 2464
NOPE, ROPE = 64, 32
EPS = 1e-6
SM_SCALE = 1.0 / math.sqrt(NOPE + ROPE)
ROPE_THETA = 10000.0
PAGE_BYTES = 128 * 160 * 4
NV = 48
C_ID, C_ONES, C_B96, C_ROT, C_INVW, C_INVC, C_MASK, C_MASK4, C_SELQ, C_BDM = 0, 128, 256, 352, 448, 450, 482, 610, 626, 754
NCF = 758


class Buf:
    __slots__ = ("w", "r")

    def __init__(self):
        self.w = None
        self.r = {}


class TT:
    def __init__(self, ap, buf=None, excl=False):
        self.ap = ap
        self.b = buf or Buf()
        self.excl = excl

    def __getitem__(self, k):
        return self.ap[k]


class KB:
    ENG = ("sp", "act", "pe", "dve", "pool")

    def __init__(self, nc, es):
        self.nc = nc
        self.es = es
        self.q = {e: [] for e in self.ENG}
        self.waited = {e: {} for e in self.ENG}
        self.cnt = {}
        self.prog = {}
        for e in ("act", "pe", "dve", "pool"):
            s = es.enter_context(nc.semaphore("prog_" + e))
            self.prog[e] = s
            self.cnt[s] = 0
        self.dsems = []
        for i in range(20):
            s = es.enter_context(nc.semaphore("dq%d" % i))
            self.dsems.append(s)
            self.cnt[s] = 0
        self.dnext = 0
        self.swsems = []
        for i in range(8):
            s = es.enter_context(nc.semaphore("sw%d" % i))
            self.swsems.append(s)
            self.cnt[s] = 0
        self.swnext = 0
        self.nid = 0

    def newsem(self, name):
        s = self.es.enter_context(self.nc.semaphore(name))
        self.cnt[s] = 0
        return s

    def barrier(self):
        toks = [(s, c) for s, c in self.cnt.items() if c > 0]
        for e in self.ENG:
            for tok in toks:
                self.wait_tok(e, tok)

    def sb(self, shape, dt, name=None):
        self.nid += 1
        t = self.es.enter_context(self.nc.sbuf_tensor("sb_" + (name or ("t%d" % self.nid)), list(shape), dt))
        return TT(t)

    def ps(self, name):
        t = self.es.enter_context(self.nc.psum_tensor(name, [128, 512], F32))
        return TT(t, excl=True)

    def emit(self, eng, fn, reads=(), writes=(), dsem=None, ninc=1, noinc=False):
        deps = {}
        xr = [t for t in reads if t.excl]
        if xr:
            reads = [t for t in reads if not t.excl]
            writes = list(writes) + [t for t in xr if t not in writes]

        def add(tok):
            if tok is None:
                return
            s, v = tok
            if deps.get(s, 0) < v:
                deps[s] = v

        for t in reads:
            add(t.b.w)
        for t in writes:
            add(t.b.w)
            for s, v in t.b.r.items():
                add((s, v))
        if eng == "sp" or dsem is not None:
            if dsem is None:
                dsem = self.dsems[self.dnext]
                self.dnext = (self.dnext + 1) % len(self.dsems)
            add((dsem, self.cnt[dsem]))
            s = dsem
            inc = 16 * ninc
        else:
            s = self.prog[eng]
            inc = 1
        w = self.waited[eng]
        for ds, v in deps.items():
            if eng == "pe" and ds is self.prog["pe"]:
                continue
            if w.get(ds, 0) < v:
                self.q[eng].append(("wait", ds, v))
                w[ds] = v
        if noinc:
            tok = (s, self.cnt[s] + 1)
            self.q[eng].append(("op", fn, s, 0))
        else:
            self.cnt[s] += inc
            tok = (s, self.cnt[s])
            self.q[eng].append(("op", fn, s, inc))
        for t in writes:
            t.b.w = tok
            t.b.r = {}
        for t in reads:
            if t.b.r.get(s, 0) < tok[1]:
                t.b.r[s] = tok[1]
        return tok

    def wait_tok(self, eng, tok):
        s, v = tok
        w = self.waited[eng]
        if w.get(s, 0) < v:
            self.q[eng].append(("wait", s, v))
            w[s] = v

    def replay(self, eng, e):
        for it in self.q[eng]:
            if it[0] == "wait":
                e.wait_ge(it[1], it[2])
            else:
                _, fn, s, inc = it
                r = fn(e)
                if inc == 0:
                    continue
                if isinstance(r, list):
                    for x in r:
                        x.then_inc(s, inc // len(r))
                else:
                    r.then_inc(s, inc)

    def mm(self, out, lhsT, rhs, reads, writes, start=True, stop=True, skip=False, inc=None):
        if inc is None:
            inc = True
        return self.emit("pe", lambda e: e.matmul(out, lhsT=lhsT, rhs=rhs, start=start, stop=stop,
                                                  skip_group_check=skip), reads, writes, noinc=not inc)

    def tr(self, out, in_, ident, reads, writes, inc=True):
        return self.emit("pe", lambda e: e.transpose(out=out, in_=in_, identity=ident), reads, writes, noinc=not inc)

    def rsqrt(self, out, in_, scale, eps_ap, reads, writes):
        self.act(out, in_, AF.Ln, reads, writes, bias=eps_ap, scale=scale)
        self.act(out, out, AF.Exp, writes, writes, scale=-0.5)

    def act(self, out, in_, func, reads, writes, bias=None, scale=None):
        kw = {}
        if bias is not None:
            kw["bias"] = bias
        if scale is not None:
            kw["scale"] = scale
        return self.emit("act", lambda e: e.activation(out=out, in_=in_, func=func, **kw), reads, writes)

    def tt(self, eng, out, in0, in1, op, reads, writes):
        return self.emit(eng, lambda e: e.tensor_tensor(out=out, in0=in0, in1=in1, op=op), reads, writes)

    def ts(self, eng, out, in0, s1, op0, reads, writes, s2=None, op1=None):
        if op1 is None:
            return self.emit(eng, lambda e: e.tensor_scalar(out=out, in0=in0, scalar1=s1, scalar2=None, op0=op0),
                             reads, writes)
        return self.emit(eng, lambda e: e.tensor_scalar(out=out, in0=in0, scalar1=s1, scalar2=s2, op0=op0, op1=op1),
                         reads, writes)

    def stt(self, eng, out, in0, scalar, in1, op0, op1, reads, writes):
        return self.emit(eng, lambda e: e.scalar_tensor_tensor(out=out, in0=in0, scalar=scalar, in1=in1,
                                                                op0=op0, op1=op1), reads, writes)

    def cp(self, eng, out, in_, reads, writes):
        if eng == "act":
            return self.act(out, in_, AF.Copy, reads, writes)
        return self.emit(eng, lambda e: e.tensor_copy(out=out, in_=in_), reads, writes)

    def recip(self, out, in_, reads, writes):
        return self.emit("dve", lambda e: e.reciprocal(out=out, in_=in_), reads, writes)

    def memset(self, eng, out, val, writes):
        return self.emit(eng, lambda e: e.memset(out, val), (), writes)

    def dma(self, out, in_, reads, writes, eng="sp", **kw):
        if eng == "sp":
            return self.emit("sp", lambda e: e.dma_start(out=out, in_=in_, **kw), reads, writes)
        ds = self.swsems[self.swnext]
        self.swnext = (self.swnext + 1) % len(self.swsems)
        return self.emit("pool", lambda e: e.dma_start(out=out, in_=in_, **kw), reads, writes, dsem=ds)


def bc(ap, shape):
    return ap.broadcast_to(list(shape))


def build(cfg):
    T, NPG, NPOOL, NS, TQ = cfg["T"], cfg["NPG"], cfg["NPOOL"], cfg["NS"], cfg["TQ"]
    NTB = 128
    NBLK = T // NTB
    NSEQ = 1 + NS
    NST = NS * TQ
    GP = 4
    NGRP = NPG // GP
    nc = bass.Bass("TRN2", target_bir_lowering=False)
    D = {}

    def din(name, shape, dt=F32):
        D[name] = nc.dram_tensor(name, list(shape), dt, kind="ExternalInput").ap()

    def dout(name, shape, dt=F32):
        D[name] = nc.dram_tensor(name, list(shape), dt, kind="ExternalOutput").ap()

    din("xT", [128, 8, T]); din("xsT", [128, 8, NST]); din("cT", [128, 8, NSEQ])
    din("cache0", [NPOOL * 128 * 160]); din("cache1", [NPOOL * 128 * 160])
    din("pt", [NS, NPG], I32)
    din("spT", [2, 128, 2, NS, 15]); din("scT", [2, 128, 2, NS, 2])
    din("wada", [2, 24, 128, 8, 128]); din("win", [2, 128, 8, D_IN]); din("wout", [2, 128, 8, 1024])
    din("wuq", [2, 128, 2, 384]); din("wuk", [2, 128, 256]); din("wuv", [2, 128, 512])
    din("wukT", [2, 64, 4, 128]); din("pw", [2, 128, 2, 128]); din("pvec", [2, 128, NV])
    din("cf", [128, NCF]); din("ropeP", [2, 96, T]); din("ropeS", [2, 96, TQ])
    dout("yT", [128, 8, T]); dout("ysT", [128, 8, NST])
    dout("latP", [2, 128, T]); dout("krP", [2, 32, T]); dout("poolP", [2, 128, 2, 15]); dout("convP", [2, 128, 2, 2])
    dout("latS", [2, 128, NST]); dout("krS", [2, 32, NST]); dout("poolS", [2, 128, 2, NS, 15])
    dout("convS", [2, 128, 2, NS, 2])
    cbytes = [D["cache0"].bitcast(U8), D["cache1"].bitcast(U8)]

    es = ExitStack()
    with es:
        k = KB(nc, es)
        xT = k.sb([128, 8, T], F32, "xT")
        xsT = k.sb([128, 8, NST], F32, "xsT")
        win = k.sb([128, 8, D_IN], BF16, "win")
        wout = k.sb([128, 8, 1024], BF16, "wout")
        wuq = k.sb([128, 2, 416], BF16, "wuq")
        wuk = k.sb([128, 256], BF16, "wuk")
        wuv = k.sb([128, 512], BF16, "wuv")
        wukT = k.sb([64, 4, 128], BF16, "wukT")
        pw = k.sb([128, 2, 128], BF16, "pw")
        pvec = k.sb([128, NV], F32, "pvec")
        cf = k.sb([128, NCF], F32, "cf")
        cfb = k.sb([128, 256], BF16, "cfb")
        siluT = k.sb([128, 8, NSEQ], F32, "siluT")
        modT = k.sb([128, 24, NSEQ], F32, "modT")
        amod = k.sb([128, 8, NSEQ], F32, "amod")
        epsT = k.sb([128, 1], F32, "eps")
        pts = k.sb([NS, NPG], I32, "pts")
        offs = k.sb([NS, NPG], I32, "offs")
        wa = [k.sb([128, 8, 128], F32, "wa0")]
        ropeP = k.sb([96, 2, NTB], F32, "ropeP")
        ropeS = k.sb([96, 2, TQ], F32, "ropeS")
        hT = k.sb([128, 8, NTB], BF16, "hT")
        mixed = k.sb([128, 8, NTB], BF16, "mixed")
        SW = max(NS * (16 + TQ), 16 + NTB)
        WN = max(NTB, NST)
        WQ = max(NTB, 2 * NST)

        def slab(name, dt=F32, w=WN):
            return k.sb([128, w], dt, name)

        sq = [slab("sq%d" % i) for i in range(2)]
        rstd = slab("rstd")
        rstdP = [slab("rstdP0", w=NTB), slab("rstdP1", w=NTB)]
        pre_rstd = {}
        tmpA = slab("tmpA")
        U = k.sb([128, 2, SW], F32, "U")
        S2 = k.sb([128, 2, SW], F32, "S2")
        S4 = k.sb([128, 2, SW], F32, "S4")
        S8 = slab("S8", w=SW)
        S16 = slab("S16", w=SW)
        szp = k.sb([128, 2, WN], BF16, "szp")
        dT = k.sb([128, 2, WN], BF16, "dT")
        cgs = k.sb([128, 2, WN], F32, "cgs")
        Vc = k.sb([128, 2, SW], F32, "Vc")
        bgs = k.sb([128, 2, WN], F32, "bgs")
        cacc = k.sb([128, 2, WN], F32, "cacc")
        szc = k.sb([128, 2, WN], BF16, "szc")
        cqs = k.sb([128, 2, WN], F32, "cqs")
        cqn = k.sb([128, 2, WN], BF16, "cqn")
        ckr = slab("ckr")
        ckvT = slab("ckvT")
        ckvb = slab("ckvb", BF16)
        qraw = slab("qraw", w=WQ)
        qsq = slab("qsq", w=WQ)
        qr_ = slab("qr_", w=WQ)
        qn = slab("qn", w=WQ)
        qt1 = slab("qt1", w=WQ)
        qrawL = [qraw, slab("qrawB", w=NTB)]
        qsqL = [qsq, slab("qsqB", w=NTB)]
        qrL = [qr_, slab("qrB", w=NTB)]
        qnL = [qn, slab("qnB", w=NTB)]
        qt1L = [qt1, slab("qt1B", w=NTB)]
        Qb = k.sb([96, 4, WN], BF16, "Qb")
        krT = slab("krT")
        szm = k.sb([128, 4, WN], BF16, "szm")
        PT = [slab("PT%d" % i, BF16, w=512) for i in range(2)]
        On = [slab("On%d" % i, w=128) for i in range(2)]
        rl = [slab("rl%d" % i, w=2) for i in range(2)]
        NSLOT = 4
        samp_specs = [("KsT", [96, 4, NST], BF16), ("qp", [64, 4, NST], BF16), ("qabs", [128, NS, 16], BF16),
                      ("qr4s", [128, NS, 16], F32), ("BD", [128, NS, 4, 16], BF16)]
        samp_specs += [("pg%d" % i, [128, GP, 160], F32) for i in range(NSLOT)]
        samp_specs += [("nat%d" % i, [128, GP, 130], BF16) for i in range(4)]
        samp_specs += [("latT%d" % i, [128, 512], BF16) for i in range(2)]
        samp_specs += [("krTb%d" % i, [128, 128], BF16) for i in range(2)]
        samp_specs += [("krp%d" % i, [128, 128], F32) for i in range(2)]
        samp_specs += [("sqb%d" % i, [128, 512], BF16) for i in range(4)]
        samp_specs += [("ssq%d" % i, [128, 16], F32) for i in range(2)]
        samp_specs += [("rinv%d" % i, [128, 16], F32) for i in range(2)]
        samp_specs += [("stmp%d" % i, [128, 64], F32) for i in range(2)]
        samp_specs += [("stmp2%d" % i, [128, 64], F32) for i in range(2)]
        samp_specs += [("pTt%d" % i, [128, 64], BF16) for i in range(3)]
        samp_specs += [("pn", [128, 16], F32), ("pnb0", [128, 16], BF16), ("pnb1", [128, 16], BF16),
                       ("natn0", [128, 130], BF16), ("natn1", [128, 130], BF16),
                       ("lo", [128, 128], F32), ("loT", [128, 16], BF16), ("rls", [128, 2], F32)]

        def nbytes(shape, dt):
            n = 1
            for d_ in shape[1:]:
                n *= d_
            return ((n * (4 if dt == F32 else 2) + 31) // 32) * 32
        samp_need = sum(nbytes(sh, dt) for _, sh, dt in samp_specs)
        prompt_need = 4 * T * 2 + (T // 128) * 4 * 130 * 2
        ABYTES = max(samp_need, prompt_need)
        arena = es.enter_context(nc.sbuf_tensor("sb_arena", [128, ABYTES // 2], BF16))

        def aview(off_b, shape, dt):
            n = 1
            for d_ in shape[1:]:
                n *= d_
            nb = n * (4 if dt == F32 else 2)
            ap = arena[0:shape[0], off_b // 2:(off_b + nb) // 2]
            if dt == F32:
                ap = ap.bitcast(F32)
            if len(shape) == 3:
                ap = ap.rearrange("p (a b) -> p a b", a=shape[1])
            elif len(shape) == 4:
                ap = ap.rearrange("p (a b c) -> p a b c", a=shape[1], b=shape[2])
            return TT(ap)
        KT = aview(0, [96, 4, T], BF16)
        VX = aview(4 * T * 2, [128, T // 128, 4, 130], BF16)
        SV = {}
        off_ = 0
        for nm, sh, dt in samp_specs:
            SV[nm] = aview(off_, sh, dt)
            off_ += nbytes(sh, dt)
        KsT, qp, qabs, qr4s, BD = SV["KsT"], SV["qp"], SV["qabs"], SV["qr4s"], SV["BD"]
        pg32 = [SV["pg%d" % i] for i in range(NSLOT)]
        pgsem = [k.newsem("pgsem%d" % i) for i in range(NSLOT)]
        nat = [SV["nat%d" % i] for i in range(4)]
        latT = [SV["latT%d" % i] for i in range(2)]
        krTb = [SV["krTb%d" % i] for i in range(2)]
        krp = [SV["krp%d" % i] for i in range(2)]
        sqb = [SV["sqb%d" % i] for i in range(4)]
        ssq = [SV["ssq%d" % i] for i in range(2)]
        rinv = [SV["rinv%d" % i] for i in range(2)]
        stmp = [SV["stmp%d" % i] for i in range(2)]
        stmp2 = [SV["stmp2%d" % i] for i in range(2)]
        pTt = [SV["pTt%d" % i] for i in range(3)]
        pn, lo, loT, rls = SV["pn"], SV["lo"], SV["loT"], SV["rls"]
        pnb = [SV["pnb0"], SV["pnb1"]]
        natn = [SV["natn0"], SV["natn1"]]
        PS = [k.ps("ps%d" % i) for i in range(8)]
        psn = [0]

        def nps():
            p = PS[psn[0] % 8]
            psn[0] += 1
            return p

        ident = cf[:, C_ID:C_ID + 128]
        ones = cf[:, C_ONES:C_ONES + 128]
        B96 = cf[0:96, C_B96:C_B96 + 96]
        ROT = cf[0:96, C_ROT:C_ROT + 96]
        maskb = cfb[:, 0:128]
        mask4 = cf[0:4, C_MASK4:C_MASK4 + 16]
        SELQ = cfb[0:96, 128:256]
        BDM = cf[:, C_BDM:C_BDM + 4]

        k.dma(cf[:], D["cf"][:, :], (), [cf])
        k.dma(xT[:], D["xT"][:, :, :], (), [xT])
        k.dma(xsT[:], D["xsT"][:, :, :], (), [xsT])
        k.dma(siluT[:], D["cT"][:, :, :], (), [siluT])
        k.dma(pts[:], D["pt"][:, :], (), [pts])
        k.dma(ropeS[:], D["ropeS"].rearrange("c p t -> p c t"), (), [ropeS])
        k.cp("dve", cfb[:, 0:128], cf[:, C_MASK:C_MASK + 128], [cf], [cfb])
        k.cp("dve", cfb[:, 128:256], cf[:, C_SELQ:C_SELQ + 128], [cf], [cfb])
        k.memset("dve", epsT[:], EPS, [epsT])
        k.ts("dve", offs[:], pts[:], float(PAGE_BYTES), ALU.mult, [pts], [offs])
        k.act(siluT[:], siluT[:], AF.Silu, [siluT], [siluT])
        k.memset("pool", wuq[:, :, 384:416], 0.0, [wuq])
        def load_weights(l):
            def cast_dma(dst, src, t):
                k.dma(dst, src, (), [t], eng="pool", max_dma_last_dim=4096)
            for kk in range(8):
                cast_dma(win[:, kk, 0:1232], D["win"][l, :, kk, 0:1232], win)
                cast_dma(win[:, kk, 1232:D_IN], D["win"][l, :, kk, 1232:D_IN], win)
            for kk in range(8):
                cast_dma(wout[:, kk, :], D["wout"][l, :, kk, :], wout)
            cast_dma(wuq[:, :, 0:384], D["wuq"][l], wuq)
            cast_dma(wuk[:], D["wuk"][l], wuk)
            cast_dma(wuv[:], D["wuv"][l], wuv)
            cast_dma(wukT[:], D["wukT"][l], wukT)
            cast_dma(pw[:], D["pw"][l], pw)
            k.dma(pvec[:], D["pvec"][l], (), [pvec])

        def adaln(l):
            pm = nps()
            for j in range(24):
                w_ = wa[0]
                k.dma(w_[:], D["wada"][l, j], (), [w_])
                for kk in range(8):
                    k.mm(pm[:, j * NSEQ:(j + 1) * NSEQ], w_[:, kk, :], siluT[:, kk, :], [w_, siluT], [pm],
                         start=(kk == 0), stop=(kk == 7), inc=(kk == 7))
            k.tt("dve", modT[:], pm[:, 0:24 * NSEQ].rearrange("p (j s) -> p j s", s=NSEQ),
                 bc(pvec[:, 8:32].unsqueeze(2), [128, 24, NSEQ]), ALU.add, [pm, pvec], [modT])
            k.ts("dve", amod[:], modT[:, 8:16, :], 1.0, ALU.add, [modT], [amod])
            k.tt("dve", amod[:], amod[:], bc(pvec[:, 0:8].unsqueeze(2), [128, 8, NSEQ]), ALU.mult, [amod, pvec], [amod])

        def rms_stats(srcs, scale, NT, rd):
            pst = nps()
            n = len(srcs)
            for i, (ap, t) in enumerate(srcs):
                s_ = sq[i % 2]
                k.act(s_[:, 0:NT], ap, AF.Square, [t], [s_])
                k.mm(pst[:, 0:NT], ones, s_[:, 0:NT], [cf, s_], [pst], start=(i == 0), stop=(i == n - 1))
            k.rsqrt(rd[:, 0:NT], pst[:, 0:NT], scale, epsT[:, 0:1], [pst, epsT], [rd])

        def norm_rope(src_ps, P_, W, gcol, cos_ap, sin_ap, out_ap, out_t, NTl, extra_reads=(), nh=1):
            def hv(ap):
                return ap if nh == 1 else ap.rearrange("p (h n) -> p h n", h=nh)
            k.cp("dve", qraw[0:P_, 0:W], src_ps[0:P_, 0:W], [src_ps], [qraw])
            k.act(qsq[0:P_, 0:W], src_ps[0:P_, 0:W], AF.Square, [src_ps], [qsq])
            p2 = nps()
            k.mm(p2[0:P_, 0:W], B96[0:P_, 0:P_], qsq[0:P_, 0:W], [cf, qsq], [p2])
            k.rsqrt(qr_[0:P_, 0:W], p2[0:P_, 0:W], 1.0, epsT[0:P_, 0:1], [p2, epsT], [qr_])
            k.stt("dve", qn[0:P_, 0:W], qraw[0:P_, 0:W], pvec[0:P_, gcol:gcol + 1], qr_[0:P_, 0:W], ALU.mult, ALU.mult,
                  [qraw, pvec, qr_], [qn])
            if cos_ap is None:
                k.cp("dve", out_ap, hv(qn[0:P_, 0:W]), [qn], [out_t])
                return
            p3 = nps()
            k.mm(p3[0:P_, 0:W], ROT[0:P_, 0:P_], qn[0:P_, 0:W], [cf, qn], [p3])
            k.tt("dve", hv(qt1[0:P_, 0:W]), hv(qn[0:P_, 0:W]), cos_ap, ALU.mult, [qn] + list(extra_reads), [qt1])
            k.tt("dve", hv(qn[0:P_, 0:W]), hv(p3[0:P_, 0:W]), sin_ap, ALU.mult, [p3] + list(extra_reads), [qn])
            k.tt("dve", out_ap, hv(qt1[0:P_, 0:W]), hv(qn[0:P_, 0:W]), ALU.add, [qt1, qn], [out_t])

        def norm_rope_pipe(calls, W, cos_ap, sin_ap, rt):
            n = len(calls)
            st = [dict() for _ in calls]

            def sA(c):
                st[c]["src"] = calls[c]["src"]()

            def sB(c):
                s_, P_, src = c % 2, calls[c]["P"], st[c]["src"]
                k.cp("dve", qrawL[s_][0:P_, 0:W], src[0:P_, 0:W], [src], [qrawL[s_]])
                k.act(qsqL[s_][0:P_, 0:W], src[0:P_, 0:W], AF.Square, [src], [qsqL[s_]])
                p2 = nps()
                st[c]["p2"] = p2
                k.mm(p2[0:P_, 0:W], B96[0:P_, 0:P_], qsqL[s_][0:P_, 0:W], [cf, qsqL[s_]], [p2])

            def sC(c):
                s_, P_, p2, g = c % 2, calls[c]["P"], st[c]["p2"], calls[c]["g"]
                k.rsqrt(qrL[s_][0:P_, 0:W], p2[0:P_, 0:W], 1.0, epsT[0:P_, 0:1], [p2, epsT], [qrL[s_]])
                k.stt("dve", qnL[s_][0:P_, 0:W], qrawL[s_][0:P_, 0:W], pvec[0:P_, g:g + 1], qrL[s_][0:P_, 0:W], ALU.mult, ALU.mult,
                      [qrawL[s_], pvec, qrL[s_]], [qnL[s_]])
                if calls[c]["rope"]:
                    p3 = nps()
                    st[c]["p3"] = p3
                    k.mm(p3[0:P_, 0:W], ROT[0:P_, 0:P_], qnL[s_][0:P_, 0:W], [cf, qnL[s_]], [p3])
                else:
                    k.cp("dve", calls[c]["out"], qnL[s_][0:P_, 0:W], [qnL[s_]], [calls[c]["ot"]])

            def sD(c):
                if not calls[c]["rope"]:
                    return
                s_, P_, p3 = c % 2, calls[c]["P"], st[c]["p3"]
                k.tt("dve", qt1L[s_][0:P_, 0:W], qnL[s_][0:P_, 0:W], cos_ap, ALU.mult, [qnL[s_], rt], [qt1L[s_]])
                k.tt("dve", qnL[s_][0:P_, 0:W], p3[0:P_, 0:W], sin_ap, ALU.mult, [p3, rt], [qnL[s_]])
                k.tt("dve", calls[c]["out"], qt1L[s_][0:P_, 0:W], qnL[s_][0:P_, 0:W], ALU.add, [qt1L[s_], qnL[s_]], [calls[c]["ot"]])

            for t in range(n + 3):
                if t < n:
                    sA(t)
                if 0 <= t - 1 < n:
                    sB(t - 1)
                if 0 <= t - 2 < n:
                    sC(t - 2)
                if 0 <= t - 3 < n:
                    sD(t - 3)

        def block_front(l, grp, bi):
            if grp == "P":
                nseq, Tq, NT = 1, NTB, NTB
                xv = xT[:, :, bi * NTB:(bi + 1) * NTB]
                xt = xT
                c0 = bi * NTB
            else:
                nseq, Tq, NT = NS, TQ, NST
                xv = xsT[:, :, :]
                xt = xsT
                c0 = 0
            L = 16 + Tq

            def v3(ap):
                return ap.rearrange("p (s t) -> p s t", t=Tq)

            def ext(tile_ap):
                return tile_ap[:, 0:nseq * L].rearrange("p (s l) -> p s l", l=L)

            for tl in (U, Vc):
                for c in range(2):
                    e_ = ext(tl[:, c, :])
                    if grp == "P":
                        if bi == 0:
                            k.memset("pool", e_[:, :, 0:16], 0.0, [tl])
                        else:
                            k.cp("pool", e_[:, :, 0:16], e_[:, :, Tq:Tq + 16], [tl], [tl])
            if grp == "S":
                for c in range(2):
                    k.memset("pool", ext(U[:, c, :])[:, :, 0:1], 0.0, [U])
                    k.dma(ext(U[:, c, :])[:, :, 1:16], D["spT"][l, :, c, :, :], (), [U])
                    k.dma(ext(Vc[:, c, :])[:, :, 14:16], D["scT"][l, :, c, :, :], (), [Vc])
            if grp == "P":
                k.dma(ropeP[:], D["ropeP"][:, :, c0:c0 + NTB].rearrange("c p t -> p c t"), (), [ropeP])
            if grp == "P":
                rs_ = rstdP[bi % 2]
                if (l, bi) not in pre_rstd:
                    rms_stats([(xv[:, kk, :], xt) for kk in range(8)], 1.0 / D_MODEL, NT, rs_)
            else:
                rs_ = rstd
                rms_stats([(xv[:, kk, :], xt) for kk in range(8)], 1.0 / D_MODEL, NT, rs_)
            for kk in range(8):
                if grp == "P":
                    k.stt("dve", tmpA[:, 0:NT], xv[:, kk, :], amod[:, kk, 0:1], rs_[:, 0:NT], ALU.mult, ALU.mult,
                          [xt, amod, rs_], [tmpA])
                    k.act(hT[:, kk, 0:NT], tmpA[:, 0:NT], AF.Identity, [tmpA, modT], [hT], bias=modT[:, kk, 0:1], scale=1.0)
                else:
                    k.tt("dve", tmpA[:, 0:NT], xv[:, kk, :], rstd[:, 0:NT], ALU.mult, [xt, rstd], [tmpA])
                    k.tt("dve", v3(tmpA[:, 0:NT]), v3(tmpA[:, 0:NT]), bc(amod[:, kk, 1:NSEQ].unsqueeze(2), [128, NS, TQ]),
                         ALU.mult, [tmpA, amod], [tmpA])
                    k.tt("dve", v3(hT[:, kk, 0:NT]), v3(tmpA[:, 0:NT]), bc(modT[:, kk, 1:NSEQ].unsqueeze(2), [128, NS, TQ]),
                         ALU.add, [tmpA, modT], [hT])

            def proj(col, M):
                p = nps()
                for kk in range(8):
                    k.mm(p[0:M, 0:NT], win[:, kk, col:col + M], hT[:, kk, 0:NT], [win, hT], [p], start=(kk == 0), stop=(kk == 7),
                         inc=(kk == 7))
                return p

            for c in range(2):
                p = proj(256 + c * 128, 128)
                k.act(szp[:, c, 0:NT], p[:, 0:NT], AF.Silu, [p], [szp])
            for c in range(2):
                p = proj(1280 + c * 128, 128)
                k.act(szc[:, c, 0:NT], p[:, 0:NT], AF.Silu, [p], [szc])
            for c in range(4):
                p = proj(1952 + c * 128, 128)
                k.act(szm[:, c, 0:NT], p[:, 0:NT], AF.Silu, [p], [szm])
            for c in range(2):
                p = proj(c * 128, 128)
                k.cp("act", ext(U[:, c, :])[:, :, 16:L], v3(p[:, 0:NT]), [p], [U])
            for c in range(2):
                p = proj(1024 + c * 128, 128)
                k.cp("act", cgs[:, c, 0:NT], p[:, 0:NT], [p], [cgs])
            for c in range(2):
                p = proj(512 + c * 128, 128)
                k.tt("dve", ext(Vc[:, c, :])[:, :, 16:L], v3(p[:, 0:NT]), v3(cgs[:, c, 0:NT]), ALU.mult, [p, cgs], [Vc])
            for c in range(2):
                p = proj(768 + c * 128, 128)
                k.cp("act", bgs[:, c, 0:NT], p[:, 0:NT], [p], [bgs])
            for c in range(2):
                p = proj(1536 + c * 128, 128)
                k.cp("act", cqs[:, c, 0:NT], p[:, 0:NT], [p], [cqs])
            p = proj(1792, 128)
            k.cp("act", ckr[:, 0:NT], p[:, 0:NT], [p], [ckr])
            pkr = proj(1856, 128)
            if grp == "P":
                cos1, sin1 = ropeP[:, 0, 0:NT], ropeP[:, 1, 0:NT]
                cos2 = bc(ropeP[:, 0:1, 0:NT], [96, 2, NT])
                sin2 = bc(ropeP[:, 1:2, 0:NT], [96, 2, NT])
                rt = ropeP
            else:
                cos1 = bc(ropeS[:, 0:1, :], [96, NS, TQ])
                sin1 = bc(ropeS[:, 1:2, :], [96, NS, TQ])
                cos2 = bc(ropeS[:, 0:1, :], [96, 2 * NS, TQ])
                sin2 = bc(ropeS[:, 1:2, :], [96, 2 * NS, TQ])
                rt = ropeS
            Kt = KT if grp == "P" else KsT
            rms_stats([(ckr[:, 0:NT], ckr)], 1.0 / 128, NT, rstd)
            k.stt("dve", ckvT[:, 0:NT], ckr[:, 0:NT], pvec[:, 42:43], rstd[:, 0:NT], ALU.mult, ALU.mult, [ckr, pvec, rstd], [ckvT])
            k.cp("dve", ckvb[:, 0:NT], ckvT[:, 0:NT], [ckvT], [ckvb])
            if grp == "P":
                k.dma(D["latP"][l, :, c0:c0 + NT], ckvT[:, 0:NT], [ckvT], ())
            else:
                k.dma(D["latS"][l, :, :], ckvT[:, 0:NT], [ckvT], ())
            rms_stats([(cqs[:, c, 0:NT], cqs) for c in range(2)], 1.0 / 256, NT, rstd)
            for c in range(2):
                k.stt("dve", cqn[:, c, 0:NT], cqs[:, c, 0:NT], pvec[:, 40 + c:41 + c], rstd[:, 0:NT], ALU.mult, ALU.mult,
                      [cqs, pvec, rstd], [cqn])
            if grp == "P":
                calls = [dict(src=(lambda: pkr), P=96, g=44, rope=True, out=krT[0:96, 0:NT], ot=krT)]
                for h in range(4):
                    def srck(h=h):
                        p = nps()
                        k.mm(p[0:64, 0:NT], wuk[:, h * 64:(h + 1) * 64], ckvb[:, 0:NT], [wuk, ckvb], [p])
                        return p
                    calls.append(dict(src=srck, P=64, g=44, rope=False, out=Kt[0:64, h, c0:c0 + NT], ot=Kt))
                for h in range(4):
                    def srcq(h=h):
                        p = nps()
                        for kk in range(2):
                            k.mm(p[0:128, 0:NT], wuq[:, kk, h * 96:h * 96 + 128], cqn[:, kk, 0:NT], [wuq, cqn], [p],
                                 start=(kk == 0), stop=(kk == 1), inc=(kk == 1))
                        return p
                    calls.append(dict(src=srcq, P=96, g=43, rope=True, out=Qb[0:96, h, 0:NT], ot=Qb))
                norm_rope_pipe(calls, NT, cos1, sin1, rt)
            else:
                norm_rope_s(pkr, 1, 44, cos1, sin1, krT[0:96, 0:NT], krT, rt)
                for hp in range(2):
                    p = nps()
                    for hh in range(2):
                        h = 2 * hp + hh
                        for kk in range(2):
                            k.mm(p[0:128, hh * NT:(hh + 1) * NT], wuq[:, kk, h * 96:h * 96 + 128], cqn[:, kk, 0:NT], [wuq, cqn], [p],
                                 start=(kk == 0), stop=(kk == 1))
                    norm_rope_s(p, 2, 43, cos2, sin2, Qb[0:96, 2 * hp:2 * hp + 2, 0:NT], Qb, rt)
                for hp in range(2):
                    p = nps()
                    for hh in range(2):
                        h = 2 * hp + hh
                        k.mm(p[0:64, hh * NT:(hh + 1) * NT], wuk[:, h * 64:(h + 1) * 64], ckvb[:, 0:NT], [wuk, ckvb], [p])
                    oap = Kt[0:64, 2 * hp:2 * hp + 2, c0:c0 + NT]
                    norm_rope(p, 64, 2 * NT, 44, None, None, oap, Kt, NT, nh=2)
            for h in range(4):
                k.cp("dve", Kt[64:96, h, c0:c0 + NT], krT[64:96, 0:NT], [krT], [Kt])
            if grp == "P":
                k.dma(D["krP"][l, :, c0:c0 + NT], krT[64:96, 0:NT], [krT], ())
            else:
                k.dma(D["krS"][l, :, :], krT[64:96, 0:NT], [krT], ())
            if grp == "P":
                for j in range(NT // 128):
                    p = nps()
                    k.mm(p[:, 0:512], ckvb[:, j * 128:(j + 1) * 128], wuv[:], [ckvb, wuv], [p])
                    tj = (c0 // 128) + j
                    k.cp("act", VX[:, tj, :, 0:128], p[:, 0:512].rearrange("p (h v) -> p h v", h=4), [p], [VX])

            if grp == "P" and bi + 1 < NBLK:
                xn = xT[:, :, (bi + 1) * NTB:(bi + 2) * NTB]
                rms_stats([(xn[:, kk, :], xT) for kk in range(8)], 1.0 / D_MODEL, NTB, rstdP[(bi + 1) % 2])
                pre_rstd[(l, bi + 1)] = True
            for c in range(2):
                u_, s2_, s4_ = ext(U[:, c, :]), ext(S2[:, c, :]), ext(S4[:, c, :])
                k.tt("pool", s2_[:, :, 1:L], u_[:, :, 1:L], u_[:, :, 0:L - 1], ALU.add, [U], [S2])
                k.tt("pool", s4_[:, :, 3:L], s2_[:, :, 3:L], s2_[:, :, 1:L - 2], ALU.add, [S2], [S4])
            s4_, s8_, s16_ = ext(S4[:, 1, :]), ext(S8[:]), ext(S16[:])
            k.tt("pool", s8_[:, :, 7:L], s4_[:, :, 7:L], s4_[:, :, 3:L - 4], ALU.add, [S4], [S8])
            k.tt("pool", s16_[:, :, 15:L], s8_[:, :, 15:L], s8_[:, :, 7:L - 8], ALU.add, [S8], [S16])
            srcs = {(0, 0): (S2[:, 0, :], S2), (1, 0): (S4[:, 0, :], S4), (0, 1): (S8[:], S8), (1, 1): (S16[:], S16)}
            for c in range(2):
                for hf in range(2):
                    pr = slice(hf * 64, (hf + 1) * 64)
                    sap, st = srcs[(hf, c)]
                    s_ = ext(sap)[pr, :, 16:L]
                    k.stt("dve", v3(dT[pr, c, 0:NT]), s_, cf[pr, C_INVW + c:C_INVW + c + 1], ext(U[:, c, :])[pr, :, 16:L],
                          ALU.mult, ALU.subtract, [st, cf, U], [dT])
                    if grp == "P" and bi == 0:
                        k.tt("dve", tmpA[pr, 0:16], sap[pr, 16:32], cf[pr, C_INVC + c * 16:C_INVC + c * 16 + 16], ALU.mult,
                             [st, cf], [tmpA])
                        k.tt("dve", dT[pr, c, 0:16], tmpA[pr, 0:16], U[pr, c, 16:32], ALU.subtract, [tmpA, U], [dT])
            for c in range(2):
                p = nps()
                k.mm(p[:, 0:NT], pw[:, c, :], dT[:, c, 0:NT], [pw, dT], [p])
                k.stt("dve", mixed[:, c, 0:NT], p[:, 0:NT], pvec[:, 32 + c:33 + c], szp[:, c, 0:NT], ALU.mult, ALU.mult,
                      [p, pvec, szp], [mixed])
            if grp == "P":
                if bi == NBLK - 1:
                    for c in range(2):
                        k.dma(D["poolP"][l, :, c, :], U[:, c, 16 + Tq - 15:16 + Tq], [U], ())
            else:
                for c in range(2):
                    k.dma(D["poolS"][l, :, c, :, :], ext(U[:, c, :])[:, :, L - 15:L], [U], ())

            for c in range(2):
                v_ = ext(Vc[:, c, :])
                a_ = v3(cacc[:, c, 0:NT])
                k.ts("pool", a_, v_[:, :, 16:L], pvec[:, 34 + 4 + c:35 + 4 + c], ALU.mult, [Vc, pvec], [cacc])
                k.stt("dve", a_, v_[:, :, 15:L - 1], pvec[:, 34 + 2 + c:35 + 2 + c], a_, ALU.mult, ALU.add, [Vc, pvec, cacc], [cacc])
                k.stt("dve", a_, v_[:, :, 14:L - 2], pvec[:, 34 + c:35 + c], a_, ALU.mult, ALU.add, [Vc, pvec, cacc], [cacc])
                k.tt("pool", cacc[:, c, 0:NT], cacc[:, c, 0:NT], bgs[:, c, 0:NT], ALU.mult, [cacc, bgs], [cacc])
                k.tt("pool", mixed[:, 2 + c, 0:NT], cacc[:, c, 0:NT], szc[:, c, 0:NT], ALU.mult, [cacc, szc], [mixed])
            if grp == "P":
                if bi == NBLK - 1:
                    for c in range(2):
                        k.dma(D["convP"][l, :, c, :], Vc[:, c, 16 + Tq - 2:16 + Tq], [Vc], ())
            else:
                for c in range(2):
                    k.dma(D["convS"][l, :, c, :, :], ext(Vc[:, c, :])[:, :, L - 2:L], [Vc], ())


        def norm_rope_s(src_ps, nh, gcol, cos_ap, sin_ap, out_ap, out_t, rt):
            W = nh * NST
            P_ = 96
            k.cp("dve", qraw[0:P_, 0:W], src_ps[0:P_, 0:W], [src_ps], [qraw])
            k.act(qsq[0:P_, 0:W], src_ps[0:P_, 0:W], AF.Square, [src_ps], [qsq])
            p2 = nps()
            k.mm(p2[0:P_, 0:W], B96, qsq[0:P_, 0:W], [cf, qsq], [p2])
            k.rsqrt(qr_[0:P_, 0:W], p2[0:P_, 0:W], 1.0, epsT[0:P_, 0:1], [p2, epsT], [qr_])
            k.stt("dve", qn[0:P_, 0:W], qraw[0:P_, 0:W], pvec[0:P_, gcol:gcol + 1], qr_[0:P_, 0:W], ALU.mult, ALU.mult,
                  [qraw, pvec, qr_], [qn])
            p3 = nps()
            k.mm(p3[0:P_, 0:W], ROT, qn[0:P_, 0:W], [cf, qn], [p3])

            def v3(ap):
                return ap.rearrange("p (s t) -> p s t", t=TQ)
            k.tt("dve", v3(qt1[0:P_, 0:W]), v3(qn[0:P_, 0:W]), cos_ap, ALU.mult, [qn, rt], [qt1])
            k.tt("dve", v3(qn[0:P_, 0:W]), v3(p3[0:P_, 0:W]), sin_ap, ALU.mult, [p3, rt], [qn])
            if nh == 1:
                k.tt("dve", out_ap, qt1[0:P_, 0:W], qn[0:P_, 0:W], ALU.add, [qt1, qn], [out_t])
            else:
                k.tt("dve", out_ap, qt1[0:P_, 0:W].rearrange("p (h n) -> p h n", h=nh),
                     qn[0:P_, 0:W].rearrange("p (h n) -> p h n", h=nh), ALU.add, [qt1, qn], [out_t])

        def block_back(l, grp, bi):
            if grp == "P":
                NT = NTB
                xv = xT[:, :, bi * NTB:(bi + 1) * NTB]
                xt = xT
            else:
                NT = NST
                xv = xsT[:, :, :]
                xt = xsT
            for j in range(8):
                p = nps()
                for kk in range(8):
                    k.mm(p[:, 0:NT], wout[:, kk, j * 128:(j + 1) * 128], mixed[:, kk, 0:NT], [wout, mixed], [p],
                         start=(kk == 0), stop=(kk == 7), inc=(kk == 7))
                if grp == "P":
                    k.stt("dve", xv[:, j, :], p[:, 0:NT], modT[:, 16 + j, 0:1], xv[:, j, :], ALU.mult, ALU.add,
                          [p, modT, xt], [xt])
                else:
                    k.tt("dve", tmpA[:, 0:NT].rearrange("p (s t) -> p s t", t=TQ), p[:, 0:NT].rearrange("p (s t) -> p s t", t=TQ),
                         bc(modT[:, 16 + j, 1:NSEQ].unsqueeze(2), [128, NS, TQ]), ALU.mult, [p, modT], [tmpA])
                    k.tt("dve", xv[:, j, :], xv[:, j, :], tmpA[:, 0:NT], ALU.add, [xt, tmpA], [xt])

        def prompt_attn(l, bi):
            NT = NTB
            assert NT == 128
            qt = bi
            nkt = qt + 1
            batches = [list(range(s0, min(s0 + 4, nkt))) for s0 in range(0, nkt, 4)]
            nb = len(batches)

            def S_stage(h, bidx):
                kts = batches[bidx]
                pss = PS[4 + bidx % 2]
                pt_ = PT[bidx % 2]
                for i, kt in enumerate(kts):
                    k.mm(pss[:, i * 128:(i + 1) * 128], KT[0:96, h, kt * 128:(kt + 1) * 128], Qb[0:96, h, 0:NT], [KT, Qb], [pss],
                         inc=(i == len(kts) - 1))
                w = len(kts) * 128
                k.act(pt_[:, 0:w], pss[:, 0:w], AF.Exp, [pss], [pt_], scale=SM_SCALE)
                if kts[-1] == qt:
                    i = len(kts) - 1
                    k.tt("dve", pt_[:, i * 128:(i + 1) * 128], pt_[:, i * 128:(i + 1) * 128], maskb, ALU.mult, [pt_, cfb], [pt_])

            def PV_stage(h, bidx):
                kts = batches[bidx]
                pt_ = PT[bidx % 2]
                pso = PS[h % 2]
                for i, kt in enumerate(kts):
                    k.mm(pso[:, 0:129], pt_[:, i * 128:(i + 1) * 128], VX[:, kt, h, 0:129], [pt_, VX], [pso],
                         start=(kt == 0), stop=(kt == qt), inc=(i == len(kts) - 1))

            def epilogue(h):
                pso = PS[h % 2]
                r_ = rl[h % 2]
                o_ = On[h % 2]
                k.recip(r_[:, 0:1], pso[:, 128:129], [pso], [r_])
                k.ts("dve", o_[:, 0:128], pso[:, 0:128], r_[:, 0:1], ALU.mult, [pso, r_], [o_])
                ptr = PS[6 + h % 2]
                k.tr(ptr[:, 0:128], o_[:, 0:128], ident, [o_, cf], [ptr])
                k.tt("dve", mixed[:, 4 + h, 0:128], ptr[:, 0:128], szm[:, h, 0:128], ALU.mult, [ptr, szm], [mixed])

            for h in range(4):
                S_stage(h, 0)
                if h > 0:
                    epilogue(h - 1)
                for b_ in range(nb):
                    if b_ + 1 < nb:
                        S_stage(h, b_ + 1)
                    PV_stage(h, b_)
            epilogue(3)

        def sample_attn(l):
            k.ts("dve", qp[:, :, :], Qb[0:64, :, 0:NST], pvec[0:64, 44:45], ALU.mult, [Qb, pvec], [qp])
            p = nps()
            for h in range(4):
                k.mm(p[:, h * NST:(h + 1) * NST], wukT[:, h, :], qp[:, h, :], [wukT, qp], [p])
            k.cp("dve", qabs[:].rearrange("p b (h q) -> p h b q", h=4),
                 p[:, 0:4 * NST].rearrange("p (h b q) -> p h b q", h=4, b=NS), [p], [qabs])
            p = nps()
            for h in range(4):
                k.mm(p[:, h * NST:(h + 1) * NST], SELQ, Qb[0:96, h, 0:NST], [cfb, Qb], [p])
            k.cp("dve", qr4s[:].rearrange("p b (h q) -> p h b q", h=4),
                 p[:, 0:4 * NST].rearrange("p (h b q) -> p h b q", h=4, b=NS), [p], [qr4s])
            k.tt("dve", BD[:], bc(qr4s[:].unsqueeze(2), [128, NS, 4, 16]),
                 bc(BDM.unsqueeze(1).unsqueeze(3), [128, NS, 4, 16]), ALU.mult, [qr4s, cf], [BD])

            psT = [PS[0], PS[1]]
            psKR = [PS[2], PS[3]]
            psA = [PS[4], PS[5]]
            psK = PS[6]
            psAcc = PS[7]
            groups = [(b, g) for b in range(NS) for g in range(NGRP)]
            NG = len(groups)
            NNAT = len(nat)

            def load(i):
                b, g = groups[i]
                s_ = i % NSLOT
                pg = pg32[s_]
                dst = pg[:].bitcast(U8)

                def fn(e):
                    regs = [e.alloc_register("pg%d_%d_%d" % (l, i, j)) for j in range(GP)]
                    e.reg_load(regs, offs[b:b + 1, g * GP:(g + 1) * GP])
                    ins = []
                    for j in range(GP):
                        v = e.snap(regs[j], donate=True, min_val=0, max_val=(NPOOL - 1) * PAGE_BYTES)
                        ins.append(e.dma_start(out=dst[:, j, :],
                                               in_=cbytes[l][bass.ds(v, PAGE_BYTES)].rearrange("(p f) -> p f", p=128)))
                    for r in regs:
                        e.free_register(r)
                    return ins
                k.emit("sp", fn, [offs], [pg], dsem=pgsem[s_], ninc=GP)

            def stageT(i):
                pg = pg32[i % NSLOT]
                n_ = nat[i % NNAT]
                kp_ = krp[i % 2]
                k.cp("pool", kp_[:, 0:128].rearrange("p (g r) -> p g r", g=GP), pg[:, :, 128:160], [pg], [kp_])
                k.cp("pool", n_[:, :, 0:128], pg[:, :, 0:128], [pg], [n_])
                pt_ = psT[i % 2]
                for j in range(GP):
                    k.tr(pt_[:, j * 128:(j + 1) * 128], pg[:, j, 0:128], ident, [pg, cf], [pt_], inc=(j == GP - 1))
                k.tr(psK[:, 0:128], kp_[:, 0:128], ident, [kp_, cf], [psK])
                lt = latT[i % 2]
                k.cp("act", lt[:, 0:512], pt_[:, 0:512], [pt_], [lt])
                kb_ = krTb[i % 2]
                k.cp("dve", kb_[:, 0:128], psK[:, 0:128], [psK], [kb_])

            def stageK1(i):
                b, g = groups[i]
                lt = latT[i % 2]
                kb_ = krTb[i % 2]
                pa = psA[i % 2]
                for j in range(GP):
                    pk = psKR[j // 2]
                    k.mm(pk[:, (j % 2) * 256:(j % 2) * 256 + 256], lt[:, j * 128:(j + 1) * 128], wuk[:], [lt, wuk], [pk],
                         inc=(j % 2 == 1))
                    k.mm(pa[:, j * 16:(j + 1) * 16], lt[:, j * 128:(j + 1) * 128], qabs[:, b, :], [lt, qabs], [pa], inc=False)
                k.mm(pa[:, 64:128], kb_[:, 0:128], BD[:, b, :, :].rearrange("p g n -> p (g n)"), [kb_, BD], [pa])
                sq_ = ssq[i % 2]
                for hb in range(2):
                    sb_ = sqb[(2 * i + hb) % 4]
                    k.act(sb_[:, 0:512], psKR[hb][:, 0:512], AF.Square, [psKR[hb]], [sb_])
                    k.emit("dve", lambda e, sb_=sb_, sq_=sq_, hb=hb: e.tensor_reduce(
                        out=sq_[:, hb * 8:(hb + 1) * 8], in_=sb_[:, 0:512].rearrange("p (a d) -> p a d", d=64),
                        axis=AX.X, op=ALU.add), [sb_], [sq_])

            def stageK2a(i):
                sq_ = ssq[i % 2]
                pa = psA[i % 2]
                ri = rinv[i % 2]
                k.rsqrt(ri[:, 0:16], sq_[:, 0:16], 1.0 / 64, epsT[:, 0:1], [sq_, epsT], [ri])
                t1, t2 = stmp[i % 2], stmp2[i % 2]
                k.tt("dve", t1[:, 0:64].rearrange("p (a q) -> p a q", q=4), pa[:, 0:64].rearrange("p (a q) -> p a q", q=4),
                     bc(ri[:, 0:16].unsqueeze(2), [128, 16, 4]), ALU.mult, [pa, ri], [t1])
                k.tt("dve", t2[:, 0:64], t1[:, 0:64], pa[:, 64:128], ALU.add, [t1, pa], [t2])

            def stageK2b(i):
                t2 = stmp2[i % 2]
                pp = pTt[i % 3]
                k.act(pp[:, 0:64], t2[:, 0:64], AF.Exp, [t2], [pp], scale=SM_SCALE)

            def stageAcc(i):
                b, g = groups[i]
                n_ = nat[i % NNAT]
                pp = pTt[i % 3]
                if g == 0:
                    k.mm(psAcc[0:16, 0:129], pnb[b % 2][0:4, 0:16], natn[b % 2][0:4, 0:129], [pnb[b % 2], natn[b % 2]], [psAcc],
                         start=True, stop=False, skip=True, inc=False)
                for j in range(GP):
                    last = (g == NGRP - 1 and j == GP - 1)
                    k.mm(psAcc[0:16, 0:129], pp[:, j * 16:(j + 1) * 16], n_[:, j, 0:129], [pp, n_], [psAcc],
                         start=False, stop=last, skip=True, inc=(j == GP - 1))
                if g == NGRP - 1:
                    finish(b)

            def start_sample(b):
                cs = slice(b * TQ, (b + 1) * TQ)
                pn_, pnb_, natn_ = pn, pnb[b % 2], natn[b % 2]
                for h in range(4):
                    k.mm(psK[0:4, 128 + h * 4:128 + (h + 1) * 4], KsT[0:96, h, cs], Qb[0:96, h, cs], [KsT, Qb], [psK],
                         inc=(h == 3))
                k.act(pn_[0:4, 0:16], psK[0:4, 128:144], AF.Exp, [psK], [pn_], scale=SM_SCALE)
                k.tt("dve", pnb_[0:4, 0:16], pn_[0:4, 0:16], mask4, ALU.mult, [pn_, cf], [pnb_])
                k.tr(psK[0:4, 256:384], ckvT[:, cs], ident, [ckvT, cf], [psK])
                k.cp("dve", natn_[0:4, 0:128], psK[0:4, 256:384], [psK], [natn_])

            def finish(b):
                cs = slice(b * TQ, (b + 1) * TQ)
                k.recip(rls[0:16, 0:1], psAcc[0:16, 128:129], [psAcc], [rls])
                k.ts("dve", lo[0:16, 0:128], psAcc[0:16, 0:128], rls[0:16, 0:1], ALU.mult, [psAcc, rls], [lo])
                k.tr(psK[:, 384:400], lo[0:16, 0:128], cf[0:16, C_ID:C_ID + 16], [lo, cf], [psK])
                k.cp("dve", loT[:, 0:16], psK[:, 384:400], [psK], [loT])
                for h in range(4):
                    k.mm(psK[:, 400 + h * 4:400 + (h + 1) * 4], wuv[:, h * 128:(h + 1) * 128], loT[:, h * 4:(h + 1) * 4],
                         [wuv, loT], [psK], inc=(h == 3))
                k.tt("dve", mixed[:, 4:8, cs], psK[:, 400:416].rearrange("p (h q) -> p h q", h=4), szm[:, :, cs], ALU.mult,
                     [psK, szm], [mixed])

            PF = 3
            for i in range(min(PF, NG)):
                load(i)
            for i in range(-1, NG + 2):
                if PF <= i + PF < NG:
                    load(i + PF)
                if 0 <= i - 1 < NG:
                    stageK2a(i - 1)
                if 0 <= i + 1 < NG:
                    stageT(i + 1)
                if 0 <= i < NG:
                    if groups[i][1] == 0:
                        start_sample(groups[i][0])
                    stageK1(i)
                if 0 <= i - 1 < NG:
                    stageK2b(i - 1)
                if 0 <= i - 2 < NG:
                    stageAcc(i - 2)

        for l in range(DEPTH):
            load_weights(l)
            adaln(l)
            k.barrier()
            k.memset("pool", VX[:, :, :, 128:130], 1.0, [VX])
            for bi in range(NBLK):
                block_front(l, "P", bi)
                prompt_attn(l, bi)
                block_back(l, "P", bi)
                if l == DEPTH - 1:
                    k.dma(D["yT"][:, :, bi * NTB:(bi + 1) * NTB], xT[:, :, bi * NTB:(bi + 1) * NTB], [xT], ())
            k.barrier()
            for n_ in nat:
                k.memset("pool", n_[:, :, 128:130], 1.0, [n_])
            for n_ in natn:
                k.memset("pool", n_[:, 128:130], 1.0, [n_])
            block_front(l, "S", 0)
            sample_attn(l)
            block_back(l, "S", 0)
        k.dma(D["ysT"][:, :, :], xsT[:], [xsT], ())
        for s in list(k.dsems) + list(pgsem) + list(k.swsems):
            if k.cnt[s] > 0:
                k.wait_tok("sp", (s, k.cnt[s]))

        with nc.Block() as block:
            @block.sync
            def _(e):
                k.replay("sp", e)

            @block.scalar
            def _(e):
                k.replay("act", e)

            @block.tensor
            def _(e):
                k.replay("pe", e)

            @block.vector
            def _(e):
                k.replay("dve", e)

            @block.gpsimd
            def _(e):
                k.replay("pool", e)
    return nc


def _fm(a):
    r, f = a.shape
    return np.ascontiguousarray(a.reshape(r, f // 128, 128).transpose(2, 1, 0))


def _consts(T, TQ, past_len):
    cf = np.zeros((128, NCF), np.float32)
    cf[:, C_ID:C_ID + 128] = np.eye(128, dtype=np.float32)
    cf[:, C_ONES:C_ONES + 128] = 1.0
    cf[0:64, C_B96:C_B96 + 64] = 1.0 / 64
    cf[64:96, C_B96 + 64:C_B96 + 96] = 1.0 / 32
    for i in range(16):
        cf[64 + 16 + i, C_ROT + 64 + i] = -1.0
        cf[64 + i, C_ROT + 64 + 16 + i] = 1.0
    wins = (2, 4, 8, 16)
    for c in range(2):
        for hf in range(2):
            w = wins[2 * c + hf]
            cf[hf * 64:(hf + 1) * 64, C_INVW + c] = 1.0 / w
            for t in range(16):
                cf[hf * 64:(hf + 1) * 64, C_INVC + c * 16 + t] = 1.0 / min(t + 1, w)
    kk = np.arange(128)[:, None]
    qq = np.arange(128)[None, :]
    cf[:, C_MASK:C_MASK + 128] = (kk <= qq).astype(np.float32)
    for kq in range(4):
        for h in range(4):
            for q in range(4):
                cf[kq, C_MASK4 + h * 4 + q] = 1.0 if kq <= q else 0.0
    for g in range(4):
        for r in range(32):
            cf[64 + r, C_SELQ + g * 32 + r] = 1.0
        cf[g * 32:(g + 1) * 32, C_BDM + g] = 1.0
    inv_freq = (1.0 / (ROPE_THETA ** (np.arange(0, ROPE, 2, dtype=np.float32) / np.float32(ROPE)))).astype(np.float32)

    def tab(pos):
        ang = pos.astype(np.float32)[:, None] * inv_freq[None, :]
        cos = np.concatenate([np.cos(ang), np.cos(ang)], -1).astype(np.float32)
        sin = np.concatenate([np.sin(ang), np.sin(ang)], -1).astype(np.float32)
        o = np.zeros((2, 96, len(pos)), np.float32)
        o[0, 0:64] = 1.0
        o[0, 64:96] = cos.T
        o[1, 64:96] = sin.T
        return o
    return cf, tab(np.arange(T)), tab(past_len + np.arange(TQ))


_NC_CACHE = {}


def kernel(x_prompt, x_sample, cache_latent, cache_krope, state_pool, state_conv, page_table,
           c_prompt, c_sample, norm_g, w_ada, b_ada, w_in, pool_w, pool_scale, conv_w,
           q_norm_g, w_uq, qn_g, qr_g, kv_norm_g, kr_g, w_uk, kn_g, w_uv, w_out, _ncores=None):
    f = lambda a: np.asarray(a, dtype=np.float32)
    x_prompt, x_sample, cache_latent, cache_krope = f(x_prompt), f(x_sample), f(cache_latent), f(cache_krope)
    B, T, _ = x_prompt.shape
    DB, TQ, _ = x_sample.shape
    ncores = _ncores or B
    NS = DB // ncores
    NPOOL = cache_latent.shape[1]
    NPG = page_table.shape[1]
    past_len = NPG * 128
    cfg = dict(T=T, NPG=NPG, NPOOL=NPOOL, NS=NS, TQ=TQ)
    key = tuple(sorted(cfg.items()))
    if key not in _NC_CACHE:
        _NC_CACHE[key] = build(cfg)
    nc = _NC_CACHE[key]

    cf, ropeP, ropeS = _consts(T, TQ, past_len)
    caches = [np.ascontiguousarray(np.concatenate([cache_latent[l], cache_krope[l]], axis=-1)).reshape(-1) for l in range(2)]
    wada = np.ascontiguousarray(f(w_ada).reshape(2, 8, 128, 24, 128).transpose(0, 3, 2, 1, 4))
    win = np.ascontiguousarray(f(w_in).reshape(2, 8, 128, D_IN).transpose(0, 2, 1, 3))
    wout = np.ascontiguousarray(f(w_out).reshape(2, 8, 128, 1024).transpose(0, 2, 1, 3))
    wuq = np.ascontiguousarray(f(w_uq).reshape(2, 2, 128, 384).transpose(0, 2, 1, 3))
    wuk = np.ascontiguousarray(f(w_uk))
    wuv = np.ascontiguousarray(f(w_uv))
    wukT = np.ascontiguousarray(f(w_uk).reshape(2, 128, 4, 64).transpose(0, 3, 2, 1))
    pw = np.zeros((2, 128, 2, 128), np.float32)
    pwf = f(pool_w)
    for c in range(2):
        pw[:, 0:64, c, 0:64] = pwf[:, 2 * c]
        pw[:, 64:128, c, 64:128] = pwf[:, 2 * c + 1]
    pvec = np.zeros((2, 128, NV), np.float32)
    pvec[:, :, 0:8] = f(norm_g).reshape(2, 8, 128).transpose(0, 2, 1)
    pvec[:, :, 8:32] = f(b_ada).reshape(2, 24, 128).transpose(0, 2, 1)
    pvec[:, :, 32:34] = f(pool_scale).reshape(2, 2, 128).transpose(0, 2, 1)
    cw = f(conv_w).reshape(2, 3, 2, 128)
    for t in range(3):
        for c in range(2):
            pvec[:, :, 34 + 2 * t + c] = cw[:, t, c]
    pvec[:, :, 40:42] = f(q_norm_g).reshape(2, 2, 128).transpose(0, 2, 1)
    pvec[:, :, 42] = f(kv_norm_g)
    pvec[:, 0:64, 43] = f(qn_g)
    pvec[:, 64:96, 43] = f(qr_g)
    pvec[:, 0:64, 44] = f(kn_g)
    pvec[:, 64:96, 44] = f(kr_g)
    sp = f(state_pool)
    sc = f(state_conv)
    pt = np.asarray(page_table, dtype=np.int32)
    cp_, cs_ = f(c_prompt), f(c_sample)
    in_maps = []
    for c in range(ncores):
        sl = slice(c * NS, (c + 1) * NS)
        m = {
            "xT": _fm(x_prompt[c]),
            "xsT": _fm(x_sample[sl].reshape(NS * TQ, D_MODEL)),
            "cT": _fm(np.concatenate([cp_[c:c + 1], cs_[sl]], 0)),
            "cache0": caches[0], "cache1": caches[1],
            "pt": np.ascontiguousarray(pt[sl]),
            "spT": np.ascontiguousarray(sp[:, sl].reshape(2, NS, 15, 2, 128).transpose(0, 4, 3, 1, 2)),
            "scT": np.ascontiguousarray(sc[:, sl].reshape(2, NS, 2, 2, 128).transpose(0, 4, 3, 1, 2)),
            "wada": wada, "win": win, "wout": wout, "wuq": wuq, "wuk": wuk, "wuv": wuv, "wukT": wukT, "pw": pw,
            "pvec": pvec, "cf": cf, "ropeP": ropeP, "ropeS": ropeS,
        }
        in_maps.append(m)
    res = run_bass_kernel_spmd(nc, in_maps, core_ids=list(range(ncores)))
    R = res.results

    def unfm(a):
        return np.ascontiguousarray(a.transpose(2, 1, 0).reshape(a.shape[2], -1))
    y_p = np.stack([unfm(R[c]["yT"]) for c in range(ncores)], 0)
    y_s = np.concatenate([unfm(R[c]["ysT"]).reshape(NS, TQ, D_MODEL) for c in range(ncores)], 0)
    lat_p = np.stack([R[c]["latP"].transpose(0, 2, 1) for c in range(ncores)], 1)
    kr_p = np.stack([R[c]["krP"].transpose(0, 2, 1) for c in range(ncores)], 1)
    pool_p = np.stack([R[c]["poolP"].transpose(0, 3, 2, 1).reshape(2, 15, 256) for c in range(ncores)], 1)
    conv_p = np.stack([R[c]["convP"].transpose(0, 3, 2, 1).reshape(2, 2, 256) for c in range(ncores)], 1)
    lat_s = np.concatenate([R[c]["latS"].transpose(0, 2, 1).reshape(2, NS, TQ, 128) for c in range(ncores)], 1)
    kr_s = np.concatenate([R[c]["krS"].transpose(0, 2, 1).reshape(2, NS, TQ, 32) for c in range(ncores)], 1)
    pool_s = np.concatenate([R[c]["poolS"].transpose(0, 3, 4, 2, 1).reshape(2, NS, 15, 256) for c in range(ncores)], 1)
    conv_s = np.concatenate([R[c]["convS"].transpose(0, 3, 4, 2, 1).reshape(2, NS, 2, 256) for c in range(ncores)], 1)
    outs = (y_p, y_s, lat_p, kr_p, pool_p, conv_p, lat_s, kr_s, pool_s, conv_s)
    return tuple(np.ascontiguousarray(o, dtype=np.float32) for o in outs)
```
